# Optimizing a Trainium2 kernel written in Bass

```python
import jax
import jax.numpy as jnp
from jax import lax
import numpy as np


D_MODEL = 1024
BATCH = 16
SEQ = 2048
DEPTH = 4

GRID_W = 64
CTX_LEN = 256
HEAD_DIM = 64
MIX_WIDTH = D_MODEL
NA_HEADS = 8
NA_WIN_R = 8
NA_WIN_C = 16
NA_QCOLS = 16
NA_KCOLS = NA_QCOLS + NA_WIN_C
MLA_HEADS = 8
MLA_Q_RANK = 384
MLA_KV_RANK = 256
MLA_NOPE = 64
MLA_ROPE = 32
MLA_V = 64
ATTN_BLOCK = 128
SWA_Q_HEADS = 16
SWA_KV_HEADS = 2
SWA_WINDOW = 128
SWA_BLOCK = 128
D_FF = 2816
CONV_W = 3
ROPE_BASE = 10000.0
LN_EPS = 1e-6
RMS_EPS = 1e-6
NEG = -1e30
DEEPNORM_ALPHA = (2 * DEPTH) ** 0.25
DEEPNORM_BETA = (8 * DEPTH) ** -0.25
N_EVEN = (DEPTH + 1) // 2
N_ODD = DEPTH // 2
NA_WIDTH = NA_HEADS * HEAD_DIM
EVEN_SPLITS = [NA_WIDTH, 2 * NA_WIDTH, 3 * NA_WIDTH, 3 * NA_WIDTH + MLA_Q_RANK, 3 * NA_WIDTH + MLA_Q_RANK + MLA_KV_RANK]
EVEN_IN = 3 * NA_WIDTH + MLA_Q_RANK + MLA_KV_RANK + MLA_ROPE
ODD_SPLITS = [SWA_Q_HEADS * HEAD_DIM, (SWA_Q_HEADS + SWA_KV_HEADS) * HEAD_DIM]
ODD_IN = (SWA_Q_HEADS + 2 * SWA_KV_HEADS) * HEAD_DIM

kernel_name = 'hybrid_natten_mla_swa_dit_prefix'


def layer_norm(x):
    xf = x.astype(jnp.float32)
    mu = jnp.mean(xf, axis=-1, keepdims=True)
    var = jnp.mean(jnp.square(xf - mu), axis=-1, keepdims=True)
    return ((xf - mu) * lax.rsqrt(var + LN_EPS)).astype(x.dtype)


def rms_norm(x, g):
    xf = x.astype(jnp.float32)
    y = xf * lax.rsqrt(jnp.mean(jnp.square(xf), axis=-1, keepdims=True) + RMS_EPS)
    return y.astype(x.dtype) * g


def modulate(h, shift, scale):
    return h * (1 + scale) + shift


def axial_rope(n_tokens, rot_dim, dtype):
    axis_dim = rot_dim // 2
    t = jnp.arange(n_tokens)
    row = (t // GRID_W).astype(jnp.float32)[:, None]
    col = (t % GRID_W).astype(jnp.float32)[:, None]
    inv_freq = ROPE_BASE ** (-jnp.arange(0, axis_dim, 2, dtype=jnp.float32) / axis_dim)
    ar, ac = row * inv_freq, col * inv_freq
    ang = jnp.concatenate([ar, ar, ac, ac], axis=-1)
    return jnp.cos(ang).astype(dtype), jnp.sin(ang).astype(dtype)


def _rotate_half(v):
    a, b = jnp.split(v, 2, axis=-1)
    return jnp.concatenate([-b, a], axis=-1)


def apply_rope(x, cos, sin):
    xr, xc = jnp.split(x, 2, axis=-1)
    return x * cos + jnp.concatenate([_rotate_half(xr), _rotate_half(xc)], axis=-1) * sin


def dense_attention(q, k, v):
    B, T, H, dh = q.shape
    s = jnp.einsum('bqhd,bkhd->bhqk', q, k).astype(jnp.float32) * dh ** -0.5
    p = jax.nn.softmax(s, axis=-1).astype(v.dtype)
    return jnp.einsum('bhqk,bkhd->bqhd', p, v).reshape(B, T, H * dh)


def _na_column_tables():
    n_cb = GRID_W // NA_QCOLS
    q_col = np.arange(GRID_W).reshape(n_cb, NA_QCOLS)
    k_start = np.clip(np.arange(n_cb) * NA_QCOLS - NA_WIN_C // 2, 0, GRID_W - NA_KCOLS)
    k_col = k_start[:, None] + np.arange(NA_KCOLS)
    w_start = np.clip(q_col - NA_WIN_C // 2, 0, GRID_W - NA_WIN_C)
    kc = k_col[:, None, :]
    col_ok = (kc >= w_start[..., None]) & (kc < w_start[..., None] + NA_WIN_C)
    dcol_idx = np.clip(kc - q_col[..., None] + NA_WIN_C - 1, 0, 2 * NA_WIN_C - 2)
    return k_col, col_ok, dcol_idx


def neighbourhood_attention(q, k, v, k_ctx, v_ctx, rpb):
    B, S, H, dh = q.shape
    rows = S // GRID_W
    kr = min(NA_WIN_R, rows)
    n_cb = GRID_W // NA_QCOLS
    n_loc = kr * NA_KCOLS
    k_col, col_ok, dcol_idx = _na_column_tables()
    scale = dh ** -0.5
    qg = q.reshape(B, rows, n_cb, NA_QCOLS, H, dh)
    kg = k.reshape(B, rows, GRID_W, H, dh)
    vg = v.reshape(B, rows, GRID_W, H, dh)
    rpb_col = rpb[:, :, dcol_idx]
    mask = col_ok[:, :, None, :]

    def one_row(r):
        r0 = jnp.clip(r - NA_WIN_R // 2, 0, rows - kr)
        q_r = lax.dynamic_index_in_dim(qg, r, axis=1, keepdims=False)
        k_blk = lax.dynamic_slice_in_dim(kg, r0, kr, axis=1)[:, :, k_col]
        v_blk = lax.dynamic_slice_in_dim(vg, r0, kr, axis=1)[:, :, k_col]
        drow_idx = r0 + jnp.arange(kr) - r + NA_WIN_R - 1
        bias = jnp.take(rpb_col, drow_idx, axis=1).transpose(0, 2, 3, 1, 4)
        s_loc = jnp.einsum('bnqhd,bmnkhd->bhnqmk', q_r, k_blk) * scale + bias
        s_loc = jnp.where(mask, s_loc.astype(jnp.float32), NEG).reshape(B, H, n_cb, NA_QCOLS, n_loc)
        s_ctx = jnp.einsum('bnqhd,bchd->bhnqc', q_r, k_ctx).astype(jnp.float32) * scale
        p = jax.nn.softmax(jnp.concatenate([s_loc, s_ctx], axis=-1), axis=-1).astype(v.dtype)
        p_loc = p[..., :n_loc].reshape(B, H, n_cb, NA_QCOLS, kr, NA_KCOLS)
        o = (jnp.einsum('bhnqmk,bmnkhd->bnqhd', p_loc, v_blk)
             + jnp.einsum('bhnqc,bchd->bnqhd', p[..., n_loc:], v_ctx))
        return o.reshape(B, GRID_W, H * dh)

    out = lax.map(one_row, jnp.arange(rows))
    return out.swapaxes(0, 1).reshape(B, S, H * dh)


def mla_attention(qn, qr, kn, kr, v):
    B, T = qn.shape[:2]
    s = jnp.einsum('bqhd,bkhd->bhqk', qn, kn) + jnp.einsum('bqhd,bkd->bhqk', qr, kr)
    p = jax.nn.softmax(s.astype(jnp.float32) * (MLA_NOPE + MLA_ROPE) ** -0.5, axis=-1).astype(v.dtype)
    return jnp.einsum('bhqk,bkhd->bqhd', p, v).reshape(B, T, -1)


def mla_latent(qn, qr, kn, kr, v):
    B, S = qn.shape[:2]
    nb = S // ATTN_BLOCK
    blk = lambda t: t.reshape(B, nb, ATTN_BLOCK, *t.shape[2:]).swapaxes(0, 1)
    out = lax.map(lambda a: mla_attention(a[0], a[1], kn, kr, v), (blk(qn), blk(qr)))
    return out.swapaxes(0, 1).reshape(B, S, -1)


def window_gqa_latent(q, k, v, k_ctx, v_ctx, sink):
    B, S, Hq, dh = q.shape
    Hkv = k.shape[2]
    G = Hq // Hkv
    nb = S // SWA_BLOCK
    span = SWA_BLOCK + 2 * SWA_WINDOW
    scale = dh ** -0.5
    qb = q.reshape(B, nb, SWA_BLOCK, Hkv, G, dh).swapaxes(0, 1)
    pad = ((0, 0), (SWA_WINDOW, SWA_WINDOW), (0, 0), (0, 0))
    kp, vp = jnp.pad(k, pad), jnp.pad(v, pad)
    qi = jnp.arange(SWA_BLOCK)[:, None]
    kk = jnp.arange(span)[None, :]
    band = (kk >= qi) & (kk <= qi + 2 * SWA_WINDOW)
    s_sink = jnp.broadcast_to(sink.astype(jnp.float32).reshape(1, Hkv, G, 1, 1), (B, Hkv, G, SWA_BLOCK, 1))

    def one_block(args):
        n, q_n = args
        start = n * SWA_BLOCK
        k_n = lax.dynamic_slice_in_dim(kp, start, span, axis=1)
        v_n = lax.dynamic_slice_in_dim(vp, start, span, axis=1)
        key_pos = start - SWA_WINDOW + kk
        ok = band & (key_pos >= 0) & (key_pos < S)
        s_loc = jnp.einsum('bqhgd,bkhd->bhgqk', q_n, k_n).astype(jnp.float32) * scale
        s_loc = jnp.where(ok, s_loc, NEG)
        s_ctx = jnp.einsum('bqhgd,bchd->bhgqc', q_n, k_ctx).astype(jnp.float32) * scale
        p = jax.nn.softmax(jnp.concatenate([s_loc, s_ctx, s_sink], axis=-1), axis=-1).astype(v.dtype)
        o = (jnp.einsum('bhgqk,bkhd->bqhgd', p[..., :span], v_n)
             + jnp.einsum('bhgqc,bchd->bqhgd', p[..., span:-1], v_ctx))
        return o.reshape(B, SWA_BLOCK, Hq * dh)

    out = lax.map(one_block, (jnp.arange(nb), qb))
    return out.swapaxes(0, 1).reshape(B, S, Hq * dh)


def gqa_sink_dense(q, k, v, sink):
    B, T, Hq, dh = q.shape
    Hkv = k.shape[2]
    G = Hq // Hkv
    qg = q.reshape(B, T, Hkv, G, dh)
    s = jnp.einsum('bqhgd,bkhd->bhgqk', qg, k).astype(jnp.float32) * dh ** -0.5
    s_sink = jnp.broadcast_to(sink.astype(jnp.float32).reshape(1, Hkv, G, 1, 1), (B, Hkv, G, T, 1))
    p = jax.nn.softmax(jnp.concatenate([s, s_sink], axis=-1), axis=-1)[..., :-1].astype(v.dtype)
    return jnp.einsum('bhgqk,bkhd->bqhgd', p, v).reshape(B, T, Hq * dh)


def conv_ffn(h, w_up, b_up, conv_w, conv_b, w_down, b_down):
    T = h.shape[1]
    u = h @ w_up + b_up
    r = CONV_W // 2
    up = jnp.pad(u, ((0, 0), (r, r), (0, 0)))
    u = sum(up[:, i:i + T] * conv_w[i] for i in range(CONV_W)) + conv_b
    a, g = jnp.split(u, 2, axis=-1)
    return (a * jax.nn.silu(g)) @ w_down + b_down


def even_project(z, q_norm_g, w_uq, kv_norm_g, w_ukv):
    B, T, _ = z.shape
    qa, ka, va, cq, ckv, k_rope = jnp.split(z, EVEN_SPLITS, axis=-1)
    heads = lambda t: t.reshape(B, T, NA_HEADS, HEAD_DIM)
    q = (rms_norm(cq, q_norm_g) @ w_uq).reshape(B, T, MLA_HEADS, MLA_NOPE + MLA_ROPE)
    kv = (rms_norm(ckv, kv_norm_g) @ w_ukv).reshape(B, T, MLA_HEADS, MLA_NOPE + MLA_V)
    return (heads(qa), heads(ka), heads(va), q[..., :MLA_NOPE], q[..., MLA_NOPE:],
            kv[..., :MLA_NOPE], k_rope, kv[..., MLA_NOPE:])


def even_mixer(h, h_ctx, w_in, rpb, q_norm_g, w_uq, kv_norm_g, w_ukv, w_out, rope, with_ctx_out):
    cos, sin = rope
    qa, ka, va, qn, qr, kn, kr, vm = even_project(h @ w_in, q_norm_g, w_uq, kv_norm_g, w_ukv)
    qa_c, ka_c, va_c, qn_c, qr_c, kn_c, kr_c, vm_c = even_project(h_ctx @ w_in, q_norm_g, w_uq, kv_norm_g, w_ukv)
    qr = apply_rope(qr, cos[:, None, :], sin[:, None, :])
    kr = apply_rope(kr, cos, sin)
    o_a = neighbourhood_attention(qa, ka, va, ka_c, va_c, rpb)
    o_b = mla_latent(qn, qr, jnp.concatenate([kn, kn_c], axis=1), jnp.concatenate([kr, kr_c], axis=1),
                     jnp.concatenate([vm, vm_c], axis=1))
    y = jnp.concatenate([o_a, o_b], axis=-1) @ w_out
    if not with_ctx_out:
        return y, None
    o_ac = dense_attention(qa_c, ka_c, va_c)
    o_bc = mla_attention(qn_c, qr_c, kn_c, kr_c, vm_c)
    return y, jnp.concatenate([o_ac, o_bc], axis=-1) @ w_out


def odd_project(z):
    B, T, _ = z.shape
    q, k, v = jnp.split(z, ODD_SPLITS, axis=-1)
    return (q.reshape(B, T, SWA_Q_HEADS, HEAD_DIM), k.reshape(B, T, SWA_KV_HEADS, HEAD_DIM),
            v.reshape(B, T, SWA_KV_HEADS, HEAD_DIM))


def odd_mixer(h, h_ctx, w_in, sink, w_out, rope, with_ctx_out):
    cos, sin = rope
    q, k, v = odd_project(h @ w_in)
    q_c, k_c, v_c = odd_project(h_ctx @ w_in)
    q = apply_rope(q, cos[:, None, :], sin[:, None, :])
    k = apply_rope(k, cos[:, None, :], sin[:, None, :])
    y = window_gqa_latent(q, k, v, k_c, v_c, sink) @ w_out
    if not with_ctx_out:
        return y, None
    return y, gqa_sink_dense(q_c, k_c, v_c, sink) @ w_out


def setup_inputs(seed: int = 0) -> dict:
    key = jax.random.key(seed)
    keys = iter(jax.random.split(key, 24))

    def nrm(shape, scale):
        return jax.random.normal(next(keys), shape, jnp.float32) * scale

    D = D_MODEL
    F2 = 2 * D_FF
    return {
        'x': nrm((BATCH, SEQ, D), 1.0),
        'c': nrm((BATCH, D), 1.0),
        'ctx': nrm((BATCH, CTX_LEN, D), 1.0),
        'c_ctx': nrm((D,), 1.0),
        'w_ada': nrm((DEPTH, D, 6 * D), 0.5 * D ** -0.5),
        'b_ada': nrm((DEPTH, 6 * D), 0.01),
        'na_rpb': nrm((N_EVEN, NA_HEADS, 2 * NA_WIN_R - 1, 2 * NA_WIN_C - 1), 0.1),
        'w_in_even': nrm((N_EVEN, D, EVEN_IN), D ** -0.5),
        'mla_q_norm': 1.0 + nrm((N_EVEN, MLA_Q_RANK), 0.01),
        'w_uq': nrm((N_EVEN, MLA_Q_RANK, MLA_HEADS * (MLA_NOPE + MLA_ROPE)), MLA_Q_RANK ** -0.5),
        'mla_kv_norm': 1.0 + nrm((N_EVEN, MLA_KV_RANK), 0.01),
        'w_ukv': nrm((N_EVEN, MLA_KV_RANK, MLA_HEADS * (MLA_NOPE + MLA_V)), MLA_KV_RANK ** -0.5),
        'w_out_even': nrm((N_EVEN, MIX_WIDTH, D), DEEPNORM_BETA * MIX_WIDTH ** -0.5),
        'w_in_odd': nrm((N_ODD, D, ODD_IN), D ** -0.5),
        'sinks': nrm((N_ODD, SWA_Q_HEADS), 0.5),
        'w_out_odd': nrm((N_ODD, SWA_Q_HEADS * HEAD_DIM, D), DEEPNORM_BETA * (SWA_Q_HEADS * HEAD_DIM) ** -0.5),
        'w_up': nrm((DEPTH, D, F2), D ** -0.5),
        'b_up': nrm((DEPTH, F2), 0.01),
        'conv_w': nrm((DEPTH, CONV_W, F2), CONV_W ** -0.5),
        'conv_b': nrm((DEPTH, F2), 0.01),
        'w_down': nrm((DEPTH, D_FF, D), DEEPNORM_BETA * D_FF ** -0.5),
        'b_down': nrm((DEPTH, D), 0.01),
    }


def reference(x, c, ctx, c_ctx, w_ada, b_ada, na_rpb, w_in_even, mla_q_norm, w_uq, mla_kv_norm, w_ukv,
              w_out_even, w_in_odd, sinks, w_out_odd, w_up, b_up, conv_w, conv_b, w_down, b_down):
    S = x.shape[1]
    rope_mla = axial_rope(S, MLA_ROPE, x.dtype)
    rope_swa = axial_rope(S, HEAD_DIM, x.dtype)
    mod_lat = jnp.einsum('bd,ldk->lbk', jax.nn.silu(c), w_ada) + b_ada[:, None, :]
    mod_ctx = jnp.einsum('d,ldk->lk', jax.nn.silu(c_ctx), w_ada) + b_ada
    z = ctx
    for l in range(DEPTH):
        i = l // 2
        with_ctx = l < DEPTH - 1
        sh_m, sc_m, g_m, sh_f, sc_f, g_f = jnp.split(mod_lat[l][:, None, :], 6, axis=-1)
        csh_m, csc_m, cg_m, csh_f, csc_f, cg_f = jnp.split(mod_ctx[l], 6, axis=-1)
        h, hc = modulate(x, sh_m, sc_m), modulate(z, csh_m, csc_m)
        if l % 2 == 0:
            y, y_c = even_mixer(h, hc, w_in_even[i], na_rpb[i], mla_q_norm[i], w_uq[i], mla_kv_norm[i],
                                w_ukv[i], w_out_even[i], rope_mla, with_ctx)
        else:
            y, y_c = odd_mixer(h, hc, w_in_odd[i], sinks[i], w_out_odd[i], rope_swa, with_ctx)
        ffn = (w_up[l], b_up[l], conv_w[l], conv_b[l], w_down[l], b_down[l])
        x = layer_norm(DEEPNORM_ALPHA * x + g_m * y)
        x = layer_norm(DEEPNORM_ALPHA * x + g_f * conv_ffn(modulate(x, sh_f, sc_f), *ffn))
        if with_ctx:
            z = layer_norm(DEEPNORM_ALPHA * z + cg_m * y_c)
            z = layer_norm(DEEPNORM_ALPHA * z + cg_f * conv_ffn(modulate(z, csh_f, csc_f), *ffn))
    return x
```

```python
import numpy as np
from contextlib import ExitStack
import concourse.bass as bass
import concourse.mybir as mybir
from concourse.bass_utils import run_bass_kernel_spmd

F32 = mybir.dt.float32
BF16 = mybir.dt.bfloat16
AF = mybir.ActivationFunctionType
ALU = mybir.AluOpType

ENGS = ("pe", "act", "dve", "pool", "sp")
NRING = 24

D = 1024
S = 2048
CT = 256
T = S + CT
DFF = 2816
NCH = 22
ALPHA = 8.0 ** 0.25
EPS = 1e-6


class Buf:
    __slots__ = ("w", "r", "name", "keep", "tmp")

    def __init__(self, name=""):
        self.w = None
        self.r = []
        self.name = name
        self.keep = False
        self.tmp = False


class Prog:
    def __init__(self, nc):
        self.nc = nc
        self.ops = {e: [] for e in ENGS}
        self.known = {e: {} for e in ENGS}
        self.known_dma = {e: set() for e in ENGS}
        self.dma_cnt = {e: 0 for e in ENGS}
        self.all_bufs = []

    def buf(self, name=""):
        b = Buf(name)
        self.all_bufs.append(b)
        return b

    def bufs(self, n, name=""):
        return [self.buf(f"{name}{i}") for i in range(n)]

    def _collect(self, eng, reads, writes):
        deps = {}
        ddeps = set()

        def add(d, raw):
            if d is None:
                return
            if d[0] == "dma":
                ddeps.add(d)
                return
            e2, i2 = d
            if e2 == eng:
                if eng == "pe" or not raw:
                    return
            if deps.get(e2, -1) < i2:
                deps[e2] = i2

        for b in reads:
            add(b.w, True)
        for b in writes:
            add(b.w, False)
            for r in b.r:
                add(r, False)
        waits = []
        kn = self.known[eng]
        for e2, i2 in deps.items():
            if kn.get(e2, -1) >= i2:
                continue
            kn[e2] = i2
            waits.append(("eng", e2, i2))
        kd = self.known_dma[eng]
        for d in ddeps:
            if d in kd:
                continue
            kd.add(d)
            waits.append(("dma", d[1], d[2]))
        return waits

    def op(self, eng, fn, reads=(), writes=()):
        waits = self._collect(eng, reads, writes)
        idx = len(self.ops[eng])
        self.ops[eng].append([fn, waits, False, None])
        me = (eng, idx)
        for b in reads:
            if len(b.r) > 64:
                b.r = b.r[-32:]
            b.r.append(me)
        for b in writes:
            b.w = me
            b.r = []
        return me

    def dma(self, queue, fn, reads=(), writes=()):
        waits = self._collect(queue, reads, writes)
        n = self.dma_cnt[queue]
        self.dma_cnt[queue] = n + 1
        if n >= NRING:
            prev = ("dma", queue, n - NRING)
            if prev not in self.known_dma[queue]:
                self.known_dma[queue].add(prev)
                waits.append(("dma", queue, n - NRING))
        self.ops[queue].append([fn, waits, False, n])
        me = ("dma", queue, n)
        for b in reads:
            b.r.append(me)
        for b in writes:
            b.w = me
            b.r = []
        return me

    def barrier(self):
        lasts = []
        for e in ENGS:
            for i in range(len(self.ops[e]) - 1, -1, -1):
                o = self.ops[e][i]
                if o[0] is not None and o[3] is None:
                    lasts.append((e, i))
                    break
        pend = set()
        for b in self.all_bufs:
            if b.keep:
                continue
            if b.w is not None and b.w[0] == "dma":
                pend.add(b.w)
            for r in b.r:
                if r[0] == "dma":
                    pend.add(r)
        for e in ENGS:
            waits = []
            kn = self.known[e]
            for e2, i2 in lasts:
                if e2 == e:
                    continue
                if kn.get(e2, -1) >= i2:
                    continue
                kn[e2] = i2
                waits.append(("eng", e2, i2))
            kd = self.known_dma[e]
            for d in pend:
                if d in kd:
                    continue
                kd.add(d)
                waits.append(("dma", d[1], d[2]))
            self.ops[e].append([None, waits, False, None])
        for b in self.all_bufs:
            if b.keep:
                b.r = []
                continue
            b.w = None
            b.r = []
        self.all_bufs = [b for b in self.all_bufs if not b.tmp]

    def emit(self):
        nc = self.nc
        ops = self.ops
        for e in ENGS:
            for o in ops[e]:
                for w in o[1]:
                    if w[0] == "eng":
                        ops[w[1]][w[2]][2] = True
        cnt = {}
        for e in ENGS:
            c = 0
            arr = []
            for o in ops[e]:
                if o[2]:
                    c += 1
                arr.append(c)
            cnt[e] = arr
        with ExitStack() as st:
            esem = {e: st.enter_context(nc.semaphore(f"s_{e}")) for e in ENGS}
            rings = {}
            for q in ENGS:
                if self.dma_cnt[q] > 0:
                    rings[q] = [st.enter_context(nc.semaphore(f"r_{q}{i}")) for i in range(min(NRING, self.dma_cnt[q]))]
            block = st.enter_context(nc.Block())

            def run(ename, eng):
                for fn, waits, sig, dman in ops[ename]:
                    for w in waits:
                        if w[0] == "eng":
                            eng.wait_ge(esem[w[1]], cnt[w[1]][w[2]])
                        else:
                            q, n = w[1], w[2]
                            eng.wait_ge(rings[q][n % NRING], 16 * (n // NRING + 1))
                    if fn is None:
                        continue
                    ins = fn(eng)
                    if dman is not None:
                        ins.then_inc(rings[ename][dman % NRING], 16)
                    elif sig:
                        ins.then_inc(esem[ename], 1)

            @block.tensor
            def _(eng):
                run("pe", eng)

            @block.scalar
            def _(eng):
                run("act", eng)

            @block.vector
            def _(eng):
                run("dve", eng)

            @block.gpsimd
            def _(eng):
                run("pool", eng)

            @block.sync
            def _(eng):
                run("sp", eng)


def _ktile(w, m):
    K, N = w.shape
    return np.ascontiguousarray(w.reshape(K // 128, 128, N // m, m).transpose(2, 1, 0, 3))


def _rtile(w):
    K, N = w.shape
    return np.ascontiguousarray(w.reshape(K // 128, 128, N).transpose(1, 0, 2))


def _colT(v):
    return np.ascontiguousarray(v.reshape(-1, 128).T)


def _rotperm(nheads, hd):
    q = hd // 4
    p = []
    for h in range(nheads):
        for d in range(hd):
            blk = (d // q)
            src = d + q if blk % 2 == 0 else d - q
            p.append(h * hd + src)
    return np.array(p)


def _rope_tables(rot_dim):
    axis_dim = rot_dim // 2
    t = np.arange(S)
    row = (t // 64).astype(np.float32)[:, None]
    col = (t % 64).astype(np.float32)[:, None]
    inv_freq = (10000.0 ** (-np.arange(0, axis_dim, 2, dtype=np.float32) / axis_dim)).astype(np.float32)
    ar, ac = row * inv_freq, col * inv_freq
    ang = np.concatenate([ar, ar, ac, ac], axis=-1)
    cos = np.cos(ang).astype(np.float32)
    sin = np.sin(ang).astype(np.float32)
    q = rot_dim // 4
    sign = np.ones(rot_dim, np.float32)
    for d in range(rot_dim):
        if (d // q) % 2 == 0:
            sign[d] = -1.0
    sin = sin * sign
    cosT = np.concatenate([cos.T, np.ones((rot_dim, CT), np.float32)], axis=1)
    sinT = np.concatenate([sin.T, np.zeros((rot_dim, CT), np.float32)], axis=1)
    return cosT, sinT


NA_SLOTS = [(d, True, True) for d in range(14)] + [(2, False, True), (4, True, True), (6, True, True), (8, True, True), (10, True, False)]


def _host_consts():
    c = {}
    c["identf"] = np.eye(128, dtype=np.float32)
    sel = np.zeros((32, 96), np.float32)
    for j in range(32):
        sel[j, 64 + j] = 1.0
    c["sel"] = sel
    qc = np.arange(64)
    ws = np.clip(qc - 8, 0, 48)
    kc = np.arange(64)[:, None]
    colok = ((kc >= ws[None, :]) & (kc < ws[None, :] + 16)).astype(np.float32)
    m = np.zeros((128, len(NA_SLOTS), 64), np.float32)
    for s, (d, vlo, vhi) in enumerate(NA_SLOTS):
        if vlo:
            m[0:64, s, :] = colok
        if vhi:
            m[64:128, s, :] = colok
    c["namask"] = m
    j = np.arange(128)[:, None]
    i = np.arange(128)[None, :]
    c["swaml"] = (j >= i).astype(np.float32)
    c["swamr"] = (j <= i).astype(np.float32)
    cs, ss = _rope_tables(64)
    c["cosS"] = np.concatenate([cs, cs], axis=0)
    c["sinS"] = np.concatenate([ss, ss], axis=0)
    cm, sm = _rope_tables(32)
    c["cosM"] = np.concatenate([cm, cm, cm, cm], axis=0)
    c["sinM"] = np.concatenate([sm, sm, sm, sm], axis=0)
    return c


def _host_weights(inp):
    w = {}
    f = lambda a: np.ascontiguousarray(np.asarray(a, dtype=np.float32))
    w_ada = f(inp["w_ada"])
    w["wada"] = np.stack([_ktile(w_ada[l], 128) for l in range(4)])
    b_ada = f(inp["b_ada"])
    w["badaT"] = np.stack([_colT(b_ada[l]) for l in range(4)])
    w["bada"] = b_ada
    wie = f(inp["w_in_even"])
    w["wie"] = np.stack([_ktile(wie[i][:, 0:2176], 128) for i in range(2)])
    pk = _rotperm(1, 32)
    w["wkr"] = np.stack([_rtile(np.concatenate([wie[i][:, 2176:2208], wie[i][:, 2176:2208][:, pk]], axis=1)) for i in range(2)])
    w["wva"] = np.stack([_rtile(wie[i][:, 1024:1536]) for i in range(2)])
    wuq = f(inp["w_uq"])
    tiles = []
    for i in range(2):
        hs = []
        for h in range(8):
            blk = wuq[i][:, h * 96:(h + 1) * 96]
            rot = blk.copy()
            rot[:, 64:96] = blk[:, 64:96][:, pk]
            hs.append(_rtile(np.concatenate([blk, rot], axis=1)))
        tiles.append(np.stack(hs))
    w["wuq"] = np.stack(tiles)
    wukv = f(inp["w_ukv"])
    w["wukvn"] = np.stack([np.stack([_rtile(wukv[i][:, h * 128:h * 128 + 64]) for h in range(8)]) for i in range(2)])
    w["wukvv"] = np.stack([_rtile(np.concatenate([wukv[i][:, h * 128 + 64:h * 128 + 128] for h in range(8)], axis=1)) for i in range(2)])
    w["qnT"] = np.stack([_colT(f(inp["mla_q_norm"])[i]) for i in range(2)])
    w["kvnT"] = np.stack([_colT(f(inp["mla_kv_norm"])[i]) for i in range(2)])
    w["woe"] = np.stack([_rtile(f(inp["w_out_even"])[i]) for i in range(2)])
    wio = f(inp["w_in_odd"])
    pq = _rotperm(16, 64)
    p1 = _rotperm(1, 64)
    tq = []
    tk = []
    for i in range(2):
        q = wio[i][:, 0:1024]
        qr = q[:, pq]
        tq.append(np.stack([_rtile(np.concatenate([q[:, c * 128:(c + 1) * 128], qr[:, c * 128:(c + 1) * 128]], axis=1)) for c in range(8)]))
        ks = []
        for g in range(2):
            k = wio[i][:, 1024 + g * 64:1024 + (g + 1) * 64]
            kr = k[:, p1]
            ks.append(_rtile(np.concatenate([k, k, kr, kr], axis=1)))
        tk.append(np.stack(ks))
    w["wioq"] = np.stack(tq)
    w["wiok"] = np.stack(tk)
    w["wiov"] = np.stack([_rtile(wio[i][:, 1152:1280]) for i in range(2)])
    w["sinks"] = f(inp["sinks"])
    w["woo"] = np.stack([_rtile(f(inp["w_out_odd"])[i]) for i in range(2)])
    wup = f(inp["w_up"])
    w["wup"] = np.stack([np.stack([_rtile(np.concatenate([wup[l][:, c * 128:(c + 1) * 128], wup[l][:, DFF + c * 128:DFF + (c + 1) * 128]], axis=1)) for c in range(NCH)]) for l in range(4)])
    w["bupT"] = np.stack([_colT(f(inp["b_up"])[l]) for l in range(4)])
    cw = f(inp["conv_w"])
    w["cwT"] = np.stack([np.stack([_colT(cw[l, i]) for i in range(3)], axis=1) for l in range(4)])
    w["cbT"] = np.stack([_colT(f(inp["conv_b"])[l]) for l in range(4)])
    w["wdn"] = np.stack([_rtile(f(inp["w_down"])[l]) for l in range(4)])
    w["bdn"] = f(inp["b_down"])
    rpb = f(inp["na_rpb"])
    kc = np.arange(64)[:, None]
    qc = np.arange(64)[None, :]
    idx = np.clip(kc - qc + 15, 0, 30)
    w["rpb"] = np.ascontiguousarray(rpb[:, :, :, idx])
    return w


CAST = ["wada", "wie", "wkr", "wva", "wuq", "wukvn", "wukvv", "woe", "wioq", "wiok", "wiov", "woo", "wup", "wdn"]


class Builder:
    def __init__(self, shapes, NL=4, NB=2):
        self.NL, self.NB = NL, NB
        nc = self.nc = bass.Bass("TRN2", target_bir_lowering=False)
        self.P = Prog(nc)
        self.din = {}
        for k, shp in shapes.items():
            self.din[k] = nc.dram_tensor(k, list(shp), F32, kind="ExternalInput").ap()
        self.dbf = {}
        self.bbf = {}
        for k in CAST:
            shp = shapes[k]
            self.dbf[k] = nc.dram_tensor(k + "_bf", list(shp), BF16, kind="Internal").ap()
            self.bbf[k] = [self.P.buf(f"{k}{i}") for i in range(shp[0])]
            for b_ in self.bbf[k]:
                b_.keep = True
        self.oT = nc.dram_tensor("oT_scr", [128, 8, T], BF16, kind="Internal").ap()
        self.b_oT = [self.P.buf(f"oT{i}") for i in range(5)]
        self.gsc = [nc.dram_tensor(f"gsc{l}", [128, 2, 2, 1024], F32, kind="Internal").ap() for l in range(4)]
        self.b_gsc = [self.P.buf(f"gsc{l}") for l in range(4)]
        for b_ in self.b_gsc:
            b_.keep = True
        self.out = nc.dram_tensor("out", [NB, S, D], F32, kind="ExternalOutput").ap()
        self.b_out = self.P.buf("out")

    def MM(self, out, lhsT, rhs, start, stop, rd, wr):
        self.P.op("pe", lambda e: e.matmul(out, lhsT=lhsT, rhs=rhs, start=start, stop=stop), rd, wr)

    def TR(self, out, in_, ident, rd, wr):
        self.P.op("pe", lambda e: e.transpose(out=out, in_=in_, identity=ident), rd, wr)

    def ACT(self, out, in_, func, rd, wr, bias=None, scale=None):
        kw = {}
        if bias is not None:
            kw["bias"] = bias
        if scale is not None:
            kw["scale"] = scale
        self.P.op("act", lambda e: e.activation(out=out, in_=in_, func=func, **kw), rd, wr)

    def TT(self, eng, out, in0, in1, op, rd, wr):
        self.P.op(eng, lambda e: e.tensor_tensor(out=out, in0=in0, in1=in1, op=op), rd, wr)

    def TS(self, eng, out, in0, s1, s2, op0, op1, rd, wr):
        if s2 is None:
            self.P.op(eng, lambda e: e.tensor_scalar(out=out, in0=in0, scalar1=s1, scalar2=None, op0=op0), rd, wr)
        else:
            self.P.op(eng, lambda e: e.tensor_scalar(out=out, in0=in0, scalar1=s1, scalar2=s2, op0=op0, op1=op1), rd, wr)

    def STT(self, out, in0, scalar, in1, op0, op1, rd, wr):
        self.P.op("dve", lambda e: e.scalar_tensor_tensor(out=out, in0=in0, scalar=scalar, in1=in1, op0=op0, op1=op1), rd, wr)

    def CP(self, eng, out, in_, rd, wr):
        if eng == "act":
            self.P.op("act", lambda e: e.copy(out=out, in_=in_), rd, wr)
        else:
            self.P.op(eng, lambda e: e.tensor_copy(out=out, in_=in_), rd, wr)

    def RECIP(self, out, in_, rd, wr):
        self.P.op("dve", lambda e: e.reciprocal(out=out, in_=in_), rd, wr)

    def MEMSET(self, eng, ap, val, wr):
        self.P.op(eng, lambda e: e.memset(ap, val), (), wr)

    def DMA(self, q, out, in_, rd, wr):
        self.P.dma(q, lambda e: e.dma_start(out=out, in_=in_), rd, wr)

    def alloc(self, words):
        off = self.top
        self.top += words
        assert self.top <= self.AW, (self.top, self.AW)
        return off

    def f32(self, off, n):
        return self.arena[:, off:off + n]

    def bf(self, off, n):
        return self.arena[:, off:off + (n + 1) // 2].bitcast(BF16)[:, 0:n]

    def tb(self, name=""):
        b = self.P.buf(name)
        b.tmp = True
        return b

    def phase(self):
        self.P.barrier()
        self.top = self.persist_top

    def build(self):
        nc, P = self.nc, self.P
        with ExitStack() as st:
            self.AW = 53000
            self.arena = st.enter_context(nc.sbuf_tensor("arena", [128, self.AW], F32))
            self.ps = st.enter_context(nc.psum_tensor("ps", [128, 8, 512], F32))
            self.pb = P.bufs(8, "psb")
            self.top = 0
            self.o_X = self.alloc(16 * 1024)
            self.o_Z = self.alloc(2 * 1024)
            self.b_X = P.bufs(16, "x")
            self.b_Z = P.bufs(2, "z")
            self.o_idf = self.alloc(128)
            self.o_idb = self.alloc(64)
            self.o_ones = self.alloc(64)
            self.o_sel = self.alloc(48)
            self.o_cs = self.alloc(12 + 4)
            self.o_fm = self.alloc(4 * 96)
            self.o_gbc = self.alloc(4096)
            self.o_cols = self.alloc(44 * 5 + 8 + 40)
            self.b_const = P.buf("const")
            self.b_cs = P.buf("cs")
            self.b_fm = P.buf("fm")
            self.b_gbc = P.buf("gbc")
            self.b_cols = P.buf("cols")
            self.persist_top = self.top
            self.prologue()
            for b in range(self.NB):
                self.load_x(b)
                for l in range(self.NL):
                    self.layer(b, l)
                self.store_x(b)
            P.barrier()
            P.emit()
        return nc

    def X(self, t):
        return self.f32(self.o_X + t * 1024, 1024)

    def Z(self, t):
        return self.f32(self.o_Z + t * 1024, 1024)

    def prologue(self):
        P = self.P
        order = []
        for l in range(4):
            order.append(("wada", l))
            i = l // 2
            if l % 2 == 0:
                for k in ("wie", "wkr", "wva", "wuq", "wukvn", "wukvv", "woe"):
                    if l < 2 or True:
                        order.append((k, i))
            else:
                for k in ("wioq", "wiok", "wiov", "woo"):
                    order.append((k, i))
            order.append(("wup", l))
            order.append(("wdn", l))
        for k, i in order:
            if i >= self.din[k].shape[0]:
                continue
            src = self.din[k][i]
            dst = self.dbf[k][i]
            n = 1
            for s_ in src.shape:
                n *= s_
            letters = "abcdefg"[: len(src.shape)]
            pat = " ".join(letters)
            srcf = src.rearrange(f"{pat} -> ({pat})").rearrange("(r j) -> r j", j=2048)
            dstf = dst.rearrange(f"{pat} -> ({pat})").rearrange("(r j) -> r j", j=2048)
            R = n // 2048
            step = 512
            for r0 in range(0, R, step):
                r1 = min(R, r0 + step)
                self.DMA("pool", dstf[r0:r1, :], srcf[r0:r1, :], (), [self.bbf[k][i]])
        idf = self.f32(self.o_idf, 128)
        self.DMA("sp", idf, self.din["identf"], (), [self.b_const])
        self.CP("dve", self.bf(self.o_idb, 128), idf, [self.b_const], [self.b_const])
        self.MEMSET("pool", self.bf(self.o_ones, 128), 1.0, [self.b_const])
        tmp = self.f32(self.alloc(96), 96)
        self.DMA("sp", tmp[0:32, :], self.din["sel"], (), [self.b_const])
        self.CP("dve", self.bf(self.o_sel, 96)[0:32, :], tmp[0:32, :], [self.b_const], [self.b_const])
        ctmp = self.f32(self.alloc(24), 24)
        self.DMA("sp", ctmp, self.din["cT"].rearrange("p k s -> p (k s)"), (), [self.b_cs])
        self.ACT(self.bf(self.o_cs, 24), ctmp, AF.Silu, [self.b_cs], [self.b_cs])
        self.phase()

    def load_x(self, b):
        for t in range(16):
            self.DMA("sp", self.X(t), self.din["x"][b, t * 128:(t + 1) * 128, :], (), [self.b_X[t]])
        for t in range(2):
            self.DMA("sp", self.Z(t), self.din["ctx"][b, t * 128:(t + 1) * 128, :], (), [self.b_Z[t]])

    def store_x(self, b):
        for t in range(16):
            self.DMA("sp", self.out[b, t * 128:(t + 1) * 128, :], self.X(t), [self.b_X[t]], [self.b_out])

    def blk_tiles(self, blk):
        if blk < 4:
            return [(self.X(4 * blk + j), self.b_X[4 * blk + j]) for j in range(4)]
        return [(self.Z(j), self.b_Z[j]) for j in range(2)]

    def blk_t0(self, blk):
        return blk * 512

    def blk_n(self, blk):
        return 512 if blk < 4 else 256

    def fmc(self, kind, k, s):
        fm = self.f32(self.o_fm + self.cur_l * 96, 96).rearrange("p (a k s) -> p a k s", a=4, k=8)
        col = self.cur_b if s == 0 else 2
        return fm[:, kind, k, col:col + 1]

    def mods(self, b, l):
        self.cur_b, self.cur_l = b, l
        gbc = self.f32(self.o_gbc, 4096).rearrange("p (g s n) -> p g s n", g=2, s=2)
        if b == 1:
            self.DMA("sp", gbc, self.gsc[l], [self.b_gsc[l]], [self.b_gbc])
            self.phase()
            return
        fm = self.f32(self.o_fm + l * 96, 96).rearrange("p (a k s) -> p a k s", a=4, k=8)
        cs = self.bf(self.o_cs, 24).rearrange("p (k s) -> p k s", s=3)
        rep = self.bf(self.alloc(1536), 3072).rearrange("p (k s m) -> p k s m", k=8, s=3)
        b_rep = self.tb()
        for s in range(3):
            self.CP("dve", rep[:, :, s, :], cs[:, :, s:s + 1].to_broadcast([128, 8, 128]), [self.b_cs], [b_rep])
        bT = self.f32(self.alloc(48), 48)
        b_bT = self.tb()
        self.DMA("sp", bT, self.din["badaT"][l], (), [b_bT])
        for c0 in (8, 32):
            self.TS("dve", bT[:, c0:c0 + 8], bT[:, c0:c0 + 8], 1.0, None, ALU.add, None, [b_bT], [b_bT])
        bb = self.f32(self.alloc(2048), 2048)
        b_bb = self.tb()
        for g, c0 in ((0, 2048), (1, 5120)):
            self.DMA("sp", bb[:, g * 1024:(g + 1) * 1024], self.din["bada"][l:l + 1, c0:c0 + 1024].to_broadcast([128, 1024]), (), [b_bb])
        g1 = self.f32(self.alloc(2048), 2048).rearrange("p (g n) -> p g n", g=2)
        b_g1 = self.tb()
        wts = [(self.bf(self.alloc(4096), 8192).rearrange("p (c k m) -> p c k m", c=8, k=8), self.tb()) for _ in range(4)]
        kinds = {0: 0, 1: 1, 3: 2, 4: 3}
        n = 0
        for c in range(48):
            sec, cc = c // 8, c % 8
            wt8, bw = wts[sec % 4]
            if cc == 0:
                self.DMA("sp", wt8, self.dbf["wada"][l, sec * 8:(sec + 1) * 8].rearrange("c p k m -> p c k m"), [self.bbf["wada"][l]], [bw])
            wt = wt8[:, cc, :, :]
            bank = n % 2
            n += 1
            if sec in kinds:
                pso = self.ps[:, bank, 0:3]
                for k in range(8):
                    self.MM(pso, wt[:, k, :], cs[:, k, :], k == 0, k == 7, [bw, self.b_cs], [self.pb[bank]])
                self.ACT(fm[:, kinds[sec], cc, :], pso, AF.Identity, [self.pb[bank], b_bT], [self.b_fm], bias=bT[:, c:c + 1])
            else:
                g = 0 if sec == 2 else 1
                for s in range(3):
                    pso = self.ps[:, bank, s * 128:(s + 1) * 128]
                    for k in range(8):
                        self.MM(pso, rep[:, k, s, :], wt[:, k, :], k == 0, k == 7, [bw, b_rep], [self.pb[bank]])
                    if s == 1:
                        dst, b_dst = g1[:, g, cc * 128:(cc + 1) * 128], b_g1
                    else:
                        dst, b_dst = gbc[:, g, s // 2, cc * 128:(cc + 1) * 128], self.b_gbc
                    self.TT("dve", dst, pso, bb[:, g * 1024 + cc * 128:g * 1024 + (cc + 1) * 128], ALU.add, [self.pb[bank], b_bb], [b_dst])
        self.DMA("sp", self.gsc[l][:, :, 0, :], g1, [b_g1], [self.b_gsc[l]])
        self.DMA("sp", self.gsc[l][:, :, 1, :], gbc[:, :, 1, :], [self.b_gbc], [self.b_gsc[l]])
        self.phase()

    def make_hT(self, blk, kind, hT, b_hT, col0=0, banks=(0, 1)):
        s = 0 if blk < 4 else 1
        tiles = self.blk_tiles(blk)
        idf = self.f32(self.o_idf, 128)
        n = len(tiles) * 128
        for k in range(8):
            bank = banks[k % 2]
            for j, (xt, bx) in enumerate(tiles):
                self.TR(self.ps[:, bank, j * 128:(j + 1) * 128], xt[:, k * 128:(k + 1) * 128], idf, [bx, self.b_const], [self.pb[bank]])
            self.ACT(hT[:, k, col0:col0 + n], self.ps[:, bank, 0:n], AF.Identity, [self.pb[bank], self.b_fm], [b_hT],
                     bias=self.fmc(2 * kind, k, s), scale=self.fmc(2 * kind + 1, k, s))

    def attend(self, qk_list, nq, scale, pv, out_rows, out_ap, b_out, rd, mask_fn=None, sink=None):
        per = max(1, 512 // nq)
        nk = len(qk_list)
        ob = self.out_banks[self.att_n % len(self.out_banks)]
        self.att_n += 1
        o_ps = self.ps[:, ob, 0:nq]
        groups = [(g0, min(nk, g0 + per)) for g0 in range(0, nk, per)]
        for gi, (g0, g1) in enumerate(groups):
            sb = self.qk_banks[self.qk_n % len(self.qk_banks)]
            self.qk_n += 1
            slot = self.pt_n % len(self.pt_bufs)
            self.pt_n += 1
            pt, ptf, b_pt = self.pt_bufs[slot]

            def A(g0=g0, g1=g1, sb=sb, pt=pt, ptf=ptf, b_pt=b_pt):
                for j in range(g0, g1):
                    kT, qT = qk_list[j]
                    self.MM(self.ps[0:128, sb, (j - g0) * nq:(j - g0 + 1) * nq], kT, qT, True, True, rd, [self.pb[sb]])
                w = (g1 - g0) * nq
                masked = mask_fn is not None and any(mask_fn(j) is not None for j in range(g0, g1))
                if not masked:
                    self.ACT(pt[:, 0:w], self.ps[:, sb, 0:w], AF.Exp, [self.pb[sb]], [b_pt], scale=scale)
                    return
                j = g0
                any_f32 = False
                while j < g1:
                    m = mask_fn(j)
                    c0 = (j - g0) * nq
                    if m is None:
                        j2 = j
                        while j2 < g1 and mask_fn(j2) is None:
                            j2 += 1
                        c1 = (j2 - g0) * nq
                        if nq <= 64:
                            if not any_f32:
                                self.ACT(ptf[:, 0:w], self.ps[:, sb, 0:w], AF.Exp, [self.pb[sb]], [b_pt], scale=scale)
                                any_f32 = True
                            self.CP("pool", pt[:, c0:c1], ptf[:, c0:c1], [b_pt], [b_pt])
                        else:
                            self.ACT(pt[:, c0:c1], self.ps[:, sb, c0:c1], AF.Exp, [self.pb[sb]], [b_pt], scale=scale)
                        j = j2
                    else:
                        m_ap, m_rd, span = m
                        c1 = c0 + span * nq
                        if nq <= 64:
                            if not any_f32:
                                self.ACT(ptf[:, 0:w], self.ps[:, sb, 0:w], AF.Exp, [self.pb[sb]], [b_pt], scale=scale)
                                any_f32 = True
                        else:
                            self.ACT(ptf[:, c0:c1], self.ps[:, sb, c0:c1], AF.Exp, [self.pb[sb]], [b_pt], scale=scale)
                        o3 = pt[:, c0:c1]
                        i3 = ptf[:, c0:c1]
                        if len(m_ap.shape) == 3:
                            o3 = o3.rearrange("p (a b) -> p a b", a=m_ap.shape[1])
                            i3 = i3.rearrange("p (a b) -> p a b", a=m_ap.shape[1])
                        self.TT("pool" if (span == 1 and nq <= 128) else "dve", o3, i3, m_ap, ALU.mult, [b_pt] + m_rd, [b_pt])
                        j += span

            def B(g0=g0, g1=g1, gi=gi, pt=pt, b_pt=b_pt):
                for j in range(g0, g1):
                    last = (j == nk - 1) and sink is None
                    self.MM(o_ps, pv[j], pt[:, (j - g0) * nq:(j - g0 + 1) * nq], gi == 0 and j == g0, last, rd + [b_pt], [self.pb[ob]])
                if gi != len(groups) - 1:
                    return
                if sink is not None:
                    s_l, s_r, s_rd = sink
                    self.MM(o_ps, s_l, s_r, False, True, s_rd, [self.pb[ob]])
                slot2 = self.rc_n % len(self.rc_bufs)
                self.rc_n += 1
                rec, b_rec = self.rc_bufs[slot2]
                sr = 64 - out_rows
                if nq <= 64:
                    self.RECIP(rec[sr:sr + 64, 0:nq], self.ps[sr:sr + 64, ob, 0:nq], [self.pb[ob]], [b_rec])
                else:
                    self.ACT(rec[sr:sr + 64, 0:nq], self.ps[sr:sr + 64, ob, 0:nq], AF.Ln, [self.pb[ob]], [b_rec])
                    self.ACT(rec[sr:sr + 64, 0:nq], rec[sr:sr + 64, 0:nq], AF.Exp, [b_rec], [b_rec], scale=-1.0)
                num = self.ps[out_rows:out_rows + 64, ob, 0:nq]
                den = rec[sr:sr + 64, 0:nq]
                if len(out_ap.shape) == 3:
                    num = num.rearrange("p (a b) -> p a b", a=out_ap.shape[1])
                    den = den.rearrange("p (a b) -> p a b", a=out_ap.shape[1])
                self.TT("dve", out_ap, num, den, ALU.mult, [self.pb[ob], b_rec], [b_out])

            A()
            self.pending.append(B)
            if len(self.pending) > self.DEPTH:
                self.pending.pop(0)()

    def pipe_flush(self):
        while self.pending:
            self.pending.pop(0)()

    def attn_bufs(self, ptw=512, depth=2, masked=True, rcw=512):
        self.att_n = self.qk_n = self.pt_n = self.rc_n = 0
        self.DEPTH = depth
        self.pending = []
        self.qk_banks = [4, 5, 2, 3, 0, 1][: depth + 1]
        self.out_banks = [6, 7]
        self.pt_bufs = []
        for i in range(depth + 2):
            o1 = self.alloc(ptw // 2)
            o2 = self.alloc(ptw) if masked else o1
            self.pt_bufs.append((self.bf(o1, ptw), self.f32(o2, ptw) if masked else None, self.tb()))
        self.rc_bufs = [(self.f32(self.alloc(rcw), rcw), self.tb()) for _ in range(3)]

    def wring(self, n, words, shape_fn):
        return [(shape_fn(self.bf(self.alloc(words), words * 2)), self.tb()) for _ in range(n)]

    def resid_ln(self, tiles, s, gate, ps_groups):
        gbc = self.f32(self.o_gbc, 4096).rearrange("p (g s n) -> p g s n", g=2, s=2)
        for j, (xt, bx) in enumerate(tiles):
            t, b_t = self.ln_t[self.ln_n % 2]
            st, b_st = self.ln_s[self.ln_n % 2]
            self.ln_n += 1
            for hf in range(2):
                bank = ps_groups[j][hf]
                self.TT("dve", t[:, hf * 512:(hf + 1) * 512], self.ps[:, bank, :], gbc[:, gate, s, hf * 512:(hf + 1) * 512], ALU.mult, [self.pb[bank], self.b_gbc], [b_t])
            self.STT(t, xt, ALPHA, t, ALU.mult, ALU.add, [bx, b_t], [b_t])
            for hf in range(2):
                self.P.op("dve", (lambda e, o=st[:, hf * 6:(hf + 1) * 6], i=t[:, hf * 512:(hf + 1) * 512]: e.bn_stats(out=o, in_=i)), [b_t], [b_st])
            self.P.op("dve", (lambda e, o=st[:, 12:14], i=st[:, 0:12]: e.bn_aggr(out=o, in_=i)), [b_st], [b_st])
            self.ACT(st[:, 14:15], st[:, 13:14], AF.Sqrt, [b_st, self.b_eps], [b_st], bias=self.epscol, scale=1.0)
            self.RECIP(st[:, 15:16], st[:, 14:15], [b_st], [b_st])
            self.TS("dve", xt, t, st[:, 12:13], st[:, 15:16], ALU.subtract, ALU.mult, [b_t, b_st], [bx])

    def ln_bufs(self):
        self.ln_n = 0
        self.ln_t = [(self.f32(self.alloc(1024), 1024), self.tb()) for _ in range(2)]
        self.ln_s = [(self.f32(self.alloc(16), 16), self.tb()) for _ in range(2)]
        o = self.alloc(1)
        self.epscol = self.f32(o, 1)
        self.b_eps = self.tb()
        self.MEMSET("pool", self.epscol, EPS, [self.b_eps])

    def outproj(self, l, with_ctx):
        i = l // 2
        wname = "woe" if l % 2 == 0 else "woo"
        wo = self.bf(self.alloc(4096), 8192).rearrange("p (k n) -> p k n", k=8)
        b_wok = [self.tb() for _ in range(8)]
        for k in range(8):
            self.DMA("sp", wo[:, k, :], self.dbf[wname][i][:, k, :], [self.bbf[wname][i]], [b_wok[k]])
        self.ln_bufs()
        obufs = [(self.bf(self.alloc(2048), 4096).rearrange("p (k n) -> p k n", k=8), self.tb()) for _ in range(2)]
        nb = 5 if with_ctx else 4
        for blk in range(nb):
            oT, b_o = obufs[blk % 2]
            n = self.blk_n(blk)
            t0 = self.blk_t0(blk)
            self.DMA("sp", oT[:, :, 0:n], self.oT[:, :, t0:t0 + n], [self.b_oT[blk]], [b_o])
            tiles = self.blk_tiles(blk)
            for j0 in range(0, len(tiles), 2):
                groups = []
                for j in range(j0, min(len(tiles), j0 + 2)):
                    banks = [2 * ((j - j0) + 2 * (self.op_n % 2)) + hf for hf in range(2)]
                    groups.append(banks)
                    for hf in range(2):
                        for k in range(8):
                            self.MM(self.ps[:, banks[hf], :], oT[:, k, j * 128:(j + 1) * 128], wo[:, k, hf * 512:(hf + 1) * 512], k == 0, k == 7, [b_o, b_wok[k]], [self.pb[banks[hf]]])
                self.op_n += 1
                self.resid_ln(tiles[j0:j0 + 2], 0 if blk < 4 else 1, 0, groups)
        self.phase()

    def ffn(self, l, with_ctx):
        P = self.P
        cols = self.f32(self.o_cols, 44 * 5 + 8)
        bup = cols[:, 0:44]
        cw = cols[:, 44:176].rearrange("p (i c) -> p i c", i=3)
        cb = cols[:, 176:220]
        b_c = self.b_cols
        self.DMA("sp", bup, self.din["bupT"][l], (), [b_c])
        self.DMA("sp", cols[:, 44:176], self.din["cwT"][l].rearrange("p i c -> p (i c)"), (), [b_c])
        self.DMA("sp", cb, self.din["cbT"][l], (), [b_c])
        o_bd = self.alloc(1024 + 512)
        bdf = self.f32(o_bd, 1024)
        bdb = self.bf(o_bd + 1024, 1024)
        b_bd = self.tb()
        self.DMA("sp", bdf[0:1, :], self.din["bdn"][l:l + 1, :], (), [b_bd])
        self.CP("dve", bdb[0:1, :], bdf[0:1, :], [b_bd], [b_bd])
        ones = self.bf(self.o_ones, 128)
        self.ln_bufs()
        hT = self.bf(self.alloc(2048 + 8), 4096 + 16).rearrange("p (k n) -> p k n", k=8)
        b_hT = self.tb()
        o_hb = self.alloc(32)
        hbnd = self.bf(o_hb, 64).rearrange("p (k n) -> p k n", k=8)
        b_hb = self.tb()
        ubnd = self.f32(self.alloc(44 * 8), 44 * 8).rearrange("p (c n) -> p c n", c=44)
        b_ub = self.tb()
        actT = self.bf(self.alloc(NCH * 256), NCH * 512).rearrange("p (c n) -> p c n", c=NCH)
        b_act = [self.tb() for _ in range(NCH)]
        NR = 4
        ua = [(self.f32(self.alloc(516), 516), self.tb(), self.tb()) for _ in range(2 * NR)]
        acc = [(self.f32(self.alloc(512), 512), self.tb()) for _ in range(2 * NR)]
        wup = [(self.bf(self.alloc(1024), 2048).rearrange("p (k m) -> p k m", k=8), self.tb()) for _ in range(4)]
        wdn = [(self.bf(self.alloc(512), 1024), self.tb()) for _ in range(4)]
        bias2 = self.f32(self.alloc(44), 44)
        b_b2 = self.tb()
        self.STT(bias2, bup, 1.0, cw[:, 1, :], ALU.mult, ALU.mult, [b_c], [b_b2])
        self.TT("dve", bias2, bias2, cb, ALU.add, [b_b2, b_c], [b_b2])
        idf = self.f32(self.o_idf, 128)
        for bi in range(3):
            for side in range(2):
                tile = 4 * (bi + 1) - 1 + side
                xt, bx = self.X(tile), self.b_X[tile]
                p0 = 64 if side == 0 else 0
                hb = 4 + (2 * bi + side) % 4
                for k in range(8):
                    self.TR(self.ps[:, hb, k * 64:(k + 1) * 64], xt[p0:p0 + 64, k * 128:(k + 1) * 128], idf[p0:p0 + 64, p0:p0 + 64], [bx, self.b_const], [self.pb[hb]])
                cc_ = 63 if side == 0 else 0
                for k in range(8):
                    self.ACT(hbnd[:, k, 2 * bi + side:2 * bi + side + 1], self.ps[:, hb, k * 64 + cc_:k * 64 + cc_ + 1], AF.Identity, [self.pb[hb], self.b_fm], [b_hb],
                             bias=self.fmc(2, k, 0), scale=self.fmc(3, k, 0))
        nwin = 5 if with_ctx else 4
        wn = 0
        dn = 0
        un = 0
        deferred = None
        fin = None
        for win in range(nwin):
            n = self.blk_n(win)
            s = 0 if win < 4 else 1
            U = [0, 1, 2, 3] if win % 2 == 0 else [4, 5, 6, 7]
            DA = [4, 5, 6, 7] if win % 2 == 0 else [0, 1, 2, 3]
            self.make_hT(win, 1, hT, b_hT, banks=(U[0], U[1]))
            for c in range(NCH):
                if c == 4 and deferred is not None:
                    deferred()
                    deferred = None
                wt, bw = wup[wn % 4]
                wn += 1
                if not (win > 0 and c < 4):
                    self.DMA("sp", wt, self.dbf["wup"][l, c], [self.bbf["wup"][l]], [bw])
                for half in range(2):
                    ci = c + half * NCH
                    bank = U[(c % 2) * 2 + half]
                    pso = self.ps[:, bank, 0:n]
                    for k in range(8):
                        self.MM(pso, wt[:, k, half * 128:(half + 1) * 128], hT[:, k, 0:n], k == 0, k == 7, [bw, b_hT], [self.pb[bank]])
                    if win == 0:
                        psb = self.ps[:, 6, ci * 8:ci * 8 + 6]
                        for k in range(8):
                            self.MM(psb, wt[:, k, half * 128:(half + 1) * 128], hbnd[:, k, 0:6], k == 0, k == 7, [bw, b_hb], [self.pb[6]])
                        self.ACT(ubnd[:, ci, 0:6], psb, AF.Identity, [self.pb[6], b_c], [b_ub], bias=bup[:, ci:ci + 1])
                    u, b_u, b_uh = ua[(un % NR) * 2 + half]
                    a_, b_a = acc[(un % NR) * 2 + half]
                    self.ACT(u[:, 1:n + 1], pso, AF.Identity, [self.pb[bank], b_c], [b_u], bias=bup[:, ci:ci + 1])
                    self.ACT(a_[:, 0:n], pso, AF.Identity, [self.pb[bank], b_c, b_b2], [b_a], bias=bias2[:, ci:ci + 1], scale=cw[:, 1, ci:ci + 1])
                    if win in (1, 2, 3):
                        self.CP("pool", u[:, 0:1], ubnd[:, ci, 2 * win - 2:2 * win - 1], [b_ub], [b_uh])
                    else:
                        self.MEMSET("pool", u[:, 0:1], 0.0, [b_uh])
                    if win in (0, 1, 2):
                        self.CP("pool", u[:, n + 1:n + 2], ubnd[:, ci, 2 * win + 1:2 * win + 2], [b_ub], [b_uh])
                    else:
                        self.MEMSET("pool", u[:, n + 1:n + 2], 0.0, [b_uh])
                    self.STT(a_[:, 0:n], u[:, 0:n], cw[:, 0, ci:ci + 1], a_[:, 0:n], ALU.mult, ALU.add, [b_u, b_uh, b_a, b_c], [b_a])
                    self.STT(a_[:, 0:n], u[:, 2:n + 2], cw[:, 2, ci:ci + 1], a_[:, 0:n], ALU.mult, ALU.add, [b_u, b_uh, b_a, b_c], [b_a])
                aa, b_aa = acc[(un % NR) * 2]
                ag, b_ag = acc[(un % NR) * 2 + 1]
                un += 1
                if fin is not None:
                    fin()

                def fin(aa=aa, ag=ag, b_aa=b_aa, b_ag=b_ag, c=c, n=n):
                    self.ACT(ag[:, 0:n], ag[:, 0:n], AF.Silu, [b_ag], [b_ag])
                    self.TT("pool", actT[:, c, 0:n], aa[:, 0:n], ag[:, 0:n], ALU.mult, [b_aa, b_ag], [b_act[c]])
            fin()
            fin = None
            tiles = self.blk_tiles(win)
            nt = len(tiles)
            if win + 1 < nwin:
                for c2 in range(4):
                    wt2, bw2 = wup[(wn + c2) % 4]
                    self.DMA("sp", wt2, self.dbf["wup"][l, c2], [self.bbf["wup"][l]], [bw2])
            for pas in range((nt + 1) // 2):
                B4 = DA if pas == 0 else U
                tl = list(range(2 * pas, min(nt, 2 * pas + 2)))
                for c in range(NCH):
                    wt, bw = wdn[dn % 4]
                    dn += 1
                    self.DMA("sp", wt, self.dbf["wdn"][l][:, c, :], [self.bbf["wdn"][l]], [bw])
                    for jj, j in enumerate(tl):
                        for hf in range(2):
                            bank = B4[2 * jj + hf]
                            self.MM(self.ps[:, bank, :], actT[:, c, j * 128:(j + 1) * 128], wt[:, hf * 512:(hf + 1) * 512], c == 0, False, [b_act[c], bw], [self.pb[bank]])
                for jj, j in enumerate(tl):
                    for hf in range(2):
                        bank = B4[2 * jj + hf]
                        self.MM(self.ps[:, bank, :], ones[0:1, 0:128], bdb[0:1, hf * 512:(hf + 1) * 512], False, True, [b_bd, self.b_const], [self.pb[bank]])
                ln = (lambda tl=tl, B4=B4, s=s, tiles=tiles: self.resid_ln([tiles[j] for j in tl], s, 1, [[B4[2 * jj], B4[2 * jj + 1]] for jj in range(len(tl))]))
                if pas == 0 or win == nwin - 1:
                    ln()
                else:
                    deferred = ln
        if deferred is not None:
            deferred()
        self.phase()

    def swa_pass(self, l, with_ctx):
        i = l // 2
        P = self.P
        kT = self.bf(self.alloc(2 * T), 4 * T).rearrange("p (r g n) -> p r g n", r=2, g=2)
        b_kT = self.tb()
        self.MEMSET("pool", kT[64:128, 0, :, :], 0.0, [b_kT])
        self.MEMSET("pool", kT[0:64, 1, :, :], 0.0, [b_kT])
        vv = self.bf(self.alloc(18 * 192), 18 * 384).rearrange("p (t g c) -> p t g c", t=18, g=2)
        b_v = self.tb()
        self.MEMSET("pool", vv[:, :, :, 64:128], 1.0, [b_v])
        cosb = self.f32(self.alloc(512), 512)
        sinb = self.f32(self.alloc(512), 512)
        b_tab = self.tb()
        hT = self.bf(self.alloc(2048), 4096).rearrange("p (k n) -> p k n", k=8)
        b_hT = self.tb()
        wv = self.bf(self.alloc(512), 1024).rearrange("p (k n) -> p k n", k=8)
        b_wv = self.tb()
        self.DMA("sp", wv, self.dbf["wiov"][i], [self.bbf["wiov"][i]], [b_wv])
        wts = self.wring(2, 1024, lambda a: a.rearrange("p (k m) -> p k m", k=8))
        t1 = [(self.f32(self.alloc(512), 512), self.tb()) for _ in range(2)]
        t2 = [(self.f32(self.alloc(512), 512), self.tb()) for _ in range(2)]
        wn = 0

        def rope_proj(wsrc, wbuf, n, t0, out_ap, b_out):
            nonlocal wn
            wt, bw = wts[wn % 2]
            self.DMA("sp", wt, wsrc, [wbuf], [bw])
            ba, bb_ = 2, 3
            for k in range(8):
                self.MM(self.ps[:, ba, 0:n], wt[:, k, 0:128], hT[:, k, 0:n], k == 0, k == 7, [bw, b_hT], [self.pb[ba]])
            for k in range(8):
                self.MM(self.ps[:, bb_, 0:n], wt[:, k, 128:256], hT[:, k, 0:n], k == 0, k == 7, [bw, b_hT], [self.pb[bb_]])
            a, b_a = t1[wn % 2]
            b2, b_b2 = t2[wn % 2]
            wn += 1
            self.TT("dve", a[:, 0:n], self.ps[:, ba, 0:n], cosb[:, 0:n], ALU.mult, [self.pb[ba], b_tab], [b_a])
            self.TT("dve", b2[:, 0:n], self.ps[:, bb_, 0:n], sinb[:, 0:n], ALU.mult, [self.pb[bb_], b_tab], [b_b2])
            if isinstance(out_ap, tuple):
                for par_, o_ in enumerate(out_ap):
                    r_ = slice(par_ * 64, par_ * 64 + 64)
                    self.TT("pool", o_[r_, :], a[r_, 0:n], b2[r_, 0:n], ALU.add, [b_a, b_b2], [b_out])
            else:
                self.TT("pool", out_ap, a[:, 0:n], b2[:, 0:n], ALU.add, [b_a, b_b2], [b_out])

        def load_tabs(blk):
            n, t0 = self.blk_n(blk), self.blk_t0(blk)
            self.DMA("sp", cosb[:, 0:n], self.din["cosS"][:, t0:t0 + n], (), [b_tab])
            self.DMA("sp", sinb[:, 0:n], self.din["sinS"][:, t0:t0 + n], (), [b_tab])

        for blk in range(5):
            n, t0 = self.blk_n(blk), self.blk_t0(blk)
            self.make_hT(blk, 0, hT, b_hT)
            load_tabs(blk)
            for g in range(2):
                rope_proj(self.dbf["wiok"][i, g], self.bbf["wiok"][i], n, t0, (kT[:, 0, g, t0:t0 + n], kT[:, 1, g, t0:t0 + n]), b_kT)
            for j in range(n // 128):
                bank = 6 + j % 2
                for k in range(8):
                    self.MM(self.ps[:, bank, 0:128], hT[:, k, j * 128:(j + 1) * 128], wv[:, k, :], k == 0, k == 7, [b_hT, b_wv], [self.pb[bank]])
                tt = t0 // 128 + j
                pv2 = self.ps[:, bank, 0:128].rearrange("p (g c) -> p g c", g=2)
                self.CP("act", vv[:, tt, :, 0:64], pv2, [self.pb[bank]], [b_v])
                self.CP("dve", vv[:, tt, :, 128:192], pv2, [self.pb[bank]], [b_v])
        o_s = self.alloc(16 + 1024)
        sraw = self.f32(o_s, 16)
        srow = self.bf(o_s + 16, 2048).rearrange("p (h q) -> p h q", h=16)
        b_s = self.tb()
        self.DMA("sp", sraw[0:1, :], self.din["sinks"][i:i + 1, :], (), [b_s])
        self.ACT(sraw[0:1, :], sraw[0:1, :], AF.Exp, [b_s], [b_s])
        self.CP("dve", srow[0:1, :, :], sraw[0:1, :].unsqueeze(2).to_broadcast([1, 16, 128]), [b_s], [b_s])
        o_sl = self.alloc(128)
        sl = self.bf(o_sl, 256)
        self.MEMSET("pool", sl[0:1, 0:256], 0.0, [b_s])
        self.MEMSET("pool", sl[0:1, 64:128], 1.0, [b_s])
        ml = self.f32(self.alloc(128), 128)
        mr = self.f32(self.alloc(128), 128)
        b_m = self.tb()
        self.DMA("sp", ml, self.din["swaml"], (), [b_m])
        self.DMA("sp", mr, self.din["swamr"], (), [b_m])
        qb = self.bf(self.alloc(2048), 4096).rearrange("p (c n) -> p c n", c=8)
        b_q = self.tb()
        ob = [(self.bf(self.alloc(2048), 4096).rearrange("p (c n) -> p c n", c=8), self.tb()) for _ in range(2)]
        self.attn_bufs(512, depth=3, masked=True)
        nb = 5 if with_ctx else 4
        for blk in range(nb):
            n, t0 = self.blk_n(blk), self.blk_t0(blk)
            self.make_hT(blk, 0, hT, b_hT)
            load_tabs(blk)
            for c in range(8):
                rope_proj(self.dbf["wioq"][i, c], self.bbf["wioq"][i], n, t0, qb[:, c, 0:n], b_q)
            oT, b_o = ob[blk % 2]
            for qi in range(n // 128):
                if blk < 4:
                    nblk = blk * 4 + qi
                    kts = [(kt, m) for kt, m in ((nblk - 1, ml), (nblk, None), (nblk + 1, mr)) if 0 <= kt < 16] + [(16, None), (17, None)]
                else:
                    kts = [(16, None), (17, None)]
                for g in range(2):
                    for par in range(2):
                        base = par * 64
                        qsl = qb[:, 4 * g:4 * g + 4, qi * 128:(qi + 1) * 128]
                        qk = [(kT[:, par, g, kt * 128:(kt + 1) * 128], qsl) for kt, _ in kts]
                        pv = [vv[:, kt, g, base:base + 128] for kt, _ in kts]
                        masks = [m for _, m in kts]
                        mf = (lambda j, masks=masks: None if masks[j] is None else (masks[j].unsqueeze(1).to_broadcast([128, 4, 128]), [b_m], 1))
                        h0 = 8 * g + par
                        sink = (sl[0:1, base:base + 128], srow[0:1, h0:h0 + 7:2, :], [b_s])
                        self.attend(qk, 512, 0.125, pv, base, oT[base:base + 64, 4 * g:4 * g + 4, qi * 128:(qi + 1) * 128], b_o, [b_kT, b_q, b_v], mf, sink)
            self.pipe_flush()
            self.DMA("sp", self.oT[:, :, t0:t0 + n], oT[:, :, 0:n], [b_o], [self.b_oT[blk]])
        self.phase()

    def na_pass(self, l, with_ctx):
        i = l // 2
        NSL = len(NA_SLOTS)
        kaT = self.bf(self.alloc(2 * T), 4 * T).rearrange("p (c n) -> p c n", c=4)
        b_k = self.tb()
        va = self.bf(self.alloc(18 * 4 * 96), 18 * 4 * 192).rearrange("p (t g c) -> p t g c", t=18, g=4)
        b_v = self.tb()
        self.MEMSET("pool", va[:, :, :, 64:128], 1.0, [b_v])
        hT = self.bf(self.alloc(2048), 4096).rearrange("p (k n) -> p k n", k=8)
        b_hT = self.tb()
        wv = self.bf(self.alloc(2048), 4096).rearrange("p (k n) -> p k n", k=8)
        b_wv = self.tb()
        self.DMA("sp", wv, self.dbf["wva"][i], [self.bbf["wva"][i]], [b_wv])
        wts = self.wring(2, 512, lambda a: a.rearrange("p (k m) -> p k m", k=8))
        wn = 0
        for blk in range(5):
            n, t0 = self.blk_n(blk), self.blk_t0(blk)
            self.make_hT(blk, 0, hT, b_hT)
            for c in range(4):
                wt, bw = wts[wn % 2]
                wn += 1
                self.DMA("sp", wt, self.dbf["wie"][i, 4 + c], [self.bbf["wie"][i]], [bw])
                bank = 2 + c % 2
                for k in range(8):
                    self.MM(self.ps[:, bank, 0:n], wt[:, k, :], hT[:, k, 0:n], k == 0, k == 7, [bw, b_hT], [self.pb[bank]])
                self.CP("act", kaT[:, c, t0:t0 + n], self.ps[:, bank, 0:n], [self.pb[bank]], [b_k])
            for j in range(n // 128):
                bank = 6 + j % 2
                for k in range(8):
                    self.MM(self.ps[:, bank, :], hT[:, k, j * 128:(j + 1) * 128], wv[:, k, :], k == 0, k == 7, [b_hT, b_wv], [self.pb[bank]])
                tt = t0 // 128 + j
                pv4 = self.ps[:, bank, :].rearrange("p (g c) -> p g c", g=4)
                self.CP("act", va[:, tt, :, 0:64], pv4[:, :, 0:64], [self.pb[bank]], [b_v])
                self.CP("dve", va[:, tt, :, 128:192], pv4[:, :, 64:128], [self.pb[bank]], [b_v])
        msk = self.f32(self.alloc(NSL * 64), NSL * 64).rearrange("p (s q) -> p s q", s=NSL)
        b_m = self.tb()
        self.DMA("sp", msk, self.din["namask"], (), [b_m])
        E = [(self.f32(self.alloc(NSL * 64), NSL * 64).rearrange("p (s q) -> p s q", s=NSL), self.tb()) for _ in range(2)]
        qb = self.bf(self.alloc(1024), 2048).rearrange("p (c n) -> p c n", c=4)
        b_q = self.tb()
        ob = [(self.bf(self.alloc(1024), 2048).rearrange("p (c n) -> p c n", c=4), self.tb()) for _ in range(2)]
        self.attn_bufs(512, depth=3, masked=True, rcw=256)
        rp = self.din["rpb"]
        nb = 5 if with_ctx else 4
        en = 0
        for blk in range(nb):
            n, t0 = self.blk_n(blk), self.blk_t0(blk)
            self.make_hT(blk, 0, hT, b_hT)
            for c in range(4):
                wt, bw = wts[wn % 2]
                wn += 1
                self.DMA("sp", wt, self.dbf["wie"][i, c], [self.bbf["wie"][i]], [bw])
                bank = 2 + c % 2
                for k in range(8):
                    self.MM(self.ps[:, bank, 0:n], wt[:, k, :], hT[:, k, 0:n], k == 0, k == 7, [bw, b_hT], [self.pb[bank]])
                self.CP("act", qb[:, c, 0:n], self.ps[:, bank, 0:n], [self.pb[bank]], [b_q])
            oT, b_o = ob[blk % 2]
            for h in range(8):
                c, base = h // 2, (h % 2) * 64
                if blk < 4:
                    Et, b_E = E[en % 2]
                    en += 1
                    def src(dr, cnt, step):
                        return rp[i, h, dr:dr + cnt * step:step, :, :].rearrange("d k q -> k d q")
                    self.DMA("sp", Et[0:64, 0:14, :], src(0, 14, 1), (), [b_E])
                    self.DMA("sp", Et[64:128, 0:14, :], src(1, 14, 1), (), [b_E])
                    self.DMA("sp", Et[0:64, 14:19, :], src(2, 5, 2), (), [b_E])
                    self.DMA("sp", Et[64:128, 14:19, :], src(3, 5, 2), (), [b_E])
                    Ef = Et.rearrange("p s q -> p (s q)")
                    self.ACT(Ef, Ef, AF.Exp, [b_E], [b_E])
                    self.TT("pool", Ef, Ef, msk.rearrange("p s q -> p (s q)"), ALU.mult, [b_E, b_m], [b_E])
                    for rr in range(8):
                        r = blk * 8 + rr
                        r0 = min(max(r - 4, 0), 24)
                        if r0 % 2 == 0:
                            tiles_ = [r0 // 2 + j for j in range(4)]
                            d0 = r0 - r + 7
                            eslice = Et[:, d0:d0 + 8:2, :] if True else None
                            span = 4
                        else:
                            tiles_ = [(r0 - 1) // 2 + j for j in range(5)]
                            eslice = Et[:, 14:19, :]
                            span = 5
                        kts = tiles_ + [16, 17]
                        qk = [(kaT[base:base + 64, c, kt * 128:(kt + 1) * 128], qb[base:base + 64, c, rr * 64:(rr + 1) * 64]) for kt in kts]
                        pv = [va[:, kt, c, base:base + 128] for kt in kts]
                        mf = (lambda j, eslice=eslice, span=span, b_E=b_E: (eslice, [b_E], span) if j == 0 else None)
                        self.attend(qk, 64, 0.125, pv, base, oT[base:base + 64, c, rr * 64:(rr + 1) * 64], b_o, [b_k, b_q, b_v], mf)
                else:
                    kts = [16, 17]
                    qk = [(kaT[base:base + 64, c, kt * 128:(kt + 1) * 128], qb[base:base + 64, c, 0:256]) for kt in kts]
                    pv = [va[:, kt, c, base:base + 128] for kt in kts]
                    self.attend(qk, 256, 0.125, pv, base, oT[base:base + 64, c, 0:256], b_o, [b_k, b_q, b_v])
            self.pipe_flush()
            self.DMA("sp", self.oT[:, 0:4, t0:t0 + n], oT[:, :, 0:n], [b_o], [self.b_oT[blk]])
        self.phase()

    def mla_pass(self, l, hh, with_ctx):
        i = l // 2
        sc = 96.0 ** -0.5
        km = self.bf(self.alloc(2 * T), 4 * T).rearrange("p (h n) -> p h n", h=4)
        b_k = self.tb()
        vm = self.bf(self.alloc(18 * 2 * 96), 18 * 2 * 192).rearrange("p (t g c) -> p t g c", t=18, g=2)
        b_v = self.tb()
        self.MEMSET("pool", vm[:, :, :, 64:128], 1.0, [b_v])
        hT = self.bf(self.alloc(2048), 4096).rearrange("p (k n) -> p k n", k=8)
        b_hT = self.tb()
        cols = self.f32(self.o_cols + 220, 8)
        b_c = self.b_cols
        self.DMA("sp", cols[:, 0:3], self.din["qnT"][i], (), [b_c])
        self.DMA("sp", cols[:, 3:5], self.din["kvnT"][i], (), [b_c])
        cqg = self.bf(self.alloc(768), 1536).rearrange("p (k n) -> p k n", k=3)
        sq = self.bf(self.alloc(768), 1536).rearrange("p (k n) -> p k n", k=3)
        ckg = self.bf(self.alloc(512), 1024).rearrange("p (k n) -> p k n", k=2)
        sk = self.bf(self.alloc(512), 1024).rearrange("p (k n) -> p k n", k=2)
        b_cq, b_ck = self.tb(), self.tb()
        rq = self.f32(self.alloc(512), 512)
        rk = self.f32(self.alloc(512), 512)
        rkc = self.f32(self.alloc(4), 4)
        b_r = self.tb()
        cosb = self.f32(self.alloc(512), 512)
        sinb = self.f32(self.alloc(512), 512)
        b_tab = self.tb()
        rc = self.f32(self.alloc(512), 512)
        rs = self.f32(self.alloc(512), 512)
        b_rc = self.tb()
        krT = self.bf(self.alloc(256), 512)
        b_kr = self.tb()
        tA = self.f32(self.alloc(512), 512)
        tB = self.f32(self.alloc(512), 512)
        b_tA, b_tB = self.tb(), self.tb()
        w5 = self.bf(self.alloc(2560), 5120).rearrange("p (c k m) -> p c k m", c=5, k=8)
        b_w5 = self.tb()
        self.DMA("sp", w5, self.dbf["wie"][i, 12:17].rearrange("c p k m -> p c k m"), [self.bbf["wie"][i]], [b_w5])
        wkr = self.bf(self.alloc(256), 512).rearrange("p (k m) -> p k m", k=8)
        b_wkr = self.tb()
        self.DMA("sp", wkr, self.dbf["wkr"][i], [self.bbf["wkr"][i]], [b_wkr])
        wq4 = self.bf(self.alloc(1152), 2304).rearrange("p (h k m) -> p h k m", h=4, k=3)
        b_wq4 = self.tb()
        self.DMA("sp", wq4, self.dbf["wuq"][i, 4 * hh:4 * hh + 4].rearrange("h p k m -> p h k m"), [self.bbf["wuq"][i]], [b_wq4])
        wkn4 = self.bf(self.alloc(256), 512).rearrange("p (h k m) -> p h k m", h=4, k=2)
        b_wkn4 = self.tb()
        self.DMA("sp", wkn4, self.dbf["wukvn"][i, 4 * hh:4 * hh + 4].rearrange("h p k m -> p h k m"), [self.bbf["wukvn"][i]], [b_wkn4])
        wvv = self.bf(self.alloc(256), 512).rearrange("p (k m) -> p k m", k=2)
        b_wvv = self.tb()
        self.DMA("sp", wvv, self.dbf["wukvv"][i][:, :, hh * 256:(hh + 1) * 256], [self.bbf["wukvv"][i]], [b_wvv])
        ones = self.bf(self.o_ones, 128)
        sel = self.bf(self.o_sel, 96)
        wn = [0, 0, 0]

        def load_tabs(blk):
            n, t0 = self.blk_n(blk), self.blk_t0(blk)
            self.DMA("sp", cosb[:, 0:n], self.din["cosM"][:, t0:t0 + n], (), [b_tab])
            self.DMA("sp", sinb[:, 0:n], self.din["sinM"][:, t0:t0 + n], (), [b_tab])

        def lowrank(blk, which):
            n = self.blk_n(blk)
            if which == "q":
                chunks, dst, dsq, col0, out_r, dim, b_d = (12, 13, 14), cqg, sq, 0, rq, 384.0, b_cq
            else:
                chunks, dst, dsq, col0, out_r, dim, b_d = (15, 16), ckg, sk, 3, rk, 256.0, b_ck
            for kk, c in enumerate(chunks):
                wt, bw = w5[:, c - 12], b_w5
                bank = 2 + kk % 2
                for k in range(8):
                    self.MM(self.ps[:, bank, 0:n], wt[:, k, :], hT[:, k, 0:n], k == 0, k == 7, [bw, b_hT], [self.pb[bank]])
                self.ACT(dst[:, kk, 0:n], self.ps[:, bank, 0:n], AF.Identity, [self.pb[bank], b_c], [b_d], scale=cols[:, col0 + kk:col0 + kk + 1])
                self.ACT(dsq[:, kk, 0:n], self.ps[:, bank, 0:n], AF.Square, [self.pb[bank]], [b_d])
            nk = len(chunks)
            bank = 2
            for kk in range(nk):
                self.MM(self.ps[:, bank, 0:n], ones[:, 0:128], dsq[:, kk, 0:n], kk == 0, kk == nk - 1, [b_d, self.b_const], [self.pb[bank]])
            self.ACT(out_r[:, 0:n], self.ps[:, bank, 0:n], AF.Sqrt, [self.pb[bank], self.b_eps], [b_r], bias=self.epscol, scale=1.0 / dim)
            self.RECIP(out_r[:, 0:n], out_r[:, 0:n], [b_r], [b_r])

        self.epscol = self.f32(self.alloc(1), 1)
        self.b_eps = self.tb()
        self.MEMSET("pool", self.epscol, EPS, [self.b_eps])
        for blk in range(5):
            n, t0 = self.blk_n(blk), self.blk_t0(blk)
            self.make_hT(blk, 0, hT, b_hT)
            load_tabs(blk)
            lowrank(blk, "k")
            for part, bank in ((0, 6), (1, 7)):
                for k in range(8):
                    self.MM(self.ps[0:32, bank, 0:n], wkr[:, k, part * 32:(part + 1) * 32], hT[:, k, 0:n], k == 0, k == 7, [b_wkr, b_hT], [self.pb[bank]])
            self.TT("dve", tA[0:32, 0:n], self.ps[0:32, 6, 0:n], cosb[0:32, 0:n], ALU.mult, [self.pb[6], b_tab], [b_tA])
            self.TT("dve", tB[0:32, 0:n], self.ps[0:32, 7, 0:n], sinb[0:32, 0:n], ALU.mult, [self.pb[7], b_tab], [b_tB])
            self.TT("pool", krT[0:32, 0:n], tA[0:32, 0:n], tB[0:32, 0:n], ALU.add, [b_tA, b_tB], [b_kr])
            for hq in range(4):
                h = hh * 4 + hq
                wt, bw = wkn4[:, hq], b_wkn4
                bank = 4 + hq % 2
                self.MM(self.ps[0:96, bank, 0:n], sel[0:32, 0:96], krT[0:32, 0:n], True, False, [b_kr, self.b_const], [self.pb[bank]])
                for k in range(2):
                    self.MM(self.ps[0:64, bank, 0:n], wt[:, k, :], ckg[:, k, 0:n], False, k == 1, [bw, b_ck], [self.pb[bank]])
                self.TT("dve", km[0:64, hq, t0:t0 + n], self.ps[0:64, bank, 0:n], rk[0:64, 0:n], ALU.mult, [self.pb[bank], b_r], [b_k])
                self.CP("act", km[64:96, hq, t0:t0 + n], self.ps[64:96, bank, 0:n], [self.pb[bank]], [b_k])
            for j in range(n // 128):
                bank = 6 + j % 2
                for k in range(2):
                    self.MM(self.ps[:, bank, 0:2], sk[:, k, j * 128:(j + 1) * 128], ones[:, 0:2], k == 0, k == 1, [b_ck, self.b_const], [self.pb[bank]])
                self.ACT(rkc[:, j:j + 1], self.ps[:, bank, 0:1], AF.Sqrt, [self.pb[bank], self.b_eps], [b_r], bias=self.epscol, scale=1.0 / 256.0)
                self.RECIP(rkc[:, j:j + 1], rkc[:, j:j + 1], [b_r], [b_r])
                bank2 = 4 + j % 2
                for k in range(2):
                    self.MM(self.ps[:, bank2, 0:256], ckg[:, k, j * 128:(j + 1) * 128], wvv[:, k, :], k == 0, k == 1, [b_ck, b_wvv], [self.pb[bank2]])
                tt = t0 // 128 + j
                pv4 = self.ps[:, bank2, 0:256].rearrange("p (g c) -> p g c", g=2)
                self.TS("dve", vm[:, tt, :, 0:64], pv4[:, :, 0:64], rkc[:, j:j + 1], None, ALU.mult, None, [self.pb[bank2], b_r], [b_v])
                self.TS("dve", vm[:, tt, :, 128:192], pv4[:, :, 64:128], rkc[:, j:j + 1], None, ALU.mult, None, [self.pb[bank2], b_r], [b_v])
        qb = self.bf(self.alloc(1024), 2048).rearrange("p (h n) -> p h n", h=4)
        b_q = self.tb()
        ob = [(self.bf(self.alloc(512), 1024).rearrange("p (c n) -> p c n", c=2), self.tb()) for _ in range(2)]
        self.attn_bufs(512, depth=5, masked=False)
        nb = 5 if with_ctx else 4
        for blk in range(nb):
            n, t0 = self.blk_n(blk), self.blk_t0(blk)
            self.make_hT(blk, 0, hT, b_hT)
            load_tabs(blk)
            lowrank(blk, "q")
            self.TT("dve", rc[64:96, 0:n], rq[64:96, 0:n], cosb[64:96, 0:n], ALU.mult, [b_r, b_tab], [b_rc])
            self.TT("dve", rs[64:96, 0:n], rq[64:96, 0:n], sinb[64:96, 0:n], ALU.mult, [b_r, b_tab], [b_rc])
            for hq in range(4):
                h = hh * 4 + hq
                wt, bw = wq4[:, hq], b_wq4
                ba, bb_ = 2, 3
                for k in range(3):
                    self.MM(self.ps[0:96, ba, 0:n], wt[:, k, 0:96], cqg[:, k, 0:n], k == 0, k == 2, [bw, b_cq], [self.pb[ba]])
                for k in range(3):
                    self.MM(self.ps[0:96, bb_, 0:n], wt[:, k, 96:192], cqg[:, k, 0:n], k == 0, k == 2, [bw, b_cq], [self.pb[bb_]])
                self.TT("dve", qb[0:64, hq, 0:n], self.ps[0:64, ba, 0:n], rq[0:64, 0:n], ALU.mult, [self.pb[ba], b_r], [b_q])
                self.TT("dve", tA[64:96, 0:n], self.ps[64:96, ba, 0:n], rc[64:96, 0:n], ALU.mult, [self.pb[ba], b_rc], [b_tA])
                self.TT("dve", tB[64:96, 0:n], self.ps[64:96, bb_, 0:n], rs[64:96, 0:n], ALU.mult, [self.pb[bb_], b_rc], [b_tB])
                self.TT("pool", qb[64:96, hq, 0:n], tA[64:96, 0:n], tB[64:96, 0:n], ALU.add, [b_tA, b_tB], [b_q])
            oT, b_o = ob[blk % 2]
            kts = list(range(18)) if blk < 4 else [16, 17]
            for hq in range(4):
                g, par = hq // 2, hq % 2
                base = par * 64
                qk = [(km[0:96, hq, kt * 128:(kt + 1) * 128], qb[0:96, hq, 0:n]) for kt in kts]
                pv = [vm[:, kt, g, base:base + 128] for kt in kts]
                self.attend(qk, n, sc, pv, base, oT[base:base + 64, g, 0:n], b_o, [b_k, b_q, b_v])
            self.pipe_flush()
            self.DMA("sp", self.oT[:, 4 + 2 * hh:6 + 2 * hh, t0:t0 + n], oT[:, :, 0:n], [b_o], [self.b_oT[blk]])
        self.phase()

    def layer(self, b, l):
        with_ctx = l < 3
        self.op_n = 0
        self.mods(b, l)
        if l % 2 == 0:
            self.na_pass(l, with_ctx)
            self.mla_pass(l, 0, with_ctx)
            self.mla_pass(l, 1, with_ctx)
        else:
            self.swa_pass(l, with_ctx)
        self.outproj(l, with_ctx)
        self.ffn(l, with_ctx)


_CACHE = {}


def _prepare(inputs):
    w = _host_weights(inputs)
    w.update(_host_consts())
    return w


def kernel(**inputs):
    NL = int(inputs.pop("_NL", 4))
    ncores = int(inputs.pop("_NCORES", 8))
    w = _prepare(inputs)
    x = np.ascontiguousarray(np.asarray(inputs["x"], np.float32))
    ctx = np.ascontiguousarray(np.asarray(inputs["ctx"], np.float32))
    c = np.asarray(inputs["c"], np.float32)
    c_ctx = np.asarray(inputs["c_ctx"], np.float32)
    in_maps = []
    for core in range(ncores):
        m = dict(w)
        m["x"] = x[2 * core:2 * core + 2]
        m["ctx"] = ctx[2 * core:2 * core + 2]
        cc = np.stack([c[2 * core], c[2 * core + 1], c_ctx], axis=1)
        m["cT"] = np.ascontiguousarray(cc.reshape(8, 128, 3).transpose(1, 0, 2))
        in_maps.append(m)
    shapes = {k: v.shape for k, v in in_maps[0].items()}
    key = (NL, tuple(sorted((k, tuple(s)) for k, s in shapes.items())))
    nc = Builder(shapes, NL=NL, NB=2).build()
    res = run_bass_kernel_spmd(nc, in_maps, core_ids=list(range(ncores)))
    if getattr(res, "exec_time_ns", None) is not None:
        print("EXEC_TIME_NS", res.exec_time_ns)
    out = np.concatenate([np.asarray(r["out"]) for r in res.results], axis=0)
    return out.astype(np.float32)
```

```python
import numpy as np
from contextlib import ExitStack
import concourse.bass as bass
import concourse.mybir as mybir
from concourse.bass_utils import run_bass_kernel_spmd

F32 = mybir.dt.float32
BF16 = mybir.dt.bfloat16
AF = mybir.ActivationFunctionType
ALU = mybir.AluOpType

ENGS = ("pe", "act", "dve", "pool", "sp")
NRING = 32

D = 1024
S = 2048
CT = 256
T = S + CT
DFF = 2816
NCH = 22
ALPHA = 8.0 ** 0.25
EPS = 1e-6


class Buf:
    __slots__ = ("w", "r", "name", "keep", "tmp")

    def __init__(self, name=""):
        self.w = None
        self.r = []
        self.name = name
        self.keep = False
        self.tmp = False


class Prog:
    def __init__(self, nc):
        self.nc = nc
        self.ops = {e: [] for e in ENGS}
        self.known = {e: {} for e in ENGS}
        self.known_dma = {e: set() for e in ENGS}
        self.dma_cnt = {e: 0 for e in ENGS}
        self.all_bufs = []

    def buf(self, name=""):
        b = Buf(name)
        self.all_bufs.append(b)
        return b

    def bufs(self, n, name=""):
        return [self.buf(f"{name}{i}") for i in range(n)]

    def _collect(self, eng, reads, writes):
        deps = {}
        ddeps = set()

        def add(d, raw):
            if d is None:
                return
            if d[0] == "dma":
                ddeps.add(d)
                return
            e2, i2 = d
            if e2 == eng:
                if eng == "pe" or not raw:
                    return
            if deps.get(e2, -1) < i2:
                deps[e2] = i2

        for b in reads:
            add(b.w, True)
        for b in writes:
            add(b.w, False)
            for r in b.r:
                add(r, False)
        waits = []
        kn = self.known[eng]
        for e2, i2 in deps.items():
            if kn.get(e2, -1) >= i2:
                continue
            kn[e2] = i2
            waits.append(("eng", e2, i2))
        kd = self.known_dma[eng]
        for d in ddeps:
            if d in kd:
                continue
            kd.add(d)
            waits.append(("dma", d[1], d[2]))
        return waits

    def op(self, eng, fn, reads=(), writes=()):
        waits = self._collect(eng, reads, writes)
        idx = len(self.ops[eng])
        self.ops[eng].append([fn, waits, False, None])
        me = (eng, idx)
        for b in reads:
            if len(b.r) > 64:
                b.r = b.r[-32:]
            b.r.append(me)
        for b in writes:
            b.w = me
            b.r = []
        return me

    def dma(self, queue, fn, reads=(), writes=()):
        waits = self._collect(queue, reads, writes)
        n = self.dma_cnt[queue]
        self.dma_cnt[queue] = n + 1
        if n >= NRING:
            prev = ("dma", queue, n - NRING)
            if prev not in self.known_dma[queue]:
                self.known_dma[queue].add(prev)
                waits.append(("dma", queue, n - NRING))
        self.ops[queue].append([fn, waits, False, n])
        me = ("dma", queue, n)
        for b in reads:
            b.r.append(me)
        for b in writes:
            b.w = me
            b.r = []
        return me

    def barrier(self):
        lasts = []
        for e in ENGS:
            for i in range(len(self.ops[e]) - 1, -1, -1):
                o = self.ops[e][i]
                if o[0] is not None and o[3] is None:
                    lasts.append((e, i))
                    break
        pend = set()
        for b in self.all_bufs:
            if b.keep:
                continue
            if b.w is not None and b.w[0] == "dma":
                pend.add(b.w)
            for r in b.r:
                if r[0] == "dma":
                    pend.add(r)
        for e in ENGS:
            waits = []
            kn = self.known[e]
            for e2, i2 in lasts:
                if e2 == e:
                    continue
                if kn.get(e2, -1) >= i2:
                    continue
                kn[e2] = i2
                waits.append(("eng", e2, i2))
            kd = self.known_dma[e]
            for d in pend:
                if d in kd:
                    continue
                kd.add(d)
                waits.append(("dma", d[1], d[2]))
            self.ops[e].append([None, waits, False, None])
        for b in self.all_bufs:
            if b.keep:
                b.r = []
                continue
            b.w = None
            b.r = []
        self.all_bufs = [b for b in self.all_bufs if not b.tmp]

    def emit(self):
        nc = self.nc
        ops = self.ops
        for e in ENGS:
            for o in ops[e]:
                for w in o[1]:
                    if w[0] == "eng":
                        ops[w[1]][w[2]][2] = True
        cnt = {}
        for e in ENGS:
            c = 0
            arr = []
            for o in ops[e]:
                if o[2]:
                    c += 1
                arr.append(c)
            cnt[e] = arr
        with ExitStack() as st:
            esem = {e: st.enter_context(nc.semaphore(f"s_{e}")) for e in ENGS}
            rings = {}
            for q in ENGS:
                if self.dma_cnt[q] > 0:
                    rings[q] = [st.enter_context(nc.semaphore(f"r_{q}{i}")) for i in range(min(NRING, self.dma_cnt[q]))]
            block = st.enter_context(nc.Block())

            def run(ename, eng):
                for fn, waits, sig, dman in ops[ename]:
                    for w in waits:
                        if w[0] == "eng":
                            eng.wait_ge(esem[w[1]], cnt[w[1]][w[2]])
                        else:
                            q, n = w[1], w[2]
                            eng.wait_ge(rings[q][n % NRING], 16 * (n // NRING + 1))
                    if fn is None:
                        continue
                    ins = fn(eng)
                    if dman is not None:
                        ins.then_inc(rings[ename][dman % NRING], 16)
                    elif sig:
                        ins.then_inc(esem[ename], 1)

            @block.tensor
            def _(eng):
                run("pe", eng)

            @block.scalar
            def _(eng):
                run("act", eng)

            @block.vector
            def _(eng):
                run("dve", eng)

            @block.gpsimd
            def _(eng):
                run("pool", eng)

            @block.sync
            def _(eng):
                run("sp", eng)


def _ktile(w, m):
    K, N = w.shape
    return np.ascontiguousarray(w.reshape(K // 128, 128, N // m, m).transpose(2, 1, 0, 3))


def _rtile(w):
    K, N = w.shape
    return np.ascontiguousarray(w.reshape(K // 128, 128, N).transpose(1, 0, 2))


def _colT(v):
    return np.ascontiguousarray(v.reshape(-1, 128).T)


def _rotperm(nheads, hd):
    q = hd // 4
    p = []
    for h in range(nheads):
        for d in range(hd):
            blk = (d // q)
            src = d + q if blk % 2 == 0 else d - q
            p.append(h * hd + src)
    return np.array(p)


def _rope_tables(rot_dim):
    axis_dim = rot_dim // 2
    t = np.arange(S)
    row = (t // 64).astype(np.float32)[:, None]
    col = (t % 64).astype(np.float32)[:, None]
    inv_freq = (10000.0 ** (-np.arange(0, axis_dim, 2, dtype=np.float32) / axis_dim)).astype(np.float32)
    ar, ac = row * inv_freq, col * inv_freq
    ang = np.concatenate([ar, ar, ac, ac], axis=-1)
    cos = np.cos(ang).astype(np.float32)
    sin = np.sin(ang).astype(np.float32)
    q = rot_dim // 4
    sign = np.ones(rot_dim, np.float32)
    for d in range(rot_dim):
        if (d // q) % 2 == 0:
            sign[d] = -1.0
    sin = sin * sign
    cosT = np.concatenate([cos.T, np.ones((rot_dim, CT), np.float32)], axis=1)
    sinT = np.concatenate([sin.T, np.zeros((rot_dim, CT), np.float32)], axis=1)
    return cosT, sinT


NA_SLOTS = [(d, True, True) for d in range(14)] + [(2, False, True), (4, True, True), (6, True, True), (8, True, True), (10, True, False)]


def _host_consts():
    c = {}
    c["identf"] = np.eye(128, dtype=np.float32)
    sel = np.zeros((32, 96), np.float32)
    for j in range(32):
        sel[j, 64 + j] = 1.0
    c["sel"] = sel
    qc = np.arange(64)
    ws = np.clip(qc - 8, 0, 48)
    kc = np.arange(64)[:, None]
    colok = ((kc >= ws[None, :]) & (kc < ws[None, :] + 16)).astype(np.float32)
    m = np.zeros((128, len(NA_SLOTS), 64), np.float32)
    for s, (d, vlo, vhi) in enumerate(NA_SLOTS):
        if vlo:
            m[0:64, s, :] = colok
        if vhi:
            m[64:128, s, :] = colok
    c["namask"] = m
    j = np.arange(128)[:, None]
    i = np.arange(128)[None, :]
    c["swaml"] = (j >= i).astype(np.float32)
    c["swamr"] = (j <= i).astype(np.float32)
    cs, ss = _rope_tables(64)
    c["cosS"] = np.concatenate([cs, cs], axis=0)
    c["sinS"] = np.concatenate([ss, ss], axis=0)
    cm, sm = _rope_tables(32)
    c["cosM"] = np.concatenate([cm, cm, cm, cm], axis=0)
    c["sinM"] = np.concatenate([sm, sm, sm, sm], axis=0)
    return c


def _host_weights(inp):
    w = {}
    f = lambda a: np.ascontiguousarray(np.asarray(a, dtype=np.float32))
    w_ada = f(inp["w_ada"])
    w["wada"] = np.stack([_ktile(w_ada[l], 128) for l in range(4)])
    b_ada = f(inp["b_ada"])
    w["badaT"] = np.stack([_colT(b_ada[l]) for l in range(4)])
    w["bada"] = b_ada
    wie = f(inp["w_in_even"])
    w["wie"] = np.stack([_ktile(wie[i][:, 0:2176], 128) for i in range(2)])
    pk = _rotperm(1, 32)
    w["wkr"] = np.stack([_rtile(np.concatenate([wie[i][:, 2176:2208], wie[i][:, 2176:2208][:, pk]], axis=1)) for i in range(2)])
    w["wva"] = np.stack([_rtile(wie[i][:, 1024:1536]) for i in range(2)])
    wuq = f(inp["w_uq"])
    tiles = []
    for i in range(2):
        hs = []
        for h in range(8):
            blk = wuq[i][:, h * 96:(h + 1) * 96]
            rot = blk.copy()
            rot[:, 64:96] = blk[:, 64:96][:, pk]
            hs.append(_rtile(np.concatenate([blk, rot], axis=1)))
        tiles.append(np.stack(hs))
    w["wuq"] = np.stack(tiles)
    wukv = f(inp["w_ukv"])
    w["wukvn"] = np.stack([np.stack([_rtile(wukv[i][:, h * 128:h * 128 + 64]) for h in range(8)]) for i in range(2)])
    w["wukvv"] = np.stack([_rtile(np.concatenate([wukv[i][:, h * 128 + 64:h * 128 + 128] for h in range(8)], axis=1)) for i in range(2)])
    w["qnT"] = np.stack([_colT(f(inp["mla_q_norm"])[i]) for i in range(2)])
    w["kvnT"] = np.stack([_colT(f(inp["mla_kv_norm"])[i]) for i in range(2)])
    w["woe"] = np.stack([_rtile(f(inp["w_out_even"])[i]) for i in range(2)])
    wio = f(inp["w_in_odd"])
    pq = _rotperm(16, 64)
    p1 = _rotperm(1, 64)
    tq = []
    tk = []
    for i in range(2):
        q = wio[i][:, 0:1024]
        qr = q[:, pq]
        tq.append(np.stack([_rtile(np.concatenate([q[:, c * 128:(c + 1) * 128], qr[:, c * 128:(c + 1) * 128]], axis=1)) for c in range(8)]))
        ks = []
        for g in range(2):
            k = wio[i][:, 1024 + g * 64:1024 + (g + 1) * 64]
            kr = k[:, p1]
            ks.append(_rtile(np.concatenate([k, k, kr, kr], axis=1)))
        tk.append(np.stack(ks))
    w["wioq"] = np.stack(tq)
    w["wiok"] = np.stack(tk)
    w["wiov"] = np.stack([_rtile(wio[i][:, 1152:1280]) for i in range(2)])
    w["sinks"] = f(inp["sinks"])
    w["woo"] = np.stack([_rtile(f(inp["w_out_odd"])[i]) for i in range(2)])
    wup = f(inp["w_up"])
    w["wup"] = np.stack([np.stack([_rtile(np.concatenate([wup[l][:, c * 128:(c + 1) * 128], wup[l][:, DFF + c * 128:DFF + (c + 1) * 128]], axis=1)) for c in range(NCH)]) for l in range(4)])
    w["bupT"] = np.stack([_colT(f(inp["b_up"])[l]) for l in range(4)])
    cw = f(inp["conv_w"])
    w["cwT"] = np.stack([np.stack([_colT(cw[l, i]) for i in range(3)], axis=1) for l in range(4)])
    w["cbT"] = np.stack([_colT(f(inp["conv_b"])[l]) for l in range(4)])
    w["wdn"] = np.stack([_rtile(f(inp["w_down"])[l]) for l in range(4)])
    w["bdn"] = f(inp["b_down"])
    rpb = f(inp["na_rpb"])
    kc = np.arange(64)[:, None]
    qc = np.arange(64)[None, :]
    idx = np.clip(kc - qc + 15, 0, 30)
    w["rpb"] = np.ascontiguousarray(rpb[:, :, :, idx])
    return w


CAST = ["wada", "wie", "wkr", "wva", "wuq", "wukvn", "wukvv", "woe", "wioq", "wiok", "wiov", "woo", "wup", "wdn"]


class Builder:
    def __init__(self, shapes, NL=4, NB=2):
        self.NL, self.NB = NL, NB
        nc = self.nc = bass.Bass("TRN2", target_bir_lowering=False)
        self.P = Prog(nc)
        self.din = {}
        for k, shp in shapes.items():
            self.din[k] = nc.dram_tensor(k, list(shp), F32, kind="ExternalInput").ap()
        self.dbf = {}
        self.bbf = {}
        for k in CAST:
            shp = shapes[k]
            self.dbf[k] = nc.dram_tensor(k + "_bf", list(shp), BF16, kind="Internal").ap()
            self.bbf[k] = [self.P.buf(f"{k}{i}") for i in range(shp[0])]
            for b_ in self.bbf[k]:
                b_.keep = True
        self.oT = nc.dram_tensor("oT_scr", [128, 8, T], BF16, kind="Internal").ap()
        self.b_oT = [self.P.buf(f"oT{i}") for i in range(5)]
        self.gsc = [nc.dram_tensor(f"gsc{l}", [128, 2, 2, 1024], F32, kind="Internal").ap() for l in range(4)]
        self.b_gsc = [self.P.buf(f"gsc{l}") for l in range(4)]
        for b_ in self.b_gsc:
            b_.keep = True
        self.out = nc.dram_tensor("out", [NB, S, D], F32, kind="ExternalOutput").ap()
        self.b_out = self.P.buf("out")

    def MM(self, out, lhsT, rhs, start, stop, rd, wr):
        self.P.op("pe", lambda e: e.matmul(out, lhsT=lhsT, rhs=rhs, start=start, stop=stop), rd, wr)

    def TR(self, out, in_, ident, rd, wr):
        self.P.op("pe", lambda e: e.transpose(out=out, in_=in_, identity=ident), rd, wr)

    def ACT(self, out, in_, func, rd, wr, bias=None, scale=None):
        kw = {}
        if bias is not None:
            kw["bias"] = bias
        if scale is not None:
            kw["scale"] = scale
        self.P.op("act", lambda e: e.activation(out=out, in_=in_, func=func, **kw), rd, wr)

    def TT(self, eng, out, in0, in1, op, rd, wr):
        self.P.op(eng, lambda e: e.tensor_tensor(out=out, in0=in0, in1=in1, op=op), rd, wr)

    def TS(self, eng, out, in0, s1, s2, op0, op1, rd, wr):
        if s2 is None:
            self.P.op(eng, lambda e: e.tensor_scalar(out=out, in0=in0, scalar1=s1, scalar2=None, op0=op0), rd, wr)
        else:
            self.P.op(eng, lambda e: e.tensor_scalar(out=out, in0=in0, scalar1=s1, scalar2=s2, op0=op0, op1=op1), rd, wr)

    def STT(self, out, in0, scalar, in1, op0, op1, rd, wr):
        self.P.op("dve", lambda e: e.scalar_tensor_tensor(out=out, in0=in0, scalar=scalar, in1=in1, op0=op0, op1=op1), rd, wr)

    def CP(self, eng, out, in_, rd, wr):
        if eng == "act":
            self.P.op("act", lambda e: e.copy(out=out, in_=in_), rd, wr)
        else:
            self.P.op(eng, lambda e: e.tensor_copy(out=out, in_=in_), rd, wr)

    def RECIP(self, out, in_, rd, wr):
        self.P.op("dve", lambda e: e.reciprocal(out=out, in_=in_), rd, wr)

    def MEMSET(self, eng, ap, val, wr):
        self.P.op(eng, lambda e: e.memset(ap, val), (), wr)

    def DMA(self, q, out, in_, rd, wr):
        self.P.dma(q, lambda e: e.dma_start(out=out, in_=in_), rd, wr)

    def alloc(self, words):
        off = self.top
        self.top += words
        assert self.top <= self.AW, (self.top, self.AW)
        return off

    def f32(self, off, n):
        return self.arena[:, off:off + n]

    def bf(self, off, n):
        return self.arena[:, off:off + (n + 1) // 2].bitcast(BF16)[:, 0:n]

    def tb(self, name=""):
        b = self.P.buf(name)
        b.tmp = True
        return b

    def phase(self):
        self.P.barrier()
        self.top = self.persist_top

    def build(self):
        nc, P = self.nc, self.P
        with ExitStack() as st:
            self.AW = 53000
            self.arena = st.enter_context(nc.sbuf_tensor("arena", [128, self.AW], F32))
            self.ps = st.enter_context(nc.psum_tensor("ps", [128, 8, 512], F32))
            self.pb = P.bufs(8, "psb")
            self.top = 0
            self.o_X = self.alloc(16 * 1024)
            self.o_Z = self.alloc(2 * 1024)
            self.b_X = P.bufs(16, "x")
            self.b_Z = P.bufs(2, "z")
            self.o_idf = self.alloc(128)
            self.o_idb = self.alloc(64)
            self.o_ones = self.alloc(64)
            self.o_sel = self.alloc(48)
            self.o_cs = self.alloc(12 + 4)
            self.o_fm = self.alloc(4 * 96)
            self.o_gbc = self.alloc(4096)
            self.o_cols = self.alloc(44 * 5 + 8 + 40)
            self.b_const = P.buf("const")
            self.b_cs = P.buf("cs")
            self.b_fm = P.buf("fm")
            self.b_gbc = P.buf("gbc")
            self.b_cols = P.buf("cols")
            self.persist_top = self.top
            self.prologue()
            for b in range(self.NB):
                self.load_x(b)
                for l in range(self.NL):
                    self.layer(b, l)
                self.store_x(b)
            P.barrier()
            P.emit()
        return nc

    def X(self, t):
        return self.f32(self.o_X + t * 1024, 1024)

    def Z(self, t):
        return self.f32(self.o_Z + t * 1024, 1024)

    def prologue(self):
        P = self.P
        order = []
        for l in range(4):
            order.append(("wada", l))
            i = l // 2
            if l % 2 == 0:
                for k in ("wie", "wkr", "wva", "wuq", "wukvn", "wukvv", "woe"):
                    if l < 2 or True:
                        order.append((k, i))
            else:
                for k in ("wioq", "wiok", "wiov", "woo"):
                    order.append((k, i))
            order.append(("wup", l))
            order.append(("wdn", l))
        for k, i in order:
            if i >= self.din[k].shape[0]:
                continue
            src = self.din[k][i]
            dst = self.dbf[k][i]
            n = 1
            for s_ in src.shape:
                n *= s_
            letters = "abcdefg"[: len(src.shape)]
            pat = " ".join(letters)
            srcf = src.rearrange(f"{pat} -> ({pat})").rearrange("(r j) -> r j", j=2048)
            dstf = dst.rearrange(f"{pat} -> ({pat})").rearrange("(r j) -> r j", j=2048)
            R = n // 2048
            step = 512
            for r0 in range(0, R, step):
                r1 = min(R, r0 + step)
                self.DMA("pool", dstf[r0:r1, :], srcf[r0:r1, :], (), [self.bbf[k][i]])
        idf = self.f32(self.o_idf, 128)
        self.DMA("sp", idf, self.din["identf"], (), [self.b_const])
        self.CP("dve", self.bf(self.o_idb, 128), idf, [self.b_const], [self.b_const])
        self.MEMSET("pool", self.bf(self.o_ones, 128), 1.0, [self.b_const])
        tmp = self.f32(self.alloc(96), 96)
        self.DMA("sp", tmp[0:32, :], self.din["sel"], (), [self.b_const])
        self.CP("dve", self.bf(self.o_sel, 96)[0:32, :], tmp[0:32, :], [self.b_const], [self.b_const])
        ctmp = self.f32(self.alloc(24), 24)
        self.DMA("sp", ctmp, self.din["cT"].rearrange("p k s -> p (k s)"), (), [self.b_cs])
        self.ACT(self.bf(self.o_cs, 24), ctmp, AF.Silu, [self.b_cs], [self.b_cs])
        self.phase()

    def load_x(self, b):
        for t in range(16):
            self.DMA("sp", self.X(t), self.din["x"][b, t * 128:(t + 1) * 128, :], (), [self.b_X[t]])
        for t in range(2):
            self.DMA("sp", self.Z(t), self.din["ctx"][b, t * 128:(t + 1) * 128, :], (), [self.b_Z[t]])

    def store_x(self, b):
        for t in range(16):
            self.DMA("sp", self.out[b, t * 128:(t + 1) * 128, :], self.X(t), [self.b_X[t]], [self.b_out])

    def blk_tiles(self, blk):
        if blk < 4:
            return [(self.X(4 * blk + j), self.b_X[4 * blk + j]) for j in range(4)]
        return [(self.Z(j), self.b_Z[j]) for j in range(2)]

    def blk_t0(self, blk):
        return blk * 512

    def blk_n(self, blk):
        return 512 if blk < 4 else 256

    def fmc(self, kind, k, s):
        fm = self.f32(self.o_fm + self.cur_l * 96, 96).rearrange("p (a k s) -> p a k s", a=4, k=8)
        col = self.cur_b if s == 0 else 2
        return fm[:, kind, k, col:col + 1]

    def mods(self, b, l):
        self.cur_b, self.cur_l = b, l
        gbc = self.f32(self.o_gbc, 4096).rearrange("p (g s n) -> p g s n", g=2, s=2)
        if b == 1:
            self.DMA("sp", gbc, self.gsc[l], [self.b_gsc[l]], [self.b_gbc])
            self.phase()
            return
        fm = self.f32(self.o_fm + l * 96, 96).rearrange("p (a k s) -> p a k s", a=4, k=8)
        cs = self.bf(self.o_cs, 24).rearrange("p (k s) -> p k s", s=3)
        rep = self.bf(self.alloc(1536), 3072).rearrange("p (k s m) -> p k s m", k=8, s=3)
        b_rep = self.tb()
        for s in range(3):
            self.CP("dve", rep[:, :, s, :], cs[:, :, s:s + 1].to_broadcast([128, 8, 128]), [self.b_cs], [b_rep])
        bT = self.f32(self.alloc(48), 48)
        b_bT = self.tb()
        self.DMA("sp", bT, self.din["badaT"][l], (), [b_bT])
        for c0 in (8, 32):
            self.TS("dve", bT[:, c0:c0 + 8], bT[:, c0:c0 + 8], 1.0, None, ALU.add, None, [b_bT], [b_bT])
        bb = self.f32(self.alloc(2048), 2048)
        b_bb = self.tb()
        for g, c0 in ((0, 2048), (1, 5120)):
            self.DMA("sp", bb[:, g * 1024:(g + 1) * 1024], self.din["bada"][l:l + 1, c0:c0 + 1024].to_broadcast([128, 1024]), (), [b_bb])
        g1 = self.f32(self.alloc(2048), 2048).rearrange("p (g n) -> p g n", g=2)
        b_g1 = self.tb()
        wts = [(self.bf(self.alloc(4096), 8192).rearrange("p (c k m) -> p c k m", c=8, k=8), self.tb()) for _ in range(4)]
        kinds = {0: 0, 1: 1, 3: 2, 4: 3}
        n = 0
        for c in range(48):
            sec, cc = c // 8, c % 8
            wt8, bw = wts[sec % 4]
            if cc == 0:
                self.DMA("sp", wt8, self.dbf["wada"][l, sec * 8:(sec + 1) * 8].rearrange("c p k m -> p c k m"), [self.bbf["wada"][l]], [bw])
            wt = wt8[:, cc, :, :]
            bank = n % 2
            n += 1
            if sec in kinds:
                pso = self.ps[:, bank, 0:3]
                for k in range(8):
                    self.MM(pso, wt[:, k, :], cs[:, k, :], k == 0, k == 7, [bw, self.b_cs], [self.pb[bank]])
                self.ACT(fm[:, kinds[sec], cc, :], pso, AF.Identity, [self.pb[bank], b_bT], [self.b_fm], bias=bT[:, c:c + 1])
            else:
                g = 0 if sec == 2 else 1
                for s in range(3):
                    pso = self.ps[:, bank, s * 128:(s + 1) * 128]
                    for k in range(8):
                        self.MM(pso, rep[:, k, s, :], wt[:, k, :], k == 0, k == 7, [bw, b_rep], [self.pb[bank]])
                    if s == 1:
                        dst, b_dst = g1[:, g, cc * 128:(cc + 1) * 128], b_g1
                    else:
                        dst, b_dst = gbc[:, g, s // 2, cc * 128:(cc + 1) * 128], self.b_gbc
                    self.TT("dve", dst, pso, bb[:, g * 1024 + cc * 128:g * 1024 + (cc + 1) * 128], ALU.add, [self.pb[bank], b_bb], [b_dst])
        self.DMA("sp", self.gsc[l][:, :, 0, :], g1, [b_g1], [self.b_gsc[l]])
        self.DMA("sp", self.gsc[l][:, :, 1, :], gbc[:, :, 1, :], [self.b_gbc], [self.b_gsc[l]])
        self.phase()

    def make_hT(self, blk, kind, hT, b_hT, col0=0, banks=(0, 1)):
        s = 0 if blk < 4 else 1
        tiles = self.blk_tiles(blk)
        idf = self.f32(self.o_idf, 128)
        n = len(tiles) * 128
        for k in range(8):
            bank = banks[k % 2]
            for j, (xt, bx) in enumerate(tiles):
                self.TR(self.ps[:, bank, j * 128:(j + 1) * 128], xt[:, k * 128:(k + 1) * 128], idf, [bx, self.b_const], [self.pb[bank]])
            self.ACT(hT[:, k, col0:col0 + n], self.ps[:, bank, 0:n], AF.Identity, [self.pb[bank], self.b_fm], [b_hT],
                     bias=self.fmc(2 * kind, k, s), scale=self.fmc(2 * kind + 1, k, s))

    def attend(self, qk_list, nq, scale, pv, out_rows, out_ap, b_out, rd, mask_fn=None, sink=None):
        per = max(1, 512 // nq)
        nk = len(qk_list)
        ob = self.out_banks[self.att_n % len(self.out_banks)]
        self.att_n += 1
        o_ps = self.ps[:, ob, 0:nq]
        groups = [(g0, min(nk, g0 + per)) for g0 in range(0, nk, per)]
        for gi, (g0, g1) in enumerate(groups):
            sb = self.qk_banks[self.qk_n % len(self.qk_banks)]
            self.qk_n += 1
            slot = self.pt_n % len(self.pt_bufs)
            self.pt_n += 1
            pt, ptf, b_pt = self.pt_bufs[slot]

            def A(g0=g0, g1=g1, sb=sb, pt=pt, ptf=ptf, b_pt=b_pt):
                for j in range(g0, g1):
                    kT, qT = qk_list[j]
                    self.MM(self.ps[0:128, sb, (j - g0) * nq:(j - g0 + 1) * nq], kT, qT, True, True, rd, [self.pb[sb]])
                w = (g1 - g0) * nq
                masked = mask_fn is not None and any(mask_fn(j) is not None for j in range(g0, g1))
                if not masked:
                    self.ACT(pt[:, 0:w], self.ps[:, sb, 0:w], AF.Exp, [self.pb[sb]], [b_pt], scale=scale)
                    return
                j = g0
                any_f32 = False
                while j < g1:
                    m = mask_fn(j)
                    c0 = (j - g0) * nq
                    if m is None:
                        j2 = j
                        while j2 < g1 and mask_fn(j2) is None:
                            j2 += 1
                        c1 = (j2 - g0) * nq
                        if nq <= 64:
                            if not any_f32:
                                self.ACT(ptf[:, 0:w], self.ps[:, sb, 0:w], AF.Exp, [self.pb[sb]], [b_pt], scale=scale)
                                any_f32 = True
                            self.CP("pool", pt[:, c0:c1], ptf[:, c0:c1], [b_pt], [b_pt])
                        else:
                            self.ACT(pt[:, c0:c1], self.ps[:, sb, c0:c1], AF.Exp, [self.pb[sb]], [b_pt], scale=scale)
                        j = j2
                    else:
                        m_ap, m_rd, span = m
                        c1 = c0 + span * nq
                        if nq <= 64:
                            if not any_f32:
                                self.ACT(ptf[:, 0:w], self.ps[:, sb, 0:w], AF.Exp, [self.pb[sb]], [b_pt], scale=scale)
                                any_f32 = True
                        else:
                            self.ACT(ptf[:, c0:c1], self.ps[:, sb, c0:c1], AF.Exp, [self.pb[sb]], [b_pt], scale=scale)
                        o3 = pt[:, c0:c1]
                        i3 = ptf[:, c0:c1]
                        if len(m_ap.shape) == 3:
                            o3 = o3.rearrange("p (a b) -> p a b", a=m_ap.shape[1])
                            i3 = i3.rearrange("p (a b) -> p a b", a=m_ap.shape[1])
                        self.TT("pool" if (span == 1 and nq <= 128) else "dve", o3, i3, m_ap, ALU.mult, [b_pt] + m_rd, [b_pt])
                        j += span

            def B(g0=g0, g1=g1, gi=gi, pt=pt, b_pt=b_pt):
                for j in range(g0, g1):
                    last = (j == nk - 1) and sink is None
                    self.MM(o_ps, pv[j], pt[:, (j - g0) * nq:(j - g0 + 1) * nq], gi == 0 and j == g0, last, rd + [b_pt], [self.pb[ob]])
                if gi != len(groups) - 1:
                    return
                if sink is not None:
                    s_l, s_r, s_rd = sink
                    self.MM(o_ps, s_l, s_r, False, True, s_rd, [self.pb[ob]])
                slot2 = self.rc_n % len(self.rc_bufs)
                self.rc_n += 1
                rec, b_rec = self.rc_bufs[slot2]
                sr = 64 - out_rows
                if nq <= 64:
                    self.RECIP(rec[sr:sr + 64, 0:nq], self.ps[sr:sr + 64, ob, 0:nq], [self.pb[ob]], [b_rec])
                else:
                    self.ACT(rec[sr:sr + 64, 0:nq], self.ps[sr:sr + 64, ob, 0:nq], AF.Ln, [self.pb[ob]], [b_rec])
                    self.ACT(rec[sr:sr + 64, 0:nq], rec[sr:sr + 64, 0:nq], AF.Exp, [b_rec], [b_rec], scale=-1.0)
                num = self.ps[out_rows:out_rows + 64, ob, 0:nq]
                den = rec[sr:sr + 64, 0:nq]
                if len(out_ap.shape) == 3:
                    num = num.rearrange("p (a b) -> p a b", a=out_ap.shape[1])
                    den = den.rearrange("p (a b) -> p a b", a=out_ap.shape[1])
                self.TT("dve", out_ap, num, den, ALU.mult, [self.pb[ob], b_rec], [b_out])

            A()
            self.pending.append(B)
            if len(self.pending) > self.DEPTH:
                self.pending.pop(0)()

    def pipe_flush(self):
        while self.pending:
            self.pending.pop(0)()

    def attn_bufs(self, ptw=512, depth=2, masked=True, rcw=512):
        self.att_n = self.qk_n = self.pt_n = self.rc_n = 0
        self.DEPTH = depth
        self.pending = []
        self.qk_banks = [4, 5, 2, 3, 0, 1][: depth + 1]
        self.out_banks = [6, 7]
        self.pt_bufs = []
        for i in range(depth + 2):
            o1 = self.alloc(ptw // 2)
            o2 = self.alloc(ptw) if masked else o1
            self.pt_bufs.append((self.bf(o1, ptw), self.f32(o2, ptw) if masked else None, self.tb()))
        self.rc_bufs = [(self.f32(self.alloc(rcw), rcw), self.tb()) for _ in range(3)]

    def wring(self, n, words, shape_fn):
        return [(shape_fn(self.bf(self.alloc(words), words * 2)), self.tb()) for _ in range(n)]

    def resid_ln(self, tiles, s, gate, ps_groups):
        gbc = self.f32(self.o_gbc, 4096).rearrange("p (g s n) -> p g s n", g=2, s=2)
        for j, (xt, bx) in enumerate(tiles):
            t, b_t = self.ln_t[self.ln_n % 2]
            st, b_st = self.ln_s[self.ln_n % 2]
            self.ln_n += 1
            for hf in range(2):
                bank = ps_groups[j][hf]
                self.TT("dve", t[:, hf * 512:(hf + 1) * 512], self.ps[:, bank, :], gbc[:, gate, s, hf * 512:(hf + 1) * 512], ALU.mult, [self.pb[bank], self.b_gbc], [b_t])
            self.STT(t, xt, ALPHA, t, ALU.mult, ALU.add, [bx, b_t], [b_t])
            for hf in range(2):
                self.P.op("dve", (lambda e, o=st[:, hf * 6:(hf + 1) * 6], i=t[:, hf * 512:(hf + 1) * 512]: e.bn_stats(out=o, in_=i)), [b_t], [b_st])
            self.P.op("dve", (lambda e, o=st[:, 12:14], i=st[:, 0:12]: e.bn_aggr(out=o, in_=i)), [b_st], [b_st])
            self.ACT(st[:, 14:15], st[:, 13:14], AF.Sqrt, [b_st, self.b_eps], [b_st], bias=self.epscol, scale=1.0)
            self.RECIP(st[:, 15:16], st[:, 14:15], [b_st], [b_st])
            self.TS("dve", xt, t, st[:, 12:13], st[:, 15:16], ALU.subtract, ALU.mult, [b_t, b_st], [bx])

    def ln_bufs(self):
        self.ln_n = 0
        self.ln_t = [(self.f32(self.alloc(1024), 1024), self.tb()) for _ in range(2)]
        self.ln_s = [(self.f32(self.alloc(16), 16), self.tb()) for _ in range(2)]
        o = self.alloc(1)
        self.epscol = self.f32(o, 1)
        self.b_eps = self.tb()
        self.MEMSET("pool", self.epscol, EPS, [self.b_eps])

    def outproj(self, l, with_ctx):
        i = l // 2
        wname = "woe" if l % 2 == 0 else "woo"
        wo = self.bf(self.alloc(4096), 8192).rearrange("p (k n) -> p k n", k=8)
        b_wok = [self.tb() for _ in range(8)]
        for k in range(8):
            self.DMA("sp", wo[:, k, :], self.dbf[wname][i][:, k, :], [self.bbf[wname][i]], [b_wok[k]])
        self.ln_bufs()
        obufs = [(self.bf(self.alloc(2048), 4096).rearrange("p (k n) -> p k n", k=8), self.tb()) for _ in range(2)]
        nb = 5 if with_ctx else 4
        for blk in range(nb):
            oT, b_o = obufs[blk % 2]
            n = self.blk_n(blk)
            t0 = self.blk_t0(blk)
            self.DMA("sp", oT[:, :, 0:n], self.oT[:, :, t0:t0 + n], [self.b_oT[blk]], [b_o])
            tiles = self.blk_tiles(blk)
            for j0 in range(0, len(tiles), 2):
                groups = []
                for j in range(j0, min(len(tiles), j0 + 2)):
                    banks = [2 * ((j - j0) + 2 * (self.op_n % 2)) + hf for hf in range(2)]
                    groups.append(banks)
                    for hf in range(2):
                        for k in range(8):
                            self.MM(self.ps[:, banks[hf], :], oT[:, k, j * 128:(j + 1) * 128], wo[:, k, hf * 512:(hf + 1) * 512], k == 0, k == 7, [b_o, b_wok[k]], [self.pb[banks[hf]]])
                self.op_n += 1
                self.resid_ln(tiles[j0:j0 + 2], 0 if blk < 4 else 1, 0, groups)
        self.phase()

    def ffn(self, l, with_ctx):
        P = self.P
        cols = self.f32(self.o_cols, 44 * 5 + 8)
        bup = cols[:, 0:44]
        cw = cols[:, 44:176].rearrange("p (i c) -> p i c", i=3)
        cb = cols[:, 176:220]
        b_c = self.b_cols
        self.DMA("sp", bup, self.din["bupT"][l], (), [b_c])
        self.DMA("sp", cols[:, 44:176], self.din["cwT"][l].rearrange("p i c -> p (i c)"), (), [b_c])
        self.DMA("sp", cb, self.din["cbT"][l], (), [b_c])
        o_bd = self.alloc(1024 + 512)
        bdf = self.f32(o_bd, 1024)
        bdb = self.bf(o_bd + 1024, 1024)
        b_bd = self.tb()
        self.DMA("sp", bdf[0:1, :], self.din["bdn"][l:l + 1, :], (), [b_bd])
        self.CP("dve", bdb[0:1, :], bdf[0:1, :], [b_bd], [b_bd])
        ones = self.bf(self.o_ones, 128)
        self.ln_bufs()
        hT = self.bf(self.alloc(2048 + 8), 4096 + 16).rearrange("p (k n) -> p k n", k=8)
        b_hT = self.tb()
        o_hb = self.alloc(32)
        hbnd = self.bf(o_hb, 64).rearrange("p (k n) -> p k n", k=8)
        b_hb = self.tb()
        ubnd = self.f32(self.alloc(44 * 8), 44 * 8).rearrange("p (c n) -> p c n", c=44)
        b_ub = self.tb()
        actT = self.bf(self.alloc(NCH * 256), NCH * 512).rearrange("p (c n) -> p c n", c=NCH)
        b_act = [self.tb() for _ in range(NCH)]
        NR = 4
        ua = [(self.f32(self.alloc(516), 516), self.tb(), self.tb()) for _ in range(2 * NR)]
        acc = [(self.f32(self.alloc(512), 512), self.tb()) for _ in range(2 * NR)]
        wup = [(self.bf(self.alloc(1024), 2048).rearrange("p (k m) -> p k m", k=8), self.tb()) for _ in range(4)]
        wdn = [(self.bf(self.alloc(512), 1024), self.tb()) for _ in range(6)]
        bias2 = self.f32(self.alloc(44), 44)
        b_b2 = self.tb()
        self.STT(bias2, bup, 1.0, cw[:, 1, :], ALU.mult, ALU.mult, [b_c], [b_b2])
        self.TT("dve", bias2, bias2, cb, ALU.add, [b_b2, b_c], [b_b2])
        idf = self.f32(self.o_idf, 128)
        for bi in range(3):
            for side in range(2):
                tile = 4 * (bi + 1) - 1 + side
                xt, bx = self.X(tile), self.b_X[tile]
                p0 = 64 if side == 0 else 0
                hb = 4 + (2 * bi + side) % 4
                for k in range(8):
                    self.TR(self.ps[:, hb, k * 64:(k + 1) * 64], xt[p0:p0 + 64, k * 128:(k + 1) * 128], idf[p0:p0 + 64, p0:p0 + 64], [bx, self.b_const], [self.pb[hb]])
                cc_ = 63 if side == 0 else 0
                for k in range(8):
                    self.ACT(hbnd[:, k, 2 * bi + side:2 * bi + side + 1], self.ps[:, hb, k * 64 + cc_:k * 64 + cc_ + 1], AF.Identity, [self.pb[hb], self.b_fm], [b_hb],
                             bias=self.fmc(2, k, 0), scale=self.fmc(3, k, 0))
        nwin = 5 if with_ctx else 4
        wn = 0
        dn = 0
        un = 0
        deferred = None
        fin = None
        for win in range(nwin):
            n = self.blk_n(win)
            s = 0 if win < 4 else 1
            U = [0, 1, 2, 3] if win % 2 == 0 else [4, 5, 6, 7]
            DA = [4, 5, 6, 7] if win % 2 == 0 else [0, 1, 2, 3]
            self.make_hT(win, 1, hT, b_hT, banks=(U[0], U[1]))
            for c in range(NCH):
                if c == 4 and deferred is not None:
                    deferred()
                    deferred = None
                wt, bw = wup[wn % 4]
                wn += 1
                if not (win > 0 and c < 4):
                    self.DMA("sp", wt, self.dbf["wup"][l, c], [self.bbf["wup"][l]], [bw])
                for half in range(2):
                    ci = c + half * NCH
                    bank = U[(c % 2) * 2 + half]
                    pso = self.ps[:, bank, 0:n]
                    for k in range(8):
                        self.MM(pso, wt[:, k, half * 128:(half + 1) * 128], hT[:, k, 0:n], k == 0, k == 7, [bw, b_hT], [self.pb[bank]])
                    if win == 0:
                        psb = self.ps[:, 6, ci * 8:ci * 8 + 6]
                        for k in range(8):
                            self.MM(psb, wt[:, k, half * 128:(half + 1) * 128], hbnd[:, k, 0:6], k == 0, k == 7, [bw, b_hb], [self.pb[6]])
                        self.ACT(ubnd[:, ci, 0:6], psb, AF.Identity, [self.pb[6], b_c], [b_ub], bias=bup[:, ci:ci + 1])
                    u, b_u, b_uh = ua[(un % NR) * 2 + half]
                    a_, b_a = acc[(un % NR) * 2 + half]
                    self.ACT(u[:, 1:n + 1], pso, AF.Identity, [self.pb[bank], b_c], [b_u], bias=bup[:, ci:ci + 1])
                    self.ACT(a_[:, 0:n], pso, AF.Identity, [self.pb[bank], b_c, b_b2], [b_a], bias=bias2[:, ci:ci + 1], scale=cw[:, 1, ci:ci + 1])
                    if win in (1, 2, 3):
                        self.CP("pool", u[:, 0:1], ubnd[:, ci, 2 * win - 2:2 * win - 1], [b_ub], [b_uh])
                    else:
                        self.MEMSET("pool", u[:, 0:1], 0.0, [b_uh])
                    if win in (0, 1, 2):
                        self.CP("pool", u[:, n + 1:n + 2], ubnd[:, ci, 2 * win + 1:2 * win + 2], [b_ub], [b_uh])
                    else:
                        self.MEMSET("pool", u[:, n + 1:n + 2], 0.0, [b_uh])
                    self.STT(a_[:, 0:n], u[:, 0:n], cw[:, 0, ci:ci + 1], a_[:, 0:n], ALU.mult, ALU.add, [b_u, b_uh, b_a, b_c], [b_a])
                    self.STT(a_[:, 0:n], u[:, 2:n + 2], cw[:, 2, ci:ci + 1], a_[:, 0:n], ALU.mult, ALU.add, [b_u, b_uh, b_a, b_c], [b_a])
                aa, b_aa = acc[(un % NR) * 2]
                ag, b_ag = acc[(un % NR) * 2 + 1]
                un += 1
                if fin is not None:
                    fin()

                def fin(aa=aa, ag=ag, b_aa=b_aa, b_ag=b_ag, c=c, n=n):
                    self.ACT(ag[:, 0:n], ag[:, 0:n], AF.Silu, [b_ag], [b_ag])
                    self.TT("pool", actT[:, c, 0:n], aa[:, 0:n], ag[:, 0:n], ALU.mult, [b_aa, b_ag], [b_act[c]])
            fin()
            fin = None
            tiles = self.blk_tiles(win)
            nt = len(tiles)
            if win + 1 < nwin:
                for c2 in range(4):
                    wt2, bw2 = wup[(wn + c2) % 4]
                    self.DMA("sp", wt2, self.dbf["wup"][l, c2], [self.bbf["wup"][l]], [bw2])
            for pas in range((nt + 1) // 2):
                B4 = DA if pas == 0 else U
                tl = list(range(2 * pas, min(nt, 2 * pas + 2)))
                for c in range(NCH):
                    wt, bw = wdn[dn % 6]
                    dn += 1
                    self.DMA("sp", wt, self.dbf["wdn"][l][:, c, :], [self.bbf["wdn"][l]], [bw])
                    for jj, j in enumerate(tl):
                        for hf in range(2):
                            bank = B4[2 * jj + hf]
                            self.MM(self.ps[:, bank, :], actT[:, c, j * 128:(j + 1) * 128], wt[:, hf * 512:(hf + 1) * 512], c == 0, False, [b_act[c], bw], [self.pb[bank]])
                for jj, j in enumerate(tl):
                    for hf in range(2):
                        bank = B4[2 * jj + hf]
                        self.MM(self.ps[:, bank, :], ones[0:1, 0:128], bdb[0:1, hf * 512:(hf + 1) * 512], False, True, [b_bd, self.b_const], [self.pb[bank]])
                ln = (lambda tl=tl, B4=B4, s=s, tiles=tiles: self.resid_ln([tiles[j] for j in tl], s, 1, [[B4[2 * jj], B4[2 * jj + 1]] for jj in range(len(tl))]))
                if pas == 0 or win == nwin - 1:
                    ln()
                else:
                    deferred = ln
        if deferred is not None:
            deferred()
        self.phase()

    def swa_pass(self, l, with_ctx):
        i = l // 2
        P = self.P
        kT = self.bf(self.alloc(2 * T), 4 * T).rearrange("p (r g n) -> p r g n", r=2, g=2)
        b_kT = self.tb()
        self.MEMSET("pool", kT[64:128, 0, :, :], 0.0, [b_kT])
        self.MEMSET("pool", kT[0:64, 1, :, :], 0.0, [b_kT])
        vv = self.bf(self.alloc(18 * 192), 18 * 384).rearrange("p (t g c) -> p t g c", t=18, g=2)
        b_v = self.tb()
        self.MEMSET("pool", vv[:, :, :, 64:128], 1.0, [b_v])
        cosb = self.f32(self.alloc(512), 512)
        sinb = self.f32(self.alloc(512), 512)
        b_tab = self.tb()
        hT = self.bf(self.alloc(2048), 4096).rearrange("p (k n) -> p k n", k=8)
        b_hT = self.tb()
        wv = self.bf(self.alloc(512), 1024).rearrange("p (k n) -> p k n", k=8)
        b_wv = self.tb()
        self.DMA("sp", wv, self.dbf["wiov"][i], [self.bbf["wiov"][i]], [b_wv])
        wts = self.wring(2, 1024, lambda a: a.rearrange("p (k m) -> p k m", k=8))
        t1 = [(self.f32(self.alloc(512), 512), self.tb()) for _ in range(2)]
        t2 = [(self.f32(self.alloc(512), 512), self.tb()) for _ in range(2)]
        wn = 0

        def rope_proj(wsrc, wbuf, n, t0, out_ap, b_out):
            nonlocal wn
            wt, bw = wts[wn % 2]
            self.DMA("sp", wt, wsrc, [wbuf], [bw])
            ba, bb_ = 2, 3
            for k in range(8):
                self.MM(self.ps[:, ba, 0:n], wt[:, k, 0:128], hT[:, k, 0:n], k == 0, k == 7, [bw, b_hT], [self.pb[ba]])
            for k in range(8):
                self.MM(self.ps[:, bb_, 0:n], wt[:, k, 128:256], hT[:, k, 0:n], k == 0, k == 7, [bw, b_hT], [self.pb[bb_]])
            a, b_a = t1[wn % 2]
            b2, b_b2 = t2[wn % 2]
            wn += 1
            self.TT("dve", a[:, 0:n], self.ps[:, ba, 0:n], cosb[:, 0:n], ALU.mult, [self.pb[ba], b_tab], [b_a])
            self.TT("dve", b2[:, 0:n], self.ps[:, bb_, 0:n], sinb[:, 0:n], ALU.mult, [self.pb[bb_], b_tab], [b_b2])
            if isinstance(out_ap, tuple):
                for par_, o_ in enumerate(out_ap):
                    r_ = slice(par_ * 64, par_ * 64 + 64)
                    self.TT("pool", o_[r_, :], a[r_, 0:n], b2[r_, 0:n], ALU.add, [b_a, b_b2], [b_out])
            else:
                self.TT("pool", out_ap, a[:, 0:n], b2[:, 0:n], ALU.add, [b_a, b_b2], [b_out])

        def load_tabs(blk):
            n, t0 = self.blk_n(blk), self.blk_t0(blk)
            self.DMA("sp", cosb[:, 0:n], self.din["cosS"][:, t0:t0 + n], (), [b_tab])
            self.DMA("sp", sinb[:, 0:n], self.din["sinS"][:, t0:t0 + n], (), [b_tab])

        for blk in range(5):
            n, t0 = self.blk_n(blk), self.blk_t0(blk)
            self.make_hT(blk, 0, hT, b_hT)
            load_tabs(blk)
            for g in range(2):
                rope_proj(self.dbf["wiok"][i, g], self.bbf["wiok"][i], n, t0, (kT[:, 0, g, t0:t0 + n], kT[:, 1, g, t0:t0 + n]), b_kT)
            for j in range(n // 128):
                bank = 6 + j % 2
                for k in range(8):
                    self.MM(self.ps[:, bank, 0:128], hT[:, k, j * 128:(j + 1) * 128], wv[:, k, :], k == 0, k == 7, [b_hT, b_wv], [self.pb[bank]])
                tt = t0 // 128 + j
                pv2 = self.ps[:, bank, 0:128].rearrange("p (g c) -> p g c", g=2)
                self.CP("act", vv[:, tt, :, 0:64], pv2, [self.pb[bank]], [b_v])
                self.CP("dve", vv[:, tt, :, 128:192], pv2, [self.pb[bank]], [b_v])
        o_s = self.alloc(16 + 1024)
        sraw = self.f32(o_s, 16)
        srow = self.bf(o_s + 16, 2048).rearrange("p (h q) -> p h q", h=16)
        b_s = self.tb()
        self.DMA("sp", sraw[0:1, :], self.din["sinks"][i:i + 1, :], (), [b_s])
        self.ACT(sraw[0:1, :], sraw[0:1, :], AF.Exp, [b_s], [b_s])
        self.CP("dve", srow[0:1, :, :], sraw[0:1, :].unsqueeze(2).to_broadcast([1, 16, 128]), [b_s], [b_s])
        o_sl = self.alloc(128)
        sl = self.bf(o_sl, 256)
        self.MEMSET("pool", sl[0:1, 0:256], 0.0, [b_s])
        self.MEMSET("pool", sl[0:1, 64:128], 1.0, [b_s])
        ml = self.f32(self.alloc(128), 128)
        mr = self.f32(self.alloc(128), 128)
        b_m = self.tb()
        self.DMA("sp", ml, self.din["swaml"], (), [b_m])
        self.DMA("sp", mr, self.din["swamr"], (), [b_m])
        qb = self.bf(self.alloc(2048), 4096).rearrange("p (c n) -> p c n", c=8)
        b_q = self.tb()
        ob = [(self.bf(self.alloc(2048), 4096).rearrange("p (c n) -> p c n", c=8), self.tb()) for _ in range(2)]
        self.attn_bufs(512, depth=3, masked=True)
        nb = 5 if with_ctx else 4
        for blk in range(nb):
            n, t0 = self.blk_n(blk), self.blk_t0(blk)
            self.make_hT(blk, 0, hT, b_hT)
            load_tabs(blk)
            for c in range(8):
                rope_proj(self.dbf["wioq"][i, c], self.bbf["wioq"][i], n, t0, qb[:, c, 0:n], b_q)
            oT, b_o = ob[blk % 2]
            for qi in range(n // 128):
                if blk < 4:
                    nblk = blk * 4 + qi
                    kts = [(kt, m) for kt, m in ((nblk - 1, ml), (nblk, None), (nblk + 1, mr)) if 0 <= kt < 16] + [(16, None), (17, None)]
                else:
                    kts = [(16, None), (17, None)]
                for g in range(2):
                    for par in range(2):
                        base = par * 64
                        qsl = qb[:, 4 * g:4 * g + 4, qi * 128:(qi + 1) * 128]
                        qk = [(kT[:, par, g, kt * 128:(kt + 1) * 128], qsl) for kt, _ in kts]
                        pv = [vv[:, kt, g, base:base + 128] for kt, _ in kts]
                        masks = [m for _, m in kts]
                        mf = (lambda j, masks=masks: None if masks[j] is None else (masks[j].unsqueeze(1).to_broadcast([128, 4, 128]), [b_m], 1))
                        h0 = 8 * g + par
                        sink = (sl[0:1, base:base + 128], srow[0:1, h0:h0 + 7:2, :], [b_s])
                        self.attend(qk, 512, 0.125, pv, base, oT[base:base + 64, 4 * g:4 * g + 4, qi * 128:(qi + 1) * 128], b_o, [b_kT, b_q, b_v], mf, sink)
            self.pipe_flush()
            self.DMA("sp", self.oT[:, :, t0:t0 + n], oT[:, :, 0:n], [b_o], [self.b_oT[blk]])
        self.phase()

    def na_pass(self, l, with_ctx):
        i = l // 2
        NSL = len(NA_SLOTS)
        kaT = self.bf(self.alloc(2 * T), 4 * T).rearrange("p (c n) -> p c n", c=4)
        b_k = self.tb()
        va = self.bf(self.alloc(18 * 4 * 96), 18 * 4 * 192).rearrange("p (t g c) -> p t g c", t=18, g=4)
        b_v = self.tb()
        self.MEMSET("pool", va[:, :, :, 64:128], 1.0, [b_v])
        hT = self.bf(self.alloc(2048), 4096).rearrange("p (k n) -> p k n", k=8)
        b_hT = self.tb()
        wv = self.bf(self.alloc(2048), 4096).rearrange("p (k n) -> p k n", k=8)
        b_wv = self.tb()
        self.DMA("sp", wv, self.dbf["wva"][i], [self.bbf["wva"][i]], [b_wv])
        wts = self.wring(4, 512, lambda a: a.rearrange("p (k m) -> p k m", k=8))
        wn = 0
        for blk in range(5):
            n, t0 = self.blk_n(blk), self.blk_t0(blk)
            self.make_hT(blk, 0, hT, b_hT)
            for c in range(4):
                wt, bw = wts[wn % 4]
                wn += 1
                self.DMA("sp", wt, self.dbf["wie"][i, 4 + c], [self.bbf["wie"][i]], [bw])
                bank = 2 + c % 2
                for k in range(8):
                    self.MM(self.ps[:, bank, 0:n], wt[:, k, :], hT[:, k, 0:n], k == 0, k == 7, [bw, b_hT], [self.pb[bank]])
                self.CP("act", kaT[:, c, t0:t0 + n], self.ps[:, bank, 0:n], [self.pb[bank]], [b_k])
            for j in range(n // 128):
                bank = 6 + j % 2
                for k in range(8):
                    self.MM(self.ps[:, bank, :], hT[:, k, j * 128:(j + 1) * 128], wv[:, k, :], k == 0, k == 7, [b_hT, b_wv], [self.pb[bank]])
                tt = t0 // 128 + j
                pv4 = self.ps[:, bank, :].rearrange("p (g c) -> p g c", g=4)
                self.CP("act", va[:, tt, :, 0:64], pv4[:, :, 0:64], [self.pb[bank]], [b_v])
                self.CP("dve", va[:, tt, :, 128:192], pv4[:, :, 64:128], [self.pb[bank]], [b_v])
        msk = self.f32(self.alloc(NSL * 64), NSL * 64).rearrange("p (s q) -> p s q", s=NSL)
        b_m = self.tb()
        self.DMA("sp", msk, self.din["namask"], (), [b_m])
        E = [(self.f32(self.alloc(NSL * 64), NSL * 64).rearrange("p (s q) -> p s q", s=NSL), self.tb()) for _ in range(2)]
        qb = self.bf(self.alloc(1024), 2048).rearrange("p (c n) -> p c n", c=4)
        b_q = self.tb()
        ob = [(self.bf(self.alloc(1024), 2048).rearrange("p (c n) -> p c n", c=4), self.tb()) for _ in range(2)]
        self.attn_bufs(512, depth=3, masked=True, rcw=256)
        rp = self.din["rpb"]
        nb = 5 if with_ctx else 4
        en = 0
        for blk in range(nb):
            n, t0 = self.blk_n(blk), self.blk_t0(blk)
            self.make_hT(blk, 0, hT, b_hT)
            for c in range(4):
                wt, bw = wts[wn % 4]
                wn += 1
                self.DMA("sp", wt, self.dbf["wie"][i, c], [self.bbf["wie"][i]], [bw])
                bank = 2 + c % 2
                for k in range(8):
                    self.MM(self.ps[:, bank, 0:n], wt[:, k, :], hT[:, k, 0:n], k == 0, k == 7, [bw, b_hT], [self.pb[bank]])
                self.CP("act", qb[:, c, 0:n], self.ps[:, bank, 0:n], [self.pb[bank]], [b_q])
            oT, b_o = ob[blk % 2]
            for h in range(8):
                c, base = h // 2, (h % 2) * 64
                if blk < 4:
                    Et, b_E = E[en % 2]
                    en += 1
                    def src(dr, cnt, step):
                        return rp[i, h, dr:dr + cnt * step:step, :, :].rearrange("d k q -> k d q")
                    self.DMA("sp", Et[0:64, 0:14, :], src(0, 14, 1), (), [b_E])
                    self.DMA("sp", Et[64:128, 0:14, :], src(1, 14, 1), (), [b_E])
                    self.DMA("sp", Et[0:64, 14:19, :], src(2, 5, 2), (), [b_E])
                    self.DMA("sp", Et[64:128, 14:19, :], src(3, 5, 2), (), [b_E])
                    Ef = Et.rearrange("p s q -> p (s q)")
                    self.ACT(Ef, Ef, AF.Exp, [b_E], [b_E])
                    self.TT("pool", Ef, Ef, msk.rearrange("p s q -> p (s q)"), ALU.mult, [b_E, b_m], [b_E])
                    for rr in range(8):
                        r = blk * 8 + rr
                        r0 = min(max(r - 4, 0), 24)
                        if r0 % 2 == 0:
                            tiles_ = [r0 // 2 + j for j in range(4)]
                            d0 = r0 - r + 7
                            eslice = Et[:, d0:d0 + 8:2, :] if True else None
                            span = 4
                        else:
                            tiles_ = [(r0 - 1) // 2 + j for j in range(5)]
                            eslice = Et[:, 14:19, :]
                            span = 5
                        kts = tiles_ + [16, 17]
                        qk = [(kaT[base:base + 64, c, kt * 128:(kt + 1) * 128], qb[base:base + 64, c, rr * 64:(rr + 1) * 64]) for kt in kts]
                        pv = [va[:, kt, c, base:base + 128] for kt in kts]
                        mf = (lambda j, eslice=eslice, span=span, b_E=b_E: (eslice, [b_E], span) if j == 0 else None)
                        self.attend(qk, 64, 0.125, pv, base, oT[base:base + 64, c, rr * 64:(rr + 1) * 64], b_o, [b_k, b_q, b_v], mf)
                else:
                    kts = [16, 17]
                    qk = [(kaT[base:base + 64, c, kt * 128:(kt + 1) * 128], qb[base:base + 64, c, 0:256]) for kt in kts]
                    pv = [va[:, kt, c, base:base + 128] for kt in kts]
                    self.attend(qk, 256, 0.125, pv, base, oT[base:base + 64, c, 0:256], b_o, [b_k, b_q, b_v])
            self.pipe_flush()
            self.DMA("sp", self.oT[:, 0:4, t0:t0 + n], oT[:, :, 0:n], [b_o], [self.b_oT[blk]])
        self.phase()

    def mla_pass(self, l, hh, with_ctx):
        i = l // 2
        sc = 96.0 ** -0.5
        km = self.bf(self.alloc(2 * T), 4 * T).rearrange("p (h n) -> p h n", h=4)
        b_k = self.tb()
        vm = self.bf(self.alloc(18 * 2 * 96), 18 * 2 * 192).rearrange("p (t g c) -> p t g c", t=18, g=2)
        b_v = self.tb()
        self.MEMSET("pool", vm[:, :, :, 64:128], 1.0, [b_v])
        hT = self.bf(self.alloc(2048), 4096).rearrange("p (k n) -> p k n", k=8)
        b_hT = self.tb()
        cols = self.f32(self.o_cols + 220, 8)
        b_c = self.b_cols
        self.DMA("sp", cols[:, 0:3], self.din["qnT"][i], (), [b_c])
        self.DMA("sp", cols[:, 3:5], self.din["kvnT"][i], (), [b_c])
        cqg = self.bf(self.alloc(768), 1536).rearrange("p (k n) -> p k n", k=3)
        sq = self.bf(self.alloc(768), 1536).rearrange("p (k n) -> p k n", k=3)
        ckg = self.bf(self.alloc(512), 1024).rearrange("p (k n) -> p k n", k=2)
        sk = self.bf(self.alloc(512), 1024).rearrange("p (k n) -> p k n", k=2)
        b_cq, b_ck = self.tb(), self.tb()
        rq = self.f32(self.alloc(512), 512)
        rk = self.f32(self.alloc(512), 512)
        rkc = self.f32(self.alloc(4), 4)
        b_r = self.tb()
        cosb = self.f32(self.alloc(512), 512)
        sinb = self.f32(self.alloc(512), 512)
        b_tab = self.tb()
        rc = self.f32(self.alloc(512), 512)
        rs = self.f32(self.alloc(512), 512)
        b_rc = self.tb()
        krT = self.bf(self.alloc(256), 512)
        b_kr = self.tb()
        tA = self.f32(self.alloc(512), 512)
        tB = self.f32(self.alloc(512), 512)
        b_tA, b_tB = self.tb(), self.tb()
        w5 = self.bf(self.alloc(2560), 5120).rearrange("p (c k m) -> p c k m", c=5, k=8)
        b_w5 = self.tb()
        self.DMA("sp", w5, self.dbf["wie"][i, 12:17].rearrange("c p k m -> p c k m"), [self.bbf["wie"][i]], [b_w5])
        wkr = self.bf(self.alloc(256), 512).rearrange("p (k m) -> p k m", k=8)
        b_wkr = self.tb()
        self.DMA("sp", wkr, self.dbf["wkr"][i], [self.bbf["wkr"][i]], [b_wkr])
        wq4 = self.bf(self.alloc(1152), 2304).rearrange("p (h k m) -> p h k m", h=4, k=3)
        b_wq4 = self.tb()
        self.DMA("sp", wq4, self.dbf["wuq"][i, 4 * hh:4 * hh + 4].rearrange("h p k m -> p h k m"), [self.bbf["wuq"][i]], [b_wq4])
        wkn4 = self.bf(self.alloc(256), 512).rearrange("p (h k m) -> p h k m", h=4, k=2)
        b_wkn4 = self.tb()
        self.DMA("sp", wkn4, self.dbf["wukvn"][i, 4 * hh:4 * hh + 4].rearrange("h p k m -> p h k m"), [self.bbf["wukvn"][i]], [b_wkn4])
        wvv = self.bf(self.alloc(256), 512).rearrange("p (k m) -> p k m", k=2)
        b_wvv = self.tb()
        self.DMA("sp", wvv, self.dbf["wukvv"][i][:, :, hh * 256:(hh + 1) * 256], [self.bbf["wukvv"][i]], [b_wvv])
        ones = self.bf(self.o_ones, 128)
        sel = self.bf(self.o_sel, 96)
        wn = [0, 0, 0]

        def load_tabs(blk):
            n, t0 = self.blk_n(blk), self.blk_t0(blk)
            self.DMA("sp", cosb[:, 0:n], self.din["cosM"][:, t0:t0 + n], (), [b_tab])
            self.DMA("sp", sinb[:, 0:n], self.din["sinM"][:, t0:t0 + n], (), [b_tab])

        def lowrank(blk, which):
            n = self.blk_n(blk)
            if which == "q":
                chunks, dst, dsq, col0, out_r, dim, b_d = (12, 13, 14), cqg, sq, 0, rq, 384.0, b_cq
            else:
                chunks, dst, dsq, col0, out_r, dim, b_d = (15, 16), ckg, sk, 3, rk, 256.0, b_ck
            for kk, c in enumerate(chunks):
                wt, bw = w5[:, c - 12], b_w5
                bank = 2 + kk % 2
                for k in range(8):
                    self.MM(self.ps[:, bank, 0:n], wt[:, k, :], hT[:, k, 0:n], k == 0, k == 7, [bw, b_hT], [self.pb[bank]])
                self.ACT(dst[:, kk, 0:n], self.ps[:, bank, 0:n], AF.Identity, [self.pb[bank], b_c], [b_d], scale=cols[:, col0 + kk:col0 + kk + 1])
                self.ACT(dsq[:, kk, 0:n], self.ps[:, bank, 0:n], AF.Square, [self.pb[bank]], [b_d])
            nk = len(chunks)
            bank = 2
            for kk in range(nk):
                self.MM(self.ps[:, bank, 0:n], ones[:, 0:128], dsq[:, kk, 0:n], kk == 0, kk == nk - 1, [b_d, self.b_const], [self.pb[bank]])
            self.ACT(out_r[:, 0:n], self.ps[:, bank, 0:n], AF.Sqrt, [self.pb[bank], self.b_eps], [b_r], bias=self.epscol, scale=1.0 / dim)
            self.RECIP(out_r[:, 0:n], out_r[:, 0:n], [b_r], [b_r])

        self.epscol = self.f32(self.alloc(1), 1)
        self.b_eps = self.tb()
        self.MEMSET("pool", self.epscol, EPS, [self.b_eps])
        for blk in range(5):
            n, t0 = self.blk_n(blk), self.blk_t0(blk)
            self.make_hT(blk, 0, hT, b_hT)
            load_tabs(blk)
            lowrank(blk, "k")
            for part, bank in ((0, 6), (1, 7)):
                for k in range(8):
                    self.MM(self.ps[0:32, bank, 0:n], wkr[:, k, part * 32:(part + 1) * 32], hT[:, k, 0:n], k == 0, k == 7, [b_wkr, b_hT], [self.pb[bank]])
            self.TT("dve", tA[0:32, 0:n], self.ps[0:32, 6, 0:n], cosb[0:32, 0:n], ALU.mult, [self.pb[6], b_tab], [b_tA])
            self.TT("dve", tB[0:32, 0:n], self.ps[0:32, 7, 0:n], sinb[0:32, 0:n], ALU.mult, [self.pb[7], b_tab], [b_tB])
            self.TT("pool", krT[0:32, 0:n], tA[0:32, 0:n], tB[0:32, 0:n], ALU.add, [b_tA, b_tB], [b_kr])
            for hq in range(4):
                h = hh * 4 + hq
                wt, bw = wkn4[:, hq], b_wkn4
                bank = 4 + hq % 2
                self.MM(self.ps[0:96, bank, 0:n], sel[0:32, 0:96], krT[0:32, 0:n], True, False, [b_kr, self.b_const], [self.pb[bank]])
                for k in range(2):
                    self.MM(self.ps[0:64, bank, 0:n], wt[:, k, :], ckg[:, k, 0:n], False, k == 1, [bw, b_ck], [self.pb[bank]])
                self.TT("dve", km[0:64, hq, t0:t0 + n], self.ps[0:64, bank, 0:n], rk[0:64, 0:n], ALU.mult, [self.pb[bank], b_r], [b_k])
                self.CP("act", km[64:96, hq, t0:t0 + n], self.ps[64:96, bank, 0:n], [self.pb[bank]], [b_k])
            for j in range(n // 128):
                bank = 6 + j % 2
                for k in range(2):
                    self.MM(self.ps[:, bank, 0:2], sk[:, k, j * 128:(j + 1) * 128], ones[:, 0:2], k == 0, k == 1, [b_ck, self.b_const], [self.pb[bank]])
                self.ACT(rkc[:, j:j + 1], self.ps[:, bank, 0:1], AF.Sqrt, [self.pb[bank], self.b_eps], [b_r], bias=self.epscol, scale=1.0 / 256.0)
                self.RECIP(rkc[:, j:j + 1], rkc[:, j:j + 1], [b_r], [b_r])
                bank2 = 4 + j % 2
                for k in range(2):
                    self.MM(self.ps[:, bank2, 0:256], ckg[:, k, j * 128:(j + 1) * 128], wvv[:, k, :], k == 0, k == 1, [b_ck, b_wvv], [self.pb[bank2]])
                tt = t0 // 128 + j
                pv4 = self.ps[:, bank2, 0:256].rearrange("p (g c) -> p g c", g=2)
                self.TS("dve", vm[:, tt, :, 0:64], pv4[:, :, 0:64], rkc[:, j:j + 1], None, ALU.mult, None, [self.pb[bank2], b_r], [b_v])
                self.TS("dve", vm[:, tt, :, 128:192], pv4[:, :, 64:128], rkc[:, j:j + 1], None, ALU.mult, None, [self.pb[bank2], b_r], [b_v])
        qb = self.bf(self.alloc(1024), 2048).rearrange("p (h n) -> p h n", h=4)
        b_q = self.tb()
        ob = [(self.bf(self.alloc(512), 1024).rearrange("p (c n) -> p c n", c=2), self.tb()) for _ in range(2)]
        self.attn_bufs(512, depth=5, masked=False)
        nb = 5 if with_ctx else 4
        for blk in range(nb):
            n, t0 = self.blk_n(blk), self.blk_t0(blk)
            self.make_hT(blk, 0, hT, b_hT)
            load_tabs(blk)
            lowrank(blk, "q")
            self.TT("dve", rc[64:96, 0:n], rq[64:96, 0:n], cosb[64:96, 0:n], ALU.mult, [b_r, b_tab], [b_rc])
            self.TT("dve", rs[64:96, 0:n], rq[64:96, 0:n], sinb[64:96, 0:n], ALU.mult, [b_r, b_tab], [b_rc])
            for hq in range(4):
                h = hh * 4 + hq
                wt, bw = wq4[:, hq], b_wq4
                ba, bb_ = 2, 3
                for k in range(3):
                    self.MM(self.ps[0:96, ba, 0:n], wt[:, k, 0:96], cqg[:, k, 0:n], k == 0, k == 2, [bw, b_cq], [self.pb[ba]])
                for k in range(3):
                    self.MM(self.ps[0:96, bb_, 0:n], wt[:, k, 96:192], cqg[:, k, 0:n], k == 0, k == 2, [bw, b_cq], [self.pb[bb_]])
                self.TT("dve", qb[0:64, hq, 0:n], self.ps[0:64, ba, 0:n], rq[0:64, 0:n], ALU.mult, [self.pb[ba], b_r], [b_q])
                self.TT("dve", tA[64:96, 0:n], self.ps[64:96, ba, 0:n], rc[64:96, 0:n], ALU.mult, [self.pb[ba], b_rc], [b_tA])
                self.TT("dve", tB[64:96, 0:n], self.ps[64:96, bb_, 0:n], rs[64:96, 0:n], ALU.mult, [self.pb[bb_], b_rc], [b_tB])
                self.TT("pool", qb[64:96, hq, 0:n], tA[64:96, 0:n], tB[64:96, 0:n], ALU.add, [b_tA, b_tB], [b_q])
            oT, b_o = ob[blk % 2]
            kts = list(range(18)) if blk < 4 else [16, 17]
            for hq in range(4):
                g, par = hq // 2, hq % 2
                base = par * 64
                qk = [(km[0:96, hq, kt * 128:(kt + 1) * 128], qb[0:96, hq, 0:n]) for kt in kts]
                pv = [vm[:, kt, g, base:base + 128] for kt in kts]
                self.attend(qk, n, sc, pv, base, oT[base:base + 64, g, 0:n], b_o, [b_k, b_q, b_v])
            self.pipe_flush()
            self.DMA("sp", self.oT[:, 4 + 2 * hh:6 + 2 * hh, t0:t0 + n], oT[:, :, 0:n], [b_o], [self.b_oT[blk]])
        self.phase()

    def layer(self, b, l):
        with_ctx = l < 3
        self.op_n = 0
        self.mods(b, l)
        if l % 2 == 0:
            self.na_pass(l, with_ctx)
            self.mla_pass(l, 0, with_ctx)
            self.mla_pass(l, 1, with_ctx)
        else:
            self.swa_pass(l, with_ctx)
        self.outproj(l, with_ctx)
        self.ffn(l, with_ctx)


_CACHE = {}


def _prepare(inputs):
    w = _host_weights(inputs)
    w.update(_host_consts())
    return w


def kernel(**inputs):
    NL = int(inputs.pop("_NL", 4))
    ncores = int(inputs.pop("_NCORES", 8))
    w = _prepare(inputs)
    x = np.ascontiguousarray(np.asarray(inputs["x"], np.float32))
    ctx = np.ascontiguousarray(np.asarray(inputs["ctx"], np.float32))
    c = np.asarray(inputs["c"], np.float32)
    c_ctx = np.asarray(inputs["c_ctx"], np.float32)
    in_maps = []
    for core in range(ncores):
        m = dict(w)
        m["x"] = x[2 * core:2 * core + 2]
        m["ctx"] = ctx[2 * core:2 * core + 2]
        cc = np.stack([c[2 * core], c[2 * core + 1], c_ctx], axis=1)
        m["cT"] = np.ascontiguousarray(cc.reshape(8, 128, 3).transpose(1, 0, 2))
        in_maps.append(m)
    shapes = {k: v.shape for k, v in in_maps[0].items()}
    key = (NL, tuple(sorted((k, tuple(s)) for k, s in shapes.items())))
    nc = Builder(shapes, NL=NL, NB=2).build()
    res = run_bass_kernel_spmd(nc, in_maps, core_ids=list(range(ncores)))
    if getattr(res, "exec_time_ns", None) is not None:
        print("EXEC_TIME_NS", res.exec_time_ns)
    out = np.concatenate([np.asarray(r["out"]) for r in res.results], axis=0)
    return out.astype(np.float32)
```

```python
import numpy as np
from contextlib import ExitStack
import concourse.bass as bass
import concourse.mybir as mybir
from concourse.bass_utils import run_bass_kernel_spmd

F32 = mybir.dt.float32
BF16 = mybir.dt.bfloat16
AF = mybir.ActivationFunctionType
ALU = mybir.AluOpType

ENGS = ("pe", "act", "dve", "pool", "sp")
NRING = 32
NRINGQ = {"sp": 48, "pool": 16}

D = 1024
S = 2048
CT = 256
T = S + CT
DFF = 2816
NCH = 22
ALPHA = 8.0 ** 0.25
EPS = 1e-6


class Buf:
    __slots__ = ("w", "r", "name", "keep", "tmp")

    def __init__(self, name=""):
        self.w = None
        self.r = []
        self.name = name
        self.keep = False
        self.tmp = False


class Prog:
    def __init__(self, nc):
        self.nc = nc
        self.ops = {e: [] for e in ENGS}
        self.known = {e: {} for e in ENGS}
        self.known_dma = {e: set() for e in ENGS}
        self.dma_cnt = {e: 0 for e in ENGS}
        self.all_bufs = []

    def buf(self, name=""):
        b = Buf(name)
        self.all_bufs.append(b)
        return b

    def bufs(self, n, name=""):
        return [self.buf(f"{name}{i}") for i in range(n)]

    def _collect(self, eng, reads, writes):
        deps = {}
        ddeps = set()

        def add(d, raw):
            if d is None:
                return
            if d[0] == "dma":
                ddeps.add(d)
                return
            e2, i2 = d
            if e2 == eng:
                if eng == "pe" or not raw:
                    return
            if deps.get(e2, -1) < i2:
                deps[e2] = i2

        for b in reads:
            add(b.w, True)
        for b in writes:
            add(b.w, False)
            for r in b.r:
                add(r, False)
        waits = []
        kn = self.known[eng]
        for e2, i2 in deps.items():
            if kn.get(e2, -1) >= i2:
                continue
            kn[e2] = i2
            waits.append(("eng", e2, i2))
        kd = self.known_dma[eng]
        for d in ddeps:
            if d in kd:
                continue
            kd.add(d)
            waits.append(("dma", d[1], d[2]))
        return waits

    def op(self, eng, fn, reads=(), writes=()):
        waits = self._collect(eng, reads, writes)
        idx = len(self.ops[eng])
        self.ops[eng].append([fn, waits, False, None])
        me = (eng, idx)
        for b in reads:
            if len(b.r) > 64:
                b.r = b.r[-32:]
            b.r.append(me)
        for b in writes:
            b.w = me
            b.r = []
        return me

    def dma(self, queue, fn, reads=(), writes=()):
        waits = self._collect(queue, reads, writes)
        n = self.dma_cnt[queue]
        self.dma_cnt[queue] = n + 1
        nr = NRINGQ.get(queue, NRING)
        if n >= nr:
            prev = ("dma", queue, n - nr)
            if prev not in self.known_dma[queue]:
                self.known_dma[queue].add(prev)
                waits.append(("dma", queue, n - nr))
        self.ops[queue].append([fn, waits, False, n])
        me = ("dma", queue, n)
        for b in reads:
            b.r.append(me)
        for b in writes:
            b.w = me
            b.r = []
        return me

    def barrier(self):
        lasts = []
        for e in ENGS:
            for i in range(len(self.ops[e]) - 1, -1, -1):
                o = self.ops[e][i]
                if o[0] is not None and o[3] is None:
                    lasts.append((e, i))
                    break
        pend = set()
        for b in self.all_bufs:
            if b.keep:
                continue
            if b.w is not None and b.w[0] == "dma":
                pend.add(b.w)
            for r in b.r:
                if r[0] == "dma":
                    pend.add(r)
        for e in ENGS:
            waits = []
            kn = self.known[e]
            for e2, i2 in lasts:
                if e2 == e:
                    continue
                if kn.get(e2, -1) >= i2:
                    continue
                kn[e2] = i2
                waits.append(("eng", e2, i2))
            kd = self.known_dma[e]
            for d in pend:
                if d in kd:
                    continue
                kd.add(d)
                waits.append(("dma", d[1], d[2]))
            self.ops[e].append([None, waits, False, None])
        for b in self.all_bufs:
            if b.keep:
                b.r = []
                continue
            b.w = None
            b.r = []
        self.all_bufs = [b for b in self.all_bufs if not b.tmp]

    def emit(self):
        nc = self.nc
        ops = self.ops
        for e in ENGS:
            for o in ops[e]:
                for w in o[1]:
                    if w[0] == "eng":
                        ops[w[1]][w[2]][2] = True
        cnt = {}
        for e in ENGS:
            c = 0
            arr = []
            for o in ops[e]:
                if o[2]:
                    c += 1
                arr.append(c)
            cnt[e] = arr
        with ExitStack() as st:
            esem = {e: st.enter_context(nc.semaphore(f"s_{e}")) for e in ENGS}
            rings = {}
            for q in ENGS:
                if self.dma_cnt[q] > 0:
                    rings[q] = [st.enter_context(nc.semaphore(f"r_{q}{i}")) for i in range(min(NRINGQ.get(q, NRING), self.dma_cnt[q]))]
            block = st.enter_context(nc.Block())

            def run(ename, eng):
                for fn, waits, sig, dman in ops[ename]:
                    for w in waits:
                        if w[0] == "eng":
                            eng.wait_ge(esem[w[1]], cnt[w[1]][w[2]])
                        else:
                            q, n = w[1], w[2]
                            nr = NRINGQ.get(q, NRING)
                            eng.wait_ge(rings[q][n % nr], 16 * (n // nr + 1))
                    if fn is None:
                        continue
                    ins = fn(eng)
                    if dman is not None:
                        ins.then_inc(rings[ename][dman % NRINGQ.get(ename, NRING)], 16)
                    elif sig:
                        ins.then_inc(esem[ename], 1)

            @block.tensor
            def _(eng):
                run("pe", eng)

            @block.scalar
            def _(eng):
                run("act", eng)

            @block.vector
            def _(eng):
                run("dve", eng)

            @block.gpsimd
            def _(eng):
                run("pool", eng)

            @block.sync
            def _(eng):
                run("sp", eng)


def _ktile(w, m):
    K, N = w.shape
    return np.ascontiguousarray(w.reshape(K // 128, 128, N // m, m).transpose(2, 1, 0, 3))


def _rtile(w):
    K, N = w.shape
    return np.ascontiguousarray(w.reshape(K // 128, 128, N).transpose(1, 0, 2))


def _colT(v):
    return np.ascontiguousarray(v.reshape(-1, 128).T)


def _rotperm(nheads, hd):
    q = hd // 4
    p = []
    for h in range(nheads):
        for d in range(hd):
            blk = (d // q)
            src = d + q if blk % 2 == 0 else d - q
            p.append(h * hd + src)
    return np.array(p)


def _rope_tables(rot_dim):
    axis_dim = rot_dim // 2
    t = np.arange(S)
    row = (t // 64).astype(np.float32)[:, None]
    col = (t % 64).astype(np.float32)[:, None]
    inv_freq = (10000.0 ** (-np.arange(0, axis_dim, 2, dtype=np.float32) / axis_dim)).astype(np.float32)
    ar, ac = row * inv_freq, col * inv_freq
    ang = np.concatenate([ar, ar, ac, ac], axis=-1)
    cos = np.cos(ang).astype(np.float32)
    sin = np.sin(ang).astype(np.float32)
    q = rot_dim // 4
    sign = np.ones(rot_dim, np.float32)
    for d in range(rot_dim):
        if (d // q) % 2 == 0:
            sign[d] = -1.0
    sin = sin * sign
    cosT = np.concatenate([cos.T, np.ones((rot_dim, CT), np.float32)], axis=1)
    sinT = np.concatenate([sin.T, np.zeros((rot_dim, CT), np.float32)], axis=1)
    return cosT, sinT


NA_SLOTS = [(d, True, True) for d in range(14)] + [(2, False, True), (4, True, True), (6, True, True), (8, True, True), (10, True, False)]


def _host_consts():
    c = {}
    c["identf"] = np.eye(128, dtype=np.float32)
    sel = np.zeros((32, 96), np.float32)
    for j in range(32):
        sel[j, 64 + j] = 1.0
    c["sel"] = sel
    qc = np.arange(64)
    ws = np.clip(qc - 8, 0, 48)
    kc = np.arange(64)[:, None]
    colok = ((kc >= ws[None, :]) & (kc < ws[None, :] + 16)).astype(np.float32)
    m = np.zeros((128, len(NA_SLOTS), 64), np.float32)
    for s, (d, vlo, vhi) in enumerate(NA_SLOTS):
        if vlo:
            m[0:64, s, :] = colok
        if vhi:
            m[64:128, s, :] = colok
    c["namask"] = m
    j = np.arange(128)[:, None]
    i = np.arange(128)[None, :]
    c["swaml"] = (j >= i).astype(np.float32)
    c["swamr"] = (j <= i).astype(np.float32)
    cs, ss = _rope_tables(64)
    c["cosS"] = np.concatenate([cs, cs], axis=0)
    c["sinS"] = np.concatenate([ss, ss], axis=0)
    cm, sm = _rope_tables(32)
    c["cosM"] = np.concatenate([cm, cm, cm, cm], axis=0)
    c["sinM"] = np.concatenate([sm, sm, sm, sm], axis=0)
    return c


def _host_weights(inp):
    w = {}
    f = lambda a: np.ascontiguousarray(np.asarray(a, dtype=np.float32))
    w_ada = f(inp["w_ada"])
    w["wada"] = np.stack([_ktile(w_ada[l], 128) for l in range(4)])
    b_ada = f(inp["b_ada"])
    w["badaT"] = np.stack([_colT(b_ada[l]) for l in range(4)])
    w["bada"] = b_ada
    wie = f(inp["w_in_even"])
    w["wie"] = np.stack([_ktile(wie[i][:, 0:2176], 128) for i in range(2)])
    pk = _rotperm(1, 32)
    w["wkr"] = np.stack([_rtile(np.concatenate([wie[i][:, 2176:2208], wie[i][:, 2176:2208][:, pk]], axis=1)) for i in range(2)])
    w["wva"] = np.stack([_rtile(wie[i][:, 1024:1536]) for i in range(2)])
    wuq = f(inp["w_uq"])
    tiles = []
    for i in range(2):
        hs = []
        for h in range(8):
            blk = wuq[i][:, h * 96:(h + 1) * 96]
            rot = blk.copy()
            rot[:, 64:96] = blk[:, 64:96][:, pk]
            hs.append(_rtile(np.concatenate([blk, rot], axis=1)))
        tiles.append(np.stack(hs))
    w["wuq"] = np.stack(tiles)
    wukv = f(inp["w_ukv"])
    w["wukvn"] = np.stack([np.stack([_rtile(wukv[i][:, h * 128:h * 128 + 64]) for h in range(8)]) for i in range(2)])
    w["wukvv"] = np.stack([_rtile(np.concatenate([wukv[i][:, h * 128 + 64:h * 128 + 128] for h in range(8)], axis=1)) for i in range(2)])
    w["qnT"] = np.stack([_colT(f(inp["mla_q_norm"])[i]) for i in range(2)])
    w["kvnT"] = np.stack([_colT(f(inp["mla_kv_norm"])[i]) for i in range(2)])
    w["woe"] = np.stack([_rtile(f(inp["w_out_even"])[i]) for i in range(2)])
    wio = f(inp["w_in_odd"])
    pq = _rotperm(16, 64)
    p1 = _rotperm(1, 64)
    tq = []
    tk = []
    for i in range(2):
        q = wio[i][:, 0:1024]
        qr = q[:, pq]
        tq.append(np.stack([_rtile(np.concatenate([q[:, c * 128:(c + 1) * 128], qr[:, c * 128:(c + 1) * 128]], axis=1)) for c in range(8)]))
        ks = []
        for g in range(2):
            k = wio[i][:, 1024 + g * 64:1024 + (g + 1) * 64]
            kr = k[:, p1]
            ks.append(_rtile(np.concatenate([k, k, kr, kr], axis=1)))
        tk.append(np.stack(ks))
    w["wioq"] = np.stack(tq)
    w["wiok"] = np.stack(tk)
    w["wiov"] = np.stack([_rtile(wio[i][:, 1152:1280]) for i in range(2)])
    w["sinks"] = f(inp["sinks"])
    w["woo"] = np.stack([_rtile(f(inp["w_out_odd"])[i]) for i in range(2)])
    wup = f(inp["w_up"])
    w["wup"] = np.stack([np.stack([_rtile(np.concatenate([wup[l][:, c * 128:(c + 1) * 128], wup[l][:, DFF + c * 128:DFF + (c + 1) * 128]], axis=1)) for c in range(NCH)]) for l in range(4)])
    w["bupT"] = np.stack([_colT(f(inp["b_up"])[l]) for l in range(4)])
    cw = f(inp["conv_w"])
    w["cwT"] = np.stack([np.stack([_colT(cw[l, i]) for i in range(3)], axis=1) for l in range(4)])
    w["cbT"] = np.stack([_colT(f(inp["conv_b"])[l]) for l in range(4)])
    w["wdn"] = np.stack([_rtile(f(inp["w_down"])[l]) for l in range(4)])
    w["bdn"] = f(inp["b_down"])
    rpb = f(inp["na_rpb"])
    kc = np.arange(64)[:, None]
    qc = np.arange(64)[None, :]
    idx = np.clip(kc - qc + 15, 0, 30)
    w["rpb"] = np.ascontiguousarray(rpb[:, :, :, idx])
    return w


CAST = ["wada", "wie", "wkr", "wva", "wuq", "wukvn", "wukvv", "woe", "wioq", "wiok", "wiov", "woo", "wup", "wdn"]


class Builder:
    def __init__(self, shapes, NL=4, NB=2):
        self.NL, self.NB = NL, NB
        nc = self.nc = bass.Bass("TRN2", target_bir_lowering=False)
        self.P = Prog(nc)
        self.din = {}
        for k, shp in shapes.items():
            self.din[k] = nc.dram_tensor(k, list(shp), F32, kind="ExternalInput").ap()
        self.dbf = {}
        self.bbf = {}
        for k in CAST:
            shp = shapes[k]
            self.dbf[k] = nc.dram_tensor(k + "_bf", list(shp), BF16, kind="Internal").ap()
            self.bbf[k] = [self.P.buf(f"{k}{i}") for i in range(shp[0])]
            for b_ in self.bbf[k]:
                b_.keep = True
        self.oT = nc.dram_tensor("oT_scr", [128, 8, T], BF16, kind="Internal").ap()
        self.b_oT = [self.P.buf(f"oT{i}") for i in range(5)]
        self.gsc = [nc.dram_tensor(f"gsc{l}", [128, 2, 2, 1024], F32, kind="Internal").ap() for l in range(4)]
        self.b_gsc = [self.P.buf(f"gsc{l}") for l in range(4)]
        for b_ in self.b_gsc:
            b_.keep = True
        self.out = nc.dram_tensor("out", [NB, S, D], F32, kind="ExternalOutput").ap()
        self.b_out = self.P.buf("out")

    def MM(self, out, lhsT, rhs, start, stop, rd, wr):
        self.P.op("pe", lambda e: e.matmul(out, lhsT=lhsT, rhs=rhs, start=start, stop=stop), rd, wr)

    def TR(self, out, in_, ident, rd, wr):
        self.P.op("pe", lambda e: e.transpose(out=out, in_=in_, identity=ident), rd, wr)

    def ACT(self, out, in_, func, rd, wr, bias=None, scale=None):
        kw = {}
        if bias is not None:
            kw["bias"] = bias
        if scale is not None:
            kw["scale"] = scale
        self.P.op("act", lambda e: e.activation(out=out, in_=in_, func=func, **kw), rd, wr)

    def TT(self, eng, out, in0, in1, op, rd, wr):
        self.P.op(eng, lambda e: e.tensor_tensor(out=out, in0=in0, in1=in1, op=op), rd, wr)

    def TS(self, eng, out, in0, s1, s2, op0, op1, rd, wr):
        if s2 is None:
            self.P.op(eng, lambda e: e.tensor_scalar(out=out, in0=in0, scalar1=s1, scalar2=None, op0=op0), rd, wr)
        else:
            self.P.op(eng, lambda e: e.tensor_scalar(out=out, in0=in0, scalar1=s1, scalar2=s2, op0=op0, op1=op1), rd, wr)

    def STT(self, out, in0, scalar, in1, op0, op1, rd, wr):
        self.P.op("dve", lambda e: e.scalar_tensor_tensor(out=out, in0=in0, scalar=scalar, in1=in1, op0=op0, op1=op1), rd, wr)

    def CP(self, eng, out, in_, rd, wr):
        if eng == "act":
            self.P.op("act", lambda e: e.copy(out=out, in_=in_), rd, wr)
        else:
            self.P.op(eng, lambda e: e.tensor_copy(out=out, in_=in_), rd, wr)

    def RECIP(self, out, in_, rd, wr):
        self.P.op("dve", lambda e: e.reciprocal(out=out, in_=in_), rd, wr)

    def MEMSET(self, eng, ap, val, wr):
        self.P.op(eng, lambda e: e.memset(ap, val), (), wr)

    def DMA(self, q, out, in_, rd, wr):
        self.P.dma(q, lambda e: e.dma_start(out=out, in_=in_), rd, wr)

    def alloc(self, words):
        off = self.top
        self.top += words
        assert self.top <= self.AW, (self.top, self.AW)
        return off

    def f32(self, off, n):
        return self.arena[:, off:off + n]

    def bf(self, off, n):
        return self.arena[:, off:off + (n + 1) // 2].bitcast(BF16)[:, 0:n]

    def tb(self, name=""):
        b = self.P.buf(name)
        b.tmp = True
        return b

    def phase(self):
        self.P.barrier()
        self.top = self.persist_top

    def build(self):
        nc, P = self.nc, self.P
        with ExitStack() as st:
            self.AW = 53000
            self.arena = st.enter_context(nc.sbuf_tensor("arena", [128, self.AW], F32))
            self.ps = st.enter_context(nc.psum_tensor("ps", [128, 8, 512], F32))
            self.pb = P.bufs(8, "psb")
            self.top = 0
            self.o_X = self.alloc(16 * 1024)
            self.o_Z = self.alloc(2 * 1024)
            self.b_X = P.bufs(16, "x")
            self.b_Z = P.bufs(2, "z")
            self.o_idf = self.alloc(128)
            self.o_idb = self.alloc(64)
            self.o_ones = self.alloc(64)
            self.o_sel = self.alloc(48)
            self.o_cs = self.alloc(12 + 4)
            self.o_fm = self.alloc(4 * 96)
            self.o_gbc = self.alloc(4096)
            self.o_cols = self.alloc(44 * 5 + 8 + 40)
            self.b_const = P.buf("const")
            self.b_cs = P.buf("cs")
            self.b_fm = P.buf("fm")
            self.b_gbc = P.buf("gbc")
            self.b_cols = P.buf("cols")
            self.persist_top = self.top
            self.prologue()
            for b in range(self.NB):
                self.load_x(b)
                for l in range(self.NL):
                    self.layer(b, l)
                self.store_x(b)
            P.barrier()
            P.emit()
        return nc

    def X(self, t):
        return self.f32(self.o_X + t * 1024, 1024)

    def Z(self, t):
        return self.f32(self.o_Z + t * 1024, 1024)

    def prologue(self):
        P = self.P
        order = []
        for l in range(4):
            order.append(("wada", l))
            i = l // 2
            if l % 2 == 0:
                for k in ("wie", "wkr", "wva", "wuq", "wukvn", "wukvv", "woe"):
                    if l < 2 or True:
                        order.append((k, i))
            else:
                for k in ("wioq", "wiok", "wiov", "woo"):
                    order.append((k, i))
            order.append(("wup", l))
            order.append(("wdn", l))
        for k, i in order:
            if i >= self.din[k].shape[0]:
                continue
            src = self.din[k][i]
            dst = self.dbf[k][i]
            n = 1
            for s_ in src.shape:
                n *= s_
            letters = "abcdefg"[: len(src.shape)]
            pat = " ".join(letters)
            srcf = src.rearrange(f"{pat} -> ({pat})").rearrange("(r j) -> r j", j=2048)
            dstf = dst.rearrange(f"{pat} -> ({pat})").rearrange("(r j) -> r j", j=2048)
            R = n // 2048
            step = 512
            for r0 in range(0, R, step):
                r1 = min(R, r0 + step)
                self.DMA("pool", dstf[r0:r1, :], srcf[r0:r1, :], (), [self.bbf[k][i]])
        idf = self.f32(self.o_idf, 128)
        self.DMA("sp", idf, self.din["identf"], (), [self.b_const])
        self.CP("dve", self.bf(self.o_idb, 128), idf, [self.b_const], [self.b_const])
        self.MEMSET("pool", self.bf(self.o_ones, 128), 1.0, [self.b_const])
        tmp = self.f32(self.alloc(96), 96)
        self.DMA("sp", tmp[0:32, :], self.din["sel"], (), [self.b_const])
        self.CP("dve", self.bf(self.o_sel, 96)[0:32, :], tmp[0:32, :], [self.b_const], [self.b_const])
        ctmp = self.f32(self.alloc(24), 24)
        self.DMA("sp", ctmp, self.din["cT"].rearrange("p k s -> p (k s)"), (), [self.b_cs])
        self.ACT(self.bf(self.o_cs, 24), ctmp, AF.Silu, [self.b_cs], [self.b_cs])
        self.phase()

    def load_x(self, b):
        for t in range(16):
            self.DMA("sp", self.X(t), self.din["x"][b, t * 128:(t + 1) * 128, :], (), [self.b_X[t]])
        for t in range(2):
            self.DMA("sp", self.Z(t), self.din["ctx"][b, t * 128:(t + 1) * 128, :], (), [self.b_Z[t]])

    def store_x(self, b):
        for t in range(16):
            self.DMA("sp", self.out[b, t * 128:(t + 1) * 128, :], self.X(t), [self.b_X[t]], [self.b_out])

    def blk_tiles(self, blk):
        if blk < 4:
            return [(self.X(4 * blk + j), self.b_X[4 * blk + j]) for j in range(4)]
        return [(self.Z(j), self.b_Z[j]) for j in range(2)]

    def blk_t0(self, blk):
        return blk * 512

    def blk_n(self, blk):
        return 512 if blk < 4 else 256

    def fmc(self, kind, k, s):
        fm = self.f32(self.o_fm + self.cur_l * 96, 96).rearrange("p (a k s) -> p a k s", a=4, k=8)
        col = self.cur_b if s == 0 else 2
        return fm[:, kind, k, col:col + 1]

    def mods(self, b, l):
        self.cur_b, self.cur_l = b, l
        gbc = self.f32(self.o_gbc, 4096).rearrange("p (g s n) -> p g s n", g=2, s=2)
        if b == 1:
            self.DMA("sp", gbc, self.gsc[l], [self.b_gsc[l]], [self.b_gbc])
            self.phase()
            return
        fm = self.f32(self.o_fm + l * 96, 96).rearrange("p (a k s) -> p a k s", a=4, k=8)
        cs = self.bf(self.o_cs, 24).rearrange("p (k s) -> p k s", s=3)
        rep = self.bf(self.alloc(1536), 3072).rearrange("p (k s m) -> p k s m", k=8, s=3)
        b_rep = self.tb()
        for s in range(3):
            self.CP("dve", rep[:, :, s, :], cs[:, :, s:s + 1].to_broadcast([128, 8, 128]), [self.b_cs], [b_rep])
        bT = self.f32(self.alloc(48), 48)
        b_bT = self.tb()
        self.DMA("sp", bT, self.din["badaT"][l], (), [b_bT])
        for c0 in (8, 32):
            self.TS("dve", bT[:, c0:c0 + 8], bT[:, c0:c0 + 8], 1.0, None, ALU.add, None, [b_bT], [b_bT])
        bb = self.f32(self.alloc(2048), 2048)
        b_bb = self.tb()
        for g, c0 in ((0, 2048), (1, 5120)):
            self.DMA("sp", bb[:, g * 1024:(g + 1) * 1024], self.din["bada"][l:l + 1, c0:c0 + 1024].to_broadcast([128, 1024]), (), [b_bb])
        g1 = self.f32(self.alloc(2048), 2048).rearrange("p (g n) -> p g n", g=2)
        b_g1 = self.tb()
        wts = [(self.bf(self.alloc(4096), 8192).rearrange("p (c k m) -> p c k m", c=8, k=8), self.tb()) for _ in range(5)]
        kinds = {0: 0, 1: 1, 3: 2, 4: 3}
        n = 0
        for c in range(48):
            sec, cc = c // 8, c % 8
            wt8, bw = wts[sec % 5]
            if cc == 0:
                self.DMA("sp", wt8, self.dbf["wada"][l, sec * 8:(sec + 1) * 8].rearrange("c p k m -> p c k m"), [self.bbf["wada"][l]], [bw])
            wt = wt8[:, cc, :, :]
            bank = n % 2
            n += 1
            if sec in kinds:
                pso = self.ps[:, bank, 0:3]
                for k in range(8):
                    self.MM(pso, wt[:, k, :], cs[:, k, :], k == 0, k == 7, [bw, self.b_cs], [self.pb[bank]])
                self.ACT(fm[:, kinds[sec], cc, :], pso, AF.Identity, [self.pb[bank], b_bT], [self.b_fm], bias=bT[:, c:c + 1])
            else:
                g = 0 if sec == 2 else 1
                for s in range(3):
                    pso = self.ps[:, bank, s * 128:(s + 1) * 128]
                    for k in range(8):
                        self.MM(pso, rep[:, k, s, :], wt[:, k, :], k == 0, k == 7, [bw, b_rep], [self.pb[bank]])
                    if s == 1:
                        dst, b_dst = g1[:, g, cc * 128:(cc + 1) * 128], b_g1
                    else:
                        dst, b_dst = gbc[:, g, s // 2, cc * 128:(cc + 1) * 128], self.b_gbc
                    self.TT("dve", dst, pso, bb[:, g * 1024 + cc * 128:g * 1024 + (cc + 1) * 128], ALU.add, [self.pb[bank], b_bb], [b_dst])
        self.DMA("sp", self.gsc[l][:, :, 0, :], g1, [b_g1], [self.b_gsc[l]])
        self.DMA("sp", self.gsc[l][:, :, 1, :], gbc[:, :, 1, :], [self.b_gbc], [self.b_gsc[l]])
        self.phase()

    def make_hT(self, blk, kind, hT, b_hT, col0=0, banks=(0, 1)):
        s = 0 if blk < 4 else 1
        tiles = self.blk_tiles(blk)
        idf = self.f32(self.o_idf, 128)
        n = len(tiles) * 128
        for k in range(8):
            bank = banks[k % 2]
            for j, (xt, bx) in enumerate(tiles):
                self.TR(self.ps[:, bank, j * 128:(j + 1) * 128], xt[:, k * 128:(k + 1) * 128], idf, [bx, self.b_const], [self.pb[bank]])
            self.ACT(hT[:, k, col0:col0 + n], self.ps[:, bank, 0:n], AF.Identity, [self.pb[bank], self.b_fm], [b_hT],
                     bias=self.fmc(2 * kind, k, s), scale=self.fmc(2 * kind + 1, k, s))

    def attend(self, qk_list, nq, scale, pv, out_rows, out_ap, b_out, rd, mask_fn=None, sink=None):
        per = max(1, 512 // nq)
        nk = len(qk_list)
        ob = self.out_banks[self.att_n % len(self.out_banks)]
        self.att_n += 1
        o_ps = self.ps[:, ob, 0:nq]
        groups = [(g0, min(nk, g0 + per)) for g0 in range(0, nk, per)]
        for gi, (g0, g1) in enumerate(groups):
            sb = self.qk_banks[self.qk_n % len(self.qk_banks)]
            self.qk_n += 1
            slot = self.pt_n % len(self.pt_bufs)
            self.pt_n += 1
            pt, ptf, b_pt = self.pt_bufs[slot]

            def A(g0=g0, g1=g1, sb=sb, pt=pt, ptf=ptf, b_pt=b_pt):
                for j in range(g0, g1):
                    kT, qT = qk_list[j]
                    self.MM(self.ps[0:128, sb, (j - g0) * nq:(j - g0 + 1) * nq], kT, qT, True, True, rd, [self.pb[sb]])
                w = (g1 - g0) * nq
                masked = mask_fn is not None and any(mask_fn(j) is not None for j in range(g0, g1))
                if not masked:
                    self.ACT(pt[:, 0:w], self.ps[:, sb, 0:w], AF.Exp, [self.pb[sb]], [b_pt], scale=scale)
                    return
                j = g0
                any_f32 = False
                while j < g1:
                    m = mask_fn(j)
                    c0 = (j - g0) * nq
                    if m is None:
                        j2 = j
                        while j2 < g1 and mask_fn(j2) is None:
                            j2 += 1
                        c1 = (j2 - g0) * nq
                        if nq <= 64:
                            if not any_f32:
                                self.ACT(ptf[:, 0:w], self.ps[:, sb, 0:w], AF.Exp, [self.pb[sb]], [b_pt], scale=scale)
                                any_f32 = True
                            self.CP("pool", pt[:, c0:c1], ptf[:, c0:c1], [b_pt], [b_pt])
                        else:
                            self.ACT(pt[:, c0:c1], self.ps[:, sb, c0:c1], AF.Exp, [self.pb[sb]], [b_pt], scale=scale)
                        j = j2
                    else:
                        m_ap, m_rd, span = m
                        c1 = c0 + span * nq
                        if nq <= 64:
                            if not any_f32:
                                self.ACT(ptf[:, 0:w], self.ps[:, sb, 0:w], AF.Exp, [self.pb[sb]], [b_pt], scale=scale)
                                any_f32 = True
                        else:
                            self.ACT(ptf[:, c0:c1], self.ps[:, sb, c0:c1], AF.Exp, [self.pb[sb]], [b_pt], scale=scale)
                        o3 = pt[:, c0:c1]
                        i3 = ptf[:, c0:c1]
                        if len(m_ap.shape) == 3:
                            o3 = o3.rearrange("p (a b) -> p a b", a=m_ap.shape[1])
                            i3 = i3.rearrange("p (a b) -> p a b", a=m_ap.shape[1])
                        self.TT("pool" if (span == 1 and nq <= 128) else "dve", o3, i3, m_ap, ALU.mult, [b_pt] + m_rd, [b_pt])
                        j += span

            def B(g0=g0, g1=g1, gi=gi, pt=pt, b_pt=b_pt):
                for j in range(g0, g1):
                    last = (j == nk - 1) and sink is None
                    self.MM(o_ps, pv[j], pt[:, (j - g0) * nq:(j - g0 + 1) * nq], gi == 0 and j == g0, last, rd + [b_pt], [self.pb[ob]])
                if gi != len(groups) - 1:
                    return
                if sink is not None:
                    s_l, s_r, s_rd = sink
                    self.MM(o_ps, s_l, s_r, False, True, s_rd, [self.pb[ob]])
                slot2 = self.rc_n % len(self.rc_bufs)
                self.rc_n += 1
                rec, b_rec = self.rc_bufs[slot2]
                sr = 64 - out_rows
                if nq <= 64:
                    self.RECIP(rec[sr:sr + 64, 0:nq], self.ps[sr:sr + 64, ob, 0:nq], [self.pb[ob]], [b_rec])
                else:
                    self.ACT(rec[sr:sr + 64, 0:nq], self.ps[sr:sr + 64, ob, 0:nq], AF.Ln, [self.pb[ob]], [b_rec])
                    self.ACT(rec[sr:sr + 64, 0:nq], rec[sr:sr + 64, 0:nq], AF.Exp, [b_rec], [b_rec], scale=-1.0)
                num = self.ps[out_rows:out_rows + 64, ob, 0:nq]
                den = rec[sr:sr + 64, 0:nq]
                if len(out_ap.shape) == 3:
                    num = num.rearrange("p (a b) -> p a b", a=out_ap.shape[1])
                    den = den.rearrange("p (a b) -> p a b", a=out_ap.shape[1])
                self.TT("dve", out_ap, num, den, ALU.mult, [self.pb[ob], b_rec], [b_out])

            A()
            self.pending.append(B)
            if len(self.pending) > self.DEPTH:
                self.pending.pop(0)()

    def pipe_flush(self):
        while self.pending:
            self.pending.pop(0)()

    def attn_bufs(self, ptw=512, depth=2, masked=True, rcw=512):
        self.att_n = self.qk_n = self.pt_n = self.rc_n = 0
        self.DEPTH = depth
        self.pending = []
        self.qk_banks = [4, 5, 2, 3, 0, 1][: depth + 1]
        self.out_banks = [6, 7]
        self.pt_bufs = []
        for i in range(depth + 2):
            o1 = self.alloc(ptw // 2)
            o2 = self.alloc(ptw) if masked else o1
            self.pt_bufs.append((self.bf(o1, ptw), self.f32(o2, ptw) if masked else None, self.tb()))
        self.rc_bufs = [(self.f32(self.alloc(rcw), rcw), self.tb()) for _ in range(3)]

    def wring(self, n, words, shape_fn):
        return [(shape_fn(self.bf(self.alloc(words), words * 2)), self.tb()) for _ in range(n)]

    def resid_ln(self, tiles, s, gate, ps_groups):
        gbc = self.f32(self.o_gbc, 4096).rearrange("p (g s n) -> p g s n", g=2, s=2)
        for j, (xt, bx) in enumerate(tiles):
            t, b_t = self.ln_t[self.ln_n % 2]
            st, b_st = self.ln_s[self.ln_n % 2]
            self.ln_n += 1
            for hf in range(2):
                bank = ps_groups[j][hf]
                self.TT("dve", t[:, hf * 512:(hf + 1) * 512], self.ps[:, bank, :], gbc[:, gate, s, hf * 512:(hf + 1) * 512], ALU.mult, [self.pb[bank], self.b_gbc], [b_t])
            self.STT(t, xt, ALPHA, t, ALU.mult, ALU.add, [bx, b_t], [b_t])
            for hf in range(2):
                self.P.op("dve", (lambda e, o=st[:, hf * 6:(hf + 1) * 6], i=t[:, hf * 512:(hf + 1) * 512]: e.bn_stats(out=o, in_=i)), [b_t], [b_st])
            self.P.op("dve", (lambda e, o=st[:, 12:14], i=st[:, 0:12]: e.bn_aggr(out=o, in_=i)), [b_st], [b_st])
            self.ACT(st[:, 14:15], st[:, 13:14], AF.Sqrt, [b_st, self.b_eps], [b_st], bias=self.epscol, scale=1.0)
            self.RECIP(st[:, 15:16], st[:, 14:15], [b_st], [b_st])
            self.TS("dve", xt, t, st[:, 12:13], st[:, 15:16], ALU.subtract, ALU.mult, [b_t, b_st], [bx])

    def ln_bufs(self):
        self.ln_n = 0
        self.ln_t = [(self.f32(self.alloc(1024), 1024), self.tb()) for _ in range(2)]
        self.ln_s = [(self.f32(self.alloc(16), 16), self.tb()) for _ in range(2)]
        o = self.alloc(1)
        self.epscol = self.f32(o, 1)
        self.b_eps = self.tb()
        self.MEMSET("pool", self.epscol, EPS, [self.b_eps])

    def outproj(self, l, with_ctx):
        i = l // 2
        wname = "woe" if l % 2 == 0 else "woo"
        wo = self.bf(self.alloc(4096), 8192).rearrange("p (k n) -> p k n", k=8)
        b_wok = [self.tb() for _ in range(8)]
        for k in range(8):
            self.DMA("sp", wo[:, k, :], self.dbf[wname][i][:, k, :], [self.bbf[wname][i]], [b_wok[k]])
        self.ln_bufs()
        obufs = [(self.bf(self.alloc(2048), 4096).rearrange("p (k n) -> p k n", k=8), self.tb()) for _ in range(3)]
        nb = 5 if with_ctx else 4
        for blk in range(nb):
            oT, b_o = obufs[blk % 3]
            n = self.blk_n(blk)
            t0 = self.blk_t0(blk)
            self.DMA("sp", oT[:, :, 0:n], self.oT[:, :, t0:t0 + n], [self.b_oT[blk]], [b_o])
            tiles = self.blk_tiles(blk)
            for j0 in range(0, len(tiles), 2):
                groups = []
                for j in range(j0, min(len(tiles), j0 + 2)):
                    banks = [2 * ((j - j0) + 2 * (self.op_n % 2)) + hf for hf in range(2)]
                    groups.append(banks)
                    for hf in range(2):
                        for k in range(8):
                            self.MM(self.ps[:, banks[hf], :], oT[:, k, j * 128:(j + 1) * 128], wo[:, k, hf * 512:(hf + 1) * 512], k == 0, k == 7, [b_o, b_wok[k]], [self.pb[banks[hf]]])
                self.op_n += 1
                self.resid_ln(tiles[j0:j0 + 2], 0 if blk < 4 else 1, 0, groups)
        self.phase()

    def ffn(self, l, with_ctx):
        P = self.P
        cols = self.f32(self.o_cols, 44 * 5 + 8)
        bup = cols[:, 0:44]
        cw = cols[:, 44:176].rearrange("p (i c) -> p i c", i=3)
        cb = cols[:, 176:220]
        b_c = self.b_cols
        self.DMA("sp", bup, self.din["bupT"][l], (), [b_c])
        self.DMA("sp", cols[:, 44:176], self.din["cwT"][l].rearrange("p i c -> p (i c)"), (), [b_c])
        self.DMA("sp", cb, self.din["cbT"][l], (), [b_c])
        o_bd = self.alloc(1024 + 512)
        bdf = self.f32(o_bd, 1024)
        bdb = self.bf(o_bd + 1024, 1024)
        b_bd = self.tb()
        self.DMA("sp", bdf[0:1, :], self.din["bdn"][l:l + 1, :], (), [b_bd])
        self.CP("dve", bdb[0:1, :], bdf[0:1, :], [b_bd], [b_bd])
        ones = self.bf(self.o_ones, 128)
        self.ln_bufs()
        hT = self.bf(self.alloc(2048 + 8), 4096 + 16).rearrange("p (k n) -> p k n", k=8)
        b_hT = self.tb()
        o_hb = self.alloc(32)
        hbnd = self.bf(o_hb, 64).rearrange("p (k n) -> p k n", k=8)
        b_hb = self.tb()
        ubnd = self.f32(self.alloc(44 * 8), 44 * 8).rearrange("p (c n) -> p c n", c=44)
        b_ub = self.tb()
        actT = self.bf(self.alloc(NCH * 256), NCH * 512).rearrange("p (c n) -> p c n", c=NCH)
        b_act = [self.tb() for _ in range(NCH)]
        NR = 4
        ua = [(self.f32(self.alloc(516), 516), self.tb(), self.tb()) for _ in range(2 * NR)]
        acc = [(self.f32(self.alloc(512), 512), self.tb()) for _ in range(2 * NR)]
        wup = [(self.bf(self.alloc(1024), 2048).rearrange("p (k m) -> p k m", k=8), self.tb()) for _ in range(4)]
        wdn = [(self.bf(self.alloc(512), 1024), self.tb()) for _ in range(6)]
        bias2 = self.f32(self.alloc(44), 44)
        b_b2 = self.tb()
        self.STT(bias2, bup, 1.0, cw[:, 1, :], ALU.mult, ALU.mult, [b_c], [b_b2])
        self.TT("dve", bias2, bias2, cb, ALU.add, [b_b2, b_c], [b_b2])
        idf = self.f32(self.o_idf, 128)
        for bi in range(3):
            for side in range(2):
                tile = 4 * (bi + 1) - 1 + side
                xt, bx = self.X(tile), self.b_X[tile]
                p0 = 64 if side == 0 else 0
                hb = 4 + (2 * bi + side) % 4
                for k in range(8):
                    self.TR(self.ps[:, hb, k * 64:(k + 1) * 64], xt[p0:p0 + 64, k * 128:(k + 1) * 128], idf[p0:p0 + 64, p0:p0 + 64], [bx, self.b_const], [self.pb[hb]])
                cc_ = 63 if side == 0 else 0
                for k in range(8):
                    self.ACT(hbnd[:, k, 2 * bi + side:2 * bi + side + 1], self.ps[:, hb, k * 64 + cc_:k * 64 + cc_ + 1], AF.Identity, [self.pb[hb], self.b_fm], [b_hb],
                             bias=self.fmc(2, k, 0), scale=self.fmc(3, k, 0))
        nwin = 5 if with_ctx else 4
        wn = 0
        dn = 0
        un = 0
        deferred = None
        fin = None
        for win in range(nwin):
            n = self.blk_n(win)
            s = 0 if win < 4 else 1
            U = [0, 1, 2, 3] if win % 2 == 0 else [4, 5, 6, 7]
            DA = [4, 5, 6, 7] if win % 2 == 0 else [0, 1, 2, 3]
            self.make_hT(win, 1, hT, b_hT, banks=(U[0], U[1]))
            for c in range(NCH):
                if c == 4 and deferred is not None:
                    deferred()
                    deferred = None
                wt, bw = wup[wn % 4]
                wn += 1
                if not (win > 0 and c < 4):
                    self.DMA("sp", wt, self.dbf["wup"][l, c], [self.bbf["wup"][l]], [bw])
                for half in range(2):
                    ci = c + half * NCH
                    bank = U[(c % 2) * 2 + half]
                    pso = self.ps[:, bank, 0:n]
                    for k in range(8):
                        self.MM(pso, wt[:, k, half * 128:(half + 1) * 128], hT[:, k, 0:n], k == 0, k == 7, [bw, b_hT], [self.pb[bank]])
                    if win == 0:
                        psb = self.ps[:, 6, ci * 8:ci * 8 + 6]
                        for k in range(8):
                            self.MM(psb, wt[:, k, half * 128:(half + 1) * 128], hbnd[:, k, 0:6], k == 0, k == 7, [bw, b_hb], [self.pb[6]])
                        self.ACT(ubnd[:, ci, 0:6], psb, AF.Identity, [self.pb[6], b_c], [b_ub], bias=bup[:, ci:ci + 1])
                    u, b_u, b_uh = ua[(un % NR) * 2 + half]
                    a_, b_a = acc[(un % NR) * 2 + half]
                    self.ACT(u[:, 1:n + 1], pso, AF.Identity, [self.pb[bank], b_c], [b_u], bias=bup[:, ci:ci + 1])
                    self.ACT(a_[:, 0:n], pso, AF.Identity, [self.pb[bank], b_c, b_b2], [b_a], bias=bias2[:, ci:ci + 1], scale=cw[:, 1, ci:ci + 1])
                    if win in (1, 2, 3):
                        self.CP("pool", u[:, 0:1], ubnd[:, ci, 2 * win - 2:2 * win - 1], [b_ub], [b_uh])
                    else:
                        self.MEMSET("pool", u[:, 0:1], 0.0, [b_uh])
                    if win in (0, 1, 2):
                        self.CP("pool", u[:, n + 1:n + 2], ubnd[:, ci, 2 * win + 1:2 * win + 2], [b_ub], [b_uh])
                    else:
                        self.MEMSET("pool", u[:, n + 1:n + 2], 0.0, [b_uh])
                    self.STT(a_[:, 0:n], u[:, 0:n], cw[:, 0, ci:ci + 1], a_[:, 0:n], ALU.mult, ALU.add, [b_u, b_uh, b_a, b_c], [b_a])
                    self.STT(a_[:, 0:n], u[:, 2:n + 2], cw[:, 2, ci:ci + 1], a_[:, 0:n], ALU.mult, ALU.add, [b_u, b_uh, b_a, b_c], [b_a])
                aa, b_aa = acc[(un % NR) * 2]
                ag, b_ag = acc[(un % NR) * 2 + 1]
                un += 1
                if fin is not None:
                    fin()

                def fin(aa=aa, ag=ag, b_aa=b_aa, b_ag=b_ag, c=c, n=n):
                    self.ACT(ag[:, 0:n], ag[:, 0:n], AF.Silu, [b_ag], [b_ag])
                    self.TT("pool", actT[:, c, 0:n], aa[:, 0:n], ag[:, 0:n], ALU.mult, [b_aa, b_ag], [b_act[c]])
            fin()
            fin = None
            tiles = self.blk_tiles(win)
            nt = len(tiles)
            if win + 1 < nwin:
                for c2 in range(4):
                    wt2, bw2 = wup[(wn + c2) % 4]
                    self.DMA("sp", wt2, self.dbf["wup"][l, c2], [self.bbf["wup"][l]], [bw2])
            for pas in range((nt + 1) // 2):
                B4 = DA if pas == 0 else U
                tl = list(range(2 * pas, min(nt, 2 * pas + 2)))
                for c in range(NCH):
                    wt, bw = wdn[dn % 6]
                    dn += 1
                    self.DMA("sp", wt, self.dbf["wdn"][l][:, c, :], [self.bbf["wdn"][l]], [bw])
                    for jj, j in enumerate(tl):
                        for hf in range(2):
                            bank = B4[2 * jj + hf]
                            self.MM(self.ps[:, bank, :], actT[:, c, j * 128:(j + 1) * 128], wt[:, hf * 512:(hf + 1) * 512], c == 0, False, [b_act[c], bw], [self.pb[bank]])
                for jj, j in enumerate(tl):
                    for hf in range(2):
                        bank = B4[2 * jj + hf]
                        self.MM(self.ps[:, bank, :], ones[0:1, 0:128], bdb[0:1, hf * 512:(hf + 1) * 512], False, True, [b_bd, self.b_const], [self.pb[bank]])
                ln = (lambda tl=tl, B4=B4, s=s, tiles=tiles: self.resid_ln([tiles[j] for j in tl], s, 1, [[B4[2 * jj], B4[2 * jj + 1]] for jj in range(len(tl))]))
                if pas == 0 or win == nwin - 1:
                    ln()
                else:
                    deferred = ln
        if deferred is not None:
            deferred()
        self.phase()

    def swa_pass(self, l, with_ctx):
        i = l // 2
        P = self.P
        kT = self.bf(self.alloc(2 * T), 4 * T).rearrange("p (r g n) -> p r g n", r=2, g=2)
        b_kT = self.tb()
        self.MEMSET("pool", kT[64:128, 0, :, :], 0.0, [b_kT])
        self.MEMSET("pool", kT[0:64, 1, :, :], 0.0, [b_kT])
        vv = self.bf(self.alloc(18 * 192), 18 * 384).rearrange("p (t g c) -> p t g c", t=18, g=2)
        b_v = self.tb()
        self.MEMSET("pool", vv[:, :, :, 64:128], 1.0, [b_v])
        cosb = self.f32(self.alloc(512), 512)
        sinb = self.f32(self.alloc(512), 512)
        b_tab = self.tb()
        hT = self.bf(self.alloc(2048), 4096).rearrange("p (k n) -> p k n", k=8)
        b_hT = self.tb()
        wv = self.bf(self.alloc(512), 1024).rearrange("p (k n) -> p k n", k=8)
        b_wv = self.tb()
        self.DMA("sp", wv, self.dbf["wiov"][i], [self.bbf["wiov"][i]], [b_wv])
        wts = self.wring(2, 1024, lambda a: a.rearrange("p (k m) -> p k m", k=8))
        t1 = [(self.f32(self.alloc(512), 512), self.tb()) for _ in range(2)]
        t2 = [(self.f32(self.alloc(512), 512), self.tb()) for _ in range(2)]
        wn = 0

        def rope_proj(wsrc, wbuf, n, t0, out_ap, b_out):
            nonlocal wn
            wt, bw = wts[wn % 2]
            self.DMA("sp", wt, wsrc, [wbuf], [bw])
            ba, bb_ = 2, 3
            for k in range(8):
                self.MM(self.ps[:, ba, 0:n], wt[:, k, 0:128], hT[:, k, 0:n], k == 0, k == 7, [bw, b_hT], [self.pb[ba]])
            for k in range(8):
                self.MM(self.ps[:, bb_, 0:n], wt[:, k, 128:256], hT[:, k, 0:n], k == 0, k == 7, [bw, b_hT], [self.pb[bb_]])
            a, b_a = t1[wn % 2]
            b2, b_b2 = t2[wn % 2]
            wn += 1
            self.TT("dve", a[:, 0:n], self.ps[:, ba, 0:n], cosb[:, 0:n], ALU.mult, [self.pb[ba], b_tab], [b_a])
            self.TT("dve", b2[:, 0:n], self.ps[:, bb_, 0:n], sinb[:, 0:n], ALU.mult, [self.pb[bb_], b_tab], [b_b2])
            if isinstance(out_ap, tuple):
                for par_, o_ in enumerate(out_ap):
                    r_ = slice(par_ * 64, par_ * 64 + 64)
                    self.TT("pool", o_[r_, :], a[r_, 0:n], b2[r_, 0:n], ALU.add, [b_a, b_b2], [b_out])
            else:
                self.TT("pool", out_ap, a[:, 0:n], b2[:, 0:n], ALU.add, [b_a, b_b2], [b_out])

        def load_tabs(blk):
            n, t0 = self.blk_n(blk), self.blk_t0(blk)
            self.DMA("sp", cosb[:, 0:n], self.din["cosS"][:, t0:t0 + n], (), [b_tab])
            self.DMA("sp", sinb[:, 0:n], self.din["sinS"][:, t0:t0 + n], (), [b_tab])

        for blk in range(5):
            n, t0 = self.blk_n(blk), self.blk_t0(blk)
            self.make_hT(blk, 0, hT, b_hT)
            load_tabs(blk)
            for g in range(2):
                rope_proj(self.dbf["wiok"][i, g], self.bbf["wiok"][i], n, t0, (kT[:, 0, g, t0:t0 + n], kT[:, 1, g, t0:t0 + n]), b_kT)
            for j in range(n // 128):
                bank = 6 + j % 2
                for k in range(8):
                    self.MM(self.ps[:, bank, 0:128], hT[:, k, j * 128:(j + 1) * 128], wv[:, k, :], k == 0, k == 7, [b_hT, b_wv], [self.pb[bank]])
                tt = t0 // 128 + j
                pv2 = self.ps[:, bank, 0:128].rearrange("p (g c) -> p g c", g=2)
                self.CP("act", vv[:, tt, :, 0:64], pv2, [self.pb[bank]], [b_v])
                self.CP("dve", vv[:, tt, :, 128:192], pv2, [self.pb[bank]], [b_v])
        o_s = self.alloc(16 + 1024)
        sraw = self.f32(o_s, 16)
        srow = self.bf(o_s + 16, 2048).rearrange("p (h q) -> p h q", h=16)
        b_s = self.tb()
        self.DMA("sp", sraw[0:1, :], self.din["sinks"][i:i + 1, :], (), [b_s])
        self.ACT(sraw[0:1, :], sraw[0:1, :], AF.Exp, [b_s], [b_s])
        self.CP("dve", srow[0:1, :, :], sraw[0:1, :].unsqueeze(2).to_broadcast([1, 16, 128]), [b_s], [b_s])
        o_sl = self.alloc(128)
        sl = self.bf(o_sl, 256)
        self.MEMSET("pool", sl[0:1, 0:256], 0.0, [b_s])
        self.MEMSET("pool", sl[0:1, 64:128], 1.0, [b_s])
        ml = self.f32(self.alloc(128), 128)
        mr = self.f32(self.alloc(128), 128)
        b_m = self.tb()
        self.DMA("sp", ml, self.din["swaml"], (), [b_m])
        self.DMA("sp", mr, self.din["swamr"], (), [b_m])
        qb = self.bf(self.alloc(2048), 4096).rearrange("p (c n) -> p c n", c=8)
        b_q = self.tb()
        ob = [(self.bf(self.alloc(2048), 4096).rearrange("p (c n) -> p c n", c=8), self.tb()) for _ in range(2)]
        self.attn_bufs(512, depth=3, masked=True)
        nb = 5 if with_ctx else 4
        for blk in range(nb):
            n, t0 = self.blk_n(blk), self.blk_t0(blk)
            self.make_hT(blk, 0, hT, b_hT)
            load_tabs(blk)
            for c in range(8):
                rope_proj(self.dbf["wioq"][i, c], self.bbf["wioq"][i], n, t0, qb[:, c, 0:n], b_q)
            oT, b_o = ob[blk % 2]
            for qi in range(n // 128):
                if blk < 4:
                    nblk = blk * 4 + qi
                    kts = [(kt, m) for kt, m in ((nblk - 1, ml), (nblk, None), (nblk + 1, mr)) if 0 <= kt < 16] + [(16, None), (17, None)]
                else:
                    kts = [(16, None), (17, None)]
                for g in range(2):
                    for par in range(2):
                        base = par * 64
                        qsl = qb[:, 4 * g:4 * g + 4, qi * 128:(qi + 1) * 128]
                        qk = [(kT[:, par, g, kt * 128:(kt + 1) * 128], qsl) for kt, _ in kts]
                        pv = [vv[:, kt, g, base:base + 128] for kt, _ in kts]
                        masks = [m for _, m in kts]
                        mf = (lambda j, masks=masks: None if masks[j] is None else (masks[j].unsqueeze(1).to_broadcast([128, 4, 128]), [b_m], 1))
                        h0 = 8 * g + par
                        sink = (sl[0:1, base:base + 128], srow[0:1, h0:h0 + 7:2, :], [b_s])
                        self.attend(qk, 512, 0.125, pv, base, oT[base:base + 64, 4 * g:4 * g + 4, qi * 128:(qi + 1) * 128], b_o, [b_kT, b_q, b_v], mf, sink)
            self.pipe_flush()
            self.DMA("sp", self.oT[:, :, t0:t0 + n], oT[:, :, 0:n], [b_o], [self.b_oT[blk]])
        self.phase()

    def na_pass(self, l, with_ctx):
        i = l // 2
        NSL = len(NA_SLOTS)
        kaT = self.bf(self.alloc(2 * T), 4 * T).rearrange("p (c n) -> p c n", c=4)
        b_k = self.tb()
        va = self.bf(self.alloc(18 * 4 * 96), 18 * 4 * 192).rearrange("p (t g c) -> p t g c", t=18, g=4)
        b_v = self.tb()
        self.MEMSET("pool", va[:, :, :, 64:128], 1.0, [b_v])
        hT = self.bf(self.alloc(2048), 4096).rearrange("p (k n) -> p k n", k=8)
        b_hT = self.tb()
        wv = self.bf(self.alloc(2048), 4096).rearrange("p (k n) -> p k n", k=8)
        b_wv = self.tb()
        self.DMA("sp", wv, self.dbf["wva"][i], [self.bbf["wva"][i]], [b_wv])
        wts = self.wring(4, 512, lambda a: a.rearrange("p (k m) -> p k m", k=8))
        wn = 0
        for blk in range(5):
            n, t0 = self.blk_n(blk), self.blk_t0(blk)
            self.make_hT(blk, 0, hT, b_hT)
            for c in range(4):
                wt, bw = wts[wn % 4]
                wn += 1
                self.DMA("sp", wt, self.dbf["wie"][i, 4 + c], [self.bbf["wie"][i]], [bw])
                bank = 2 + c % 2
                for k in range(8):
                    self.MM(self.ps[:, bank, 0:n], wt[:, k, :], hT[:, k, 0:n], k == 0, k == 7, [bw, b_hT], [self.pb[bank]])
                self.CP("act", kaT[:, c, t0:t0 + n], self.ps[:, bank, 0:n], [self.pb[bank]], [b_k])
            for j in range(n // 128):
                bank = 6 + j % 2
                for k in range(8):
                    self.MM(self.ps[:, bank, :], hT[:, k, j * 128:(j + 1) * 128], wv[:, k, :], k == 0, k == 7, [b_hT, b_wv], [self.pb[bank]])
                tt = t0 // 128 + j
                pv4 = self.ps[:, bank, :].rearrange("p (g c) -> p g c", g=4)
                self.CP("act", va[:, tt, :, 0:64], pv4[:, :, 0:64], [self.pb[bank]], [b_v])
                self.CP("dve", va[:, tt, :, 128:192], pv4[:, :, 64:128], [self.pb[bank]], [b_v])
        msk = self.f32(self.alloc(NSL * 64), NSL * 64).rearrange("p (s q) -> p s q", s=NSL)
        b_m = self.tb()
        self.DMA("sp", msk, self.din["namask"], (), [b_m])
        E = [(self.f32(self.alloc(NSL * 64), NSL * 64).rearrange("p (s q) -> p s q", s=NSL), self.tb()) for _ in range(2)]
        qb = self.bf(self.alloc(1024), 2048).rearrange("p (c n) -> p c n", c=4)
        b_q = self.tb()
        ob = [(self.bf(self.alloc(1024), 2048).rearrange("p (c n) -> p c n", c=4), self.tb()) for _ in range(2)]
        self.attn_bufs(512, depth=3, masked=True, rcw=256)
        rp = self.din["rpb"]
        nb = 5 if with_ctx else 4
        en = 0
        for blk in range(nb):
            n, t0 = self.blk_n(blk), self.blk_t0(blk)
            self.make_hT(blk, 0, hT, b_hT)
            for c in range(4):
                wt, bw = wts[wn % 4]
                wn += 1
                self.DMA("sp", wt, self.dbf["wie"][i, c], [self.bbf["wie"][i]], [bw])
                bank = 2 + c % 2
                for k in range(8):
                    self.MM(self.ps[:, bank, 0:n], wt[:, k, :], hT[:, k, 0:n], k == 0, k == 7, [bw, b_hT], [self.pb[bank]])
                self.CP("act", qb[:, c, 0:n], self.ps[:, bank, 0:n], [self.pb[bank]], [b_q])
            oT, b_o = ob[blk % 2]
            for h in range(8):
                c, base = h // 2, (h % 2) * 64
                if blk < 4:
                    Et, b_E = E[en % 2]
                    en += 1
                    def src(dr, cnt, step):
                        return rp[i, h, dr:dr + cnt * step:step, :, :].rearrange("d k q -> k d q")
                    self.DMA("sp", Et[0:64, 0:14, :], src(0, 14, 1), (), [b_E])
                    self.DMA("sp", Et[64:128, 0:14, :], src(1, 14, 1), (), [b_E])
                    self.DMA("sp", Et[0:64, 14:19, :], src(2, 5, 2), (), [b_E])
                    self.DMA("sp", Et[64:128, 14:19, :], src(3, 5, 2), (), [b_E])
                    Ef = Et.rearrange("p s q -> p (s q)")
                    self.ACT(Ef, Ef, AF.Exp, [b_E], [b_E])
                    self.TT("pool", Ef, Ef, msk.rearrange("p s q -> p (s q)"), ALU.mult, [b_E, b_m], [b_E])
                    for rr in range(8):
                        r = blk * 8 + rr
                        r0 = min(max(r - 4, 0), 24)
                        if r0 % 2 == 0:
                            tiles_ = [r0 // 2 + j for j in range(4)]
                            d0 = r0 - r + 7
                            eslice = Et[:, d0:d0 + 8:2, :] if True else None
                            span = 4
                        else:
                            tiles_ = [(r0 - 1) // 2 + j for j in range(5)]
                            eslice = Et[:, 14:19, :]
                            span = 5
                        kts = tiles_ + [16, 17]
                        qk = [(kaT[base:base + 64, c, kt * 128:(kt + 1) * 128], qb[base:base + 64, c, rr * 64:(rr + 1) * 64]) for kt in kts]
                        pv = [va[:, kt, c, base:base + 128] for kt in kts]
                        mf = (lambda j, eslice=eslice, span=span, b_E=b_E: (eslice, [b_E], span) if j == 0 else None)
                        self.attend(qk, 64, 0.125, pv, base, oT[base:base + 64, c, rr * 64:(rr + 1) * 64], b_o, [b_k, b_q, b_v], mf)
                else:
                    kts = [16, 17]
                    qk = [(kaT[base:base + 64, c, kt * 128:(kt + 1) * 128], qb[base:base + 64, c, 0:256]) for kt in kts]
                    pv = [va[:, kt, c, base:base + 128] for kt in kts]
                    self.attend(qk, 256, 0.125, pv, base, oT[base:base + 64, c, 0:256], b_o, [b_k, b_q, b_v])
            self.pipe_flush()
            self.DMA("sp", self.oT[:, 0:4, t0:t0 + n], oT[:, :, 0:n], [b_o], [self.b_oT[blk]])
        self.phase()

    def mla_pass(self, l, hh, with_ctx):
        i = l // 2
        sc = 96.0 ** -0.5
        km = self.bf(self.alloc(2 * T), 4 * T).rearrange("p (h n) -> p h n", h=4)
        b_k = self.tb()
        vm = self.bf(self.alloc(18 * 2 * 96), 18 * 2 * 192).rearrange("p (t g c) -> p t g c", t=18, g=2)
        b_v = self.tb()
        self.MEMSET("pool", vm[:, :, :, 64:128], 1.0, [b_v])
        hT = self.bf(self.alloc(2048), 4096).rearrange("p (k n) -> p k n", k=8)
        b_hT = self.tb()
        cols = self.f32(self.o_cols + 220, 8)
        b_c = self.b_cols
        self.DMA("sp", cols[:, 0:3], self.din["qnT"][i], (), [b_c])
        self.DMA("sp", cols[:, 3:5], self.din["kvnT"][i], (), [b_c])
        cqg = self.bf(self.alloc(768), 1536).rearrange("p (k n) -> p k n", k=3)
        sq = self.bf(self.alloc(768), 1536).rearrange("p (k n) -> p k n", k=3)
        ckg = self.bf(self.alloc(512), 1024).rearrange("p (k n) -> p k n", k=2)
        sk = self.bf(self.alloc(512), 1024).rearrange("p (k n) -> p k n", k=2)
        b_cq, b_ck = self.tb(), self.tb()
        rq = self.f32(self.alloc(512), 512)
        rk = self.f32(self.alloc(512), 512)
        rkc = self.f32(self.alloc(4), 4)
        b_r = self.tb()
        cosb = self.f32(self.alloc(512), 512)
        sinb = self.f32(self.alloc(512), 512)
        b_tab = self.tb()
        rc = self.f32(self.alloc(512), 512)
        rs = self.f32(self.alloc(512), 512)
        b_rc = self.tb()
        krT = self.bf(self.alloc(256), 512)
        b_kr = self.tb()
        tA = self.f32(self.alloc(512), 512)
        tB = self.f32(self.alloc(512), 512)
        b_tA, b_tB = self.tb(), self.tb()
        w5 = self.bf(self.alloc(2560), 5120).rearrange("p (c k m) -> p c k m", c=5, k=8)
        b_w5 = self.tb()
        self.DMA("sp", w5, self.dbf["wie"][i, 12:17].rearrange("c p k m -> p c k m"), [self.bbf["wie"][i]], [b_w5])
        wkr = self.bf(self.alloc(256), 512).rearrange("p (k m) -> p k m", k=8)
        b_wkr = self.tb()
        self.DMA("sp", wkr, self.dbf["wkr"][i], [self.bbf["wkr"][i]], [b_wkr])
        wq4 = self.bf(self.alloc(1152), 2304).rearrange("p (h k m) -> p h k m", h=4, k=3)
        b_wq4 = self.tb()
        self.DMA("sp", wq4, self.dbf["wuq"][i, 4 * hh:4 * hh + 4].rearrange("h p k m -> p h k m"), [self.bbf["wuq"][i]], [b_wq4])
        wkn4 = self.bf(self.alloc(256), 512).rearrange("p (h k m) -> p h k m", h=4, k=2)
        b_wkn4 = self.tb()
        self.DMA("sp", wkn4, self.dbf["wukvn"][i, 4 * hh:4 * hh + 4].rearrange("h p k m -> p h k m"), [self.bbf["wukvn"][i]], [b_wkn4])
        wvv = self.bf(self.alloc(256), 512).rearrange("p (k m) -> p k m", k=2)
        b_wvv = self.tb()
        self.DMA("sp", wvv, self.dbf["wukvv"][i][:, :, hh * 256:(hh + 1) * 256], [self.bbf["wukvv"][i]], [b_wvv])
        ones = self.bf(self.o_ones, 128)
        sel = self.bf(self.o_sel, 96)
        wn = [0, 0, 0]

        def load_tabs(blk):
            n, t0 = self.blk_n(blk), self.blk_t0(blk)
            self.DMA("sp", cosb[:, 0:n], self.din["cosM"][:, t0:t0 + n], (), [b_tab])
            self.DMA("sp", sinb[:, 0:n], self.din["sinM"][:, t0:t0 + n], (), [b_tab])

        def lowrank(blk, which):
            n = self.blk_n(blk)
            if which == "q":
                chunks, dst, dsq, col0, out_r, dim, b_d = (12, 13, 14), cqg, sq, 0, rq, 384.0, b_cq
            else:
                chunks, dst, dsq, col0, out_r, dim, b_d = (15, 16), ckg, sk, 3, rk, 256.0, b_ck
            for kk, c in enumerate(chunks):
                wt, bw = w5[:, c - 12], b_w5
                bank = 2 + kk % 2
                for k in range(8):
                    self.MM(self.ps[:, bank, 0:n], wt[:, k, :], hT[:, k, 0:n], k == 0, k == 7, [bw, b_hT], [self.pb[bank]])
                self.ACT(dst[:, kk, 0:n], self.ps[:, bank, 0:n], AF.Identity, [self.pb[bank], b_c], [b_d], scale=cols[:, col0 + kk:col0 + kk + 1])
                self.ACT(dsq[:, kk, 0:n], self.ps[:, bank, 0:n], AF.Square, [self.pb[bank]], [b_d])
            nk = len(chunks)
            bank = 2
            for kk in range(nk):
                self.MM(self.ps[:, bank, 0:n], ones[:, 0:128], dsq[:, kk, 0:n], kk == 0, kk == nk - 1, [b_d, self.b_const], [self.pb[bank]])
            self.ACT(out_r[:, 0:n], self.ps[:, bank, 0:n], AF.Sqrt, [self.pb[bank], self.b_eps], [b_r], bias=self.epscol, scale=1.0 / dim)
            self.RECIP(out_r[:, 0:n], out_r[:, 0:n], [b_r], [b_r])

        self.epscol = self.f32(self.alloc(1), 1)
        self.b_eps = self.tb()
        self.MEMSET("pool", self.epscol, EPS, [self.b_eps])
        for blk in range(5):
            n, t0 = self.blk_n(blk), self.blk_t0(blk)
            self.make_hT(blk, 0, hT, b_hT)
            load_tabs(blk)
            lowrank(blk, "k")
            for part, bank in ((0, 6), (1, 7)):
                for k in range(8):
                    self.MM(self.ps[0:32, bank, 0:n], wkr[:, k, part * 32:(part + 1) * 32], hT[:, k, 0:n], k == 0, k == 7, [b_wkr, b_hT], [self.pb[bank]])
            self.TT("dve", tA[0:32, 0:n], self.ps[0:32, 6, 0:n], cosb[0:32, 0:n], ALU.mult, [self.pb[6], b_tab], [b_tA])
            self.TT("dve", tB[0:32, 0:n], self.ps[0:32, 7, 0:n], sinb[0:32, 0:n], ALU.mult, [self.pb[7], b_tab], [b_tB])
            self.TT("pool", krT[0:32, 0:n], tA[0:32, 0:n], tB[0:32, 0:n], ALU.add, [b_tA, b_tB], [b_kr])
            for hq in range(4):
                h = hh * 4 + hq
                wt, bw = wkn4[:, hq], b_wkn4
                bank = 4 + hq % 2
                self.MM(self.ps[0:96, bank, 0:n], sel[0:32, 0:96], krT[0:32, 0:n], True, False, [b_kr, self.b_const], [self.pb[bank]])
                for k in range(2):
                    self.MM(self.ps[0:64, bank, 0:n], wt[:, k, :], ckg[:, k, 0:n], False, k == 1, [bw, b_ck], [self.pb[bank]])
                self.TT("dve", km[0:64, hq, t0:t0 + n], self.ps[0:64, bank, 0:n], rk[0:64, 0:n], ALU.mult, [self.pb[bank], b_r], [b_k])
                self.CP("act", km[64:96, hq, t0:t0 + n], self.ps[64:96, bank, 0:n], [self.pb[bank]], [b_k])
            for j in range(n // 128):
                bank = 6 + j % 2
                for k in range(2):
                    self.MM(self.ps[:, bank, 0:2], sk[:, k, j * 128:(j + 1) * 128], ones[:, 0:2], k == 0, k == 1, [b_ck, self.b_const], [self.pb[bank]])
                self.ACT(rkc[:, j:j + 1], self.ps[:, bank, 0:1], AF.Sqrt, [self.pb[bank], self.b_eps], [b_r], bias=self.epscol, scale=1.0 / 256.0)
                self.RECIP(rkc[:, j:j + 1], rkc[:, j:j + 1], [b_r], [b_r])
                bank2 = 4 + j % 2
                for k in range(2):
                    self.MM(self.ps[:, bank2, 0:256], ckg[:, k, j * 128:(j + 1) * 128], wvv[:, k, :], k == 0, k == 1, [b_ck, b_wvv], [self.pb[bank2]])
                tt = t0 // 128 + j
                pv4 = self.ps[:, bank2, 0:256].rearrange("p (g c) -> p g c", g=2)
                self.TS("dve", vm[:, tt, :, 0:64], pv4[:, :, 0:64], rkc[:, j:j + 1], None, ALU.mult, None, [self.pb[bank2], b_r], [b_v])
                self.TS("dve", vm[:, tt, :, 128:192], pv4[:, :, 64:128], rkc[:, j:j + 1], None, ALU.mult, None, [self.pb[bank2], b_r], [b_v])
        qb = self.bf(self.alloc(1024), 2048).rearrange("p (h n) -> p h n", h=4)
        b_q = self.tb()
        ob = [(self.bf(self.alloc(512), 1024).rearrange("p (c n) -> p c n", c=2), self.tb()) for _ in range(2)]
        self.attn_bufs(512, depth=5, masked=False)
        nb = 5 if with_ctx else 4
        for blk in range(nb):
            n, t0 = self.blk_n(blk), self.blk_t0(blk)
            self.make_hT(blk, 0, hT, b_hT)
            load_tabs(blk)
            lowrank(blk, "q")
            self.TT("dve", rc[64:96, 0:n], rq[64:96, 0:n], cosb[64:96, 0:n], ALU.mult, [b_r, b_tab], [b_rc])
            self.TT("dve", rs[64:96, 0:n], rq[64:96, 0:n], sinb[64:96, 0:n], ALU.mult, [b_r, b_tab], [b_rc])
            for hq in range(4):
                h = hh * 4 + hq
                wt, bw = wq4[:, hq], b_wq4
                ba, bb_ = 2, 3
                for k in range(3):
                    self.MM(self.ps[0:96, ba, 0:n], wt[:, k, 0:96], cqg[:, k, 0:n], k == 0, k == 2, [bw, b_cq], [self.pb[ba]])
                for k in range(3):
                    self.MM(self.ps[0:96, bb_, 0:n], wt[:, k, 96:192], cqg[:, k, 0:n], k == 0, k == 2, [bw, b_cq], [self.pb[bb_]])
                self.TT("dve", qb[0:64, hq, 0:n], self.ps[0:64, ba, 0:n], rq[0:64, 0:n], ALU.mult, [self.pb[ba], b_r], [b_q])
                self.TT("dve", tA[64:96, 0:n], self.ps[64:96, ba, 0:n], rc[64:96, 0:n], ALU.mult, [self.pb[ba], b_rc], [b_tA])
                self.TT("dve", tB[64:96, 0:n], self.ps[64:96, bb_, 0:n], rs[64:96, 0:n], ALU.mult, [self.pb[bb_], b_rc], [b_tB])
                self.TT("pool", qb[64:96, hq, 0:n], tA[64:96, 0:n], tB[64:96, 0:n], ALU.add, [b_tA, b_tB], [b_q])
            oT, b_o = ob[blk % 2]
            kts = list(range(18)) if blk < 4 else [16, 17]
            for hq in range(4):
                g, par = hq // 2, hq % 2
                base = par * 64
                qk = [(km[0:96, hq, kt * 128:(kt + 1) * 128], qb[0:96, hq, 0:n]) for kt in kts]
                pv = [vm[:, kt, g, base:base + 128] for kt in kts]
                self.attend(qk, n, sc, pv, base, oT[base:base + 64, g, 0:n], b_o, [b_k, b_q, b_v])
            self.pipe_flush()
            self.DMA("sp", self.oT[:, 4 + 2 * hh:6 + 2 * hh, t0:t0 + n], oT[:, :, 0:n], [b_o], [self.b_oT[blk]])
        self.phase()

    def layer(self, b, l):
        with_ctx = l < 3
        self.op_n = 0
        self.mods(b, l)
        if l % 2 == 0:
            self.na_pass(l, with_ctx)
            self.mla_pass(l, 0, with_ctx)
            self.mla_pass(l, 1, with_ctx)
        else:
            self.swa_pass(l, with_ctx)
        self.outproj(l, with_ctx)
        self.ffn(l, with_ctx)


_CACHE = {}


def _prepare(inputs):
    w = _host_weights(inputs)
    w.update(_host_consts())
    return w


def kernel(**inputs):
    NL = int(inputs.pop("_NL", 4))
    ncores = int(inputs.pop("_NCORES", 8))
    w = _prepare(inputs)
    x = np.ascontiguousarray(np.asarray(inputs["x"], np.float32))
    ctx = np.ascontiguousarray(np.asarray(inputs["ctx"], np.float32))
    c = np.asarray(inputs["c"], np.float32)
    c_ctx = np.asarray(inputs["c_ctx"], np.float32)
    in_maps = []
    for core in range(ncores):
        m = dict(w)
        m["x"] = x[2 * core:2 * core + 2]
        m["ctx"] = ctx[2 * core:2 * core + 2]
        cc = np.stack([c[2 * core], c[2 * core + 1], c_ctx], axis=1)
        m["cT"] = np.ascontiguousarray(cc.reshape(8, 128, 3).transpose(1, 0, 2))
        in_maps.append(m)
    shapes = {k: v.shape for k, v in in_maps[0].items()}
    key = (NL, tuple(sorted((k, tuple(s)) for k, s in shapes.items())))
    nc = Builder(shapes, NL=NL, NB=2).build()
    res = run_bass_kernel_spmd(nc, in_maps, core_ids=list(range(ncores)))
    if getattr(res, "exec_time_ns", None) is not None:
        print("EXEC_TIME_NS", res.exec_time_ns)
    out = np.concatenate([np.asarray(r["out"]) for r in res.results], axis=0)
    return out.astype(np.float32)
```
